# Optimizing a Trainium2 kernel written in Bass

```python
import math
import jax, jax.numpy as jnp
from jax import lax
import numpy as np

D_MODEL = 1024
BATCH = 16
SEQ = 2048
DEPTH = 1

HEAD_DIM = 64
N_ATTN_HEADS = 12
ATTN_WIDTH = N_ATTN_HEADS * HEAD_DIM
CONV_WIDTH = D_MODEL - ATTN_WIDTH
IN_WIDTH = 3 * ATTN_WIDTH + 2 * CONV_WIDTH
CONV_KERNEL = 31
DILATED_CONFIGS = ((128, 1), (512, 4), (2048, 16))
ATTN_BLOCK = 128
REL_BUCKETS = 32
REL_MAX_DIST = 2048
D_FF = 2816
FFN_CONV_KERNEL = 3
ALPHA = (2 * DEPTH) ** 0.25
BETA = (8 * DEPTH) ** -0.25
LN_EPS = 1e-5
NEG_INF = -1e30

kernel_name = "hymba_dilated_conformer_convffn_deepnorm"


def _layer_norm(x, g, b):
    xf = x.astype(jnp.float32)
    mu = jnp.mean(xf, axis=-1, keepdims=True)
    var = jnp.mean(jnp.square(xf - mu), axis=-1, keepdims=True)
    y = (xf - mu) * lax.rsqrt(var + LN_EPS)
    return (y * g.astype(jnp.float32) + b.astype(jnp.float32)).astype(x.dtype)


def _rms_norm(x, g):
    xf = x.astype(jnp.float32)
    y = xf * lax.rsqrt(jnp.mean(jnp.square(xf), axis=-1, keepdims=True) + LN_EPS)
    return (y * g.astype(jnp.float32)).astype(x.dtype)


def _causal_dwconv(x, w, b):
    K, C = w.shape
    y = lax.conv_general_dilated(
        x, w[:, None, :].astype(x.dtype), window_strides=(1,), padding=[(K - 1, 0)],
        dimension_numbers=("NWC", "WIO", "NWC"), feature_group_count=C)
    return y + b.astype(x.dtype)


def _t5_bucket(dist):
    exact = REL_BUCKETS // 2
    d_f = jnp.maximum(dist, 1).astype(jnp.float32)
    large = exact + (jnp.log(d_f / exact) / math.log(REL_MAX_DIST / exact)
                     * (REL_BUCKETS - exact)).astype(jnp.int32)
    large = jnp.minimum(large, REL_BUCKETS - 1)
    return jnp.where(dist < exact, dist, large)


def _dilated_branch(q, k, v, rel_table, window, dilation):
    B, H, S, E = q.shape
    L = S // dilation
    nb = -(-L // ATTN_BLOCK)
    Lp = nb * ATTN_BLOCK
    max_steps = window // dilation
    scale = 1.0 / math.sqrt(E)

    def to_sub(t):
        t = t.reshape(B, H, L, dilation, E).transpose(0, 1, 3, 2, 4)
        t = jnp.pad(t, ((0, 0), (0, 0), (0, 0), (0, Lp - L), (0, 0)))
        return t.reshape(B, H, dilation, nb, ATTN_BLOCK, E)

    def with_prev(t):
        prev = jnp.pad(t, ((0, 0), (0, 0), (0, 0), (1, 0), (0, 0), (0, 0)))[:, :, :, :nb]
        return jnp.concatenate([prev, t], axis=4)

    qs = to_sub(q)
    kk = with_prev(to_sub(k))
    vv = with_prev(to_sub(v))
    s = jnp.einsum("bhrnqe,bhrnke->bhrnqk", qs, kk,
                   preferred_element_type=jnp.float32) * scale

    qi = jnp.arange(ATTN_BLOCK)[:, None]
    kj = jnp.arange(2 * ATTN_BLOCK)[None, :]
    steps = qi + ATTN_BLOCK - kj
    band = (steps >= 0) & (steps <= max_steps)
    has_prev = (jnp.arange(nb)[:, None, None] > 0) | (kj >= ATTN_BLOCK)[None]
    valid = band[None] & has_prev
    bucket = _t5_bucket(jnp.maximum(steps, 0) * dilation)
    bias = rel_table[bucket].astype(jnp.float32).transpose(2, 0, 1)
    s = s + bias[:, None, None]
    s = jnp.where(valid, s, NEG_INF)

    m = jnp.max(s, axis=-1)
    p = jnp.exp(s - m[..., None])
    l = jnp.sum(p, axis=-1)
    o = jnp.einsum("bhrnqk,bhrnke->bhrnqe", p, vv.astype(jnp.float32))

    def from_sub(t):
        tail = t.shape[5:]
        t = t.reshape((B, H, dilation, Lp) + tail)[:, :, :, :L]
        t = jnp.moveaxis(t, 2, 3)
        return t.reshape((B, H, S) + tail)

    return from_sub(o), from_sub(m), from_sub(l)


def _dilated_attention(q, k, v, rel_table):
    branches = [_dilated_branch(q, k, v, rel_table, w, d) for (w, d) in DILATED_CONFIGS]
    m_all = jnp.max(jnp.stack([b[1] for b in branches]), axis=0)
    num = 0.0
    den = 0.0
    for o_i, m_i, l_i in branches:
        c = jnp.exp(m_i - m_all)
        num = num + o_i * c[..., None]
        den = den + l_i * c
    return num / den[..., None]


def _layer(x, rel_table, w_in, b_in, conv_w, conv_b, conv_ln_g, conv_ln_b,
           attn_norm_g, conv_norm_g, w_out, ln1_g, ln1_b,
           w_up, ffn_conv_w, ffn_conv_b, w_down, ln2_g, ln2_b):
    B, S, _ = x.shape
    h = x @ w_in + b_in
    def heads(t):
        return t.reshape(B, S, N_ATTN_HEADS, HEAD_DIM).transpose(0, 2, 1, 3)
    q = heads(h[..., :ATTN_WIDTH])
    k = heads(h[..., ATTN_WIDTH:2 * ATTN_WIDTH])
    v = heads(h[..., 2 * ATTN_WIDTH:3 * ATTN_WIDTH])
    attn = _dilated_attention(q, k, v, rel_table)
    attn = attn.transpose(0, 2, 1, 3).reshape(B, S, ATTN_WIDTH).astype(x.dtype)

    a, g = jnp.split(h[..., 3 * ATTN_WIDTH:], 2, axis=-1)
    u = a * jax.nn.sigmoid(g)
    u = _causal_dwconv(u, conv_w, conv_b)
    u = jax.nn.silu(_layer_norm(u, conv_ln_g, conv_ln_b))

    mixed = jnp.concatenate([_rms_norm(attn, attn_norm_g), _rms_norm(u, conv_norm_g)], axis=-1)
    x = _layer_norm(ALPHA * x + mixed @ w_out, ln1_g, ln1_b)

    up = _causal_dwconv(x @ w_up, ffn_conv_w, ffn_conv_b)
    gate, val = jnp.split(up, 2, axis=-1)
    y = (jax.nn.silu(gate) * val) @ w_down
    return _layer_norm(ALPHA * x + y, ln2_g, ln2_b)


def setup_inputs(seed: int = 0) -> dict:
    key = jax.random.key(seed)
    ks = jax.random.split(key, 20)
    f32 = jnp.float32
    nrm = lambda k, shape, s: jax.random.normal(k, shape, f32) * s
    w_in = nrm(ks[1], (DEPTH, D_MODEL, IN_WIDTH), D_MODEL ** -0.5)
    v_scale = jnp.ones((IN_WIDTH,), f32).at[2 * ATTN_WIDTH:3 * ATTN_WIDTH].set(BETA)
    w_in = w_in * v_scale
    return {
        "x": jax.random.normal(ks[0], (BATCH, SEQ, D_MODEL), f32),
        "rel_table": nrm(ks[2], (REL_BUCKETS, N_ATTN_HEADS), 0.5),
        "w_in": w_in,
        "b_in": nrm(ks[3], (DEPTH, IN_WIDTH), 0.02),
        "conv_w": nrm(ks[4], (DEPTH, CONV_KERNEL, CONV_WIDTH), CONV_KERNEL ** -0.5),
        "conv_b": nrm(ks[5], (DEPTH, CONV_WIDTH), 0.02),
        "conv_ln_g": 1.0 + nrm(ks[6], (DEPTH, CONV_WIDTH), 0.02),
        "conv_ln_b": nrm(ks[7], (DEPTH, CONV_WIDTH), 0.02),
        "attn_norm_g": 1.0 + nrm(ks[8], (DEPTH, ATTN_WIDTH), 0.02),
        "conv_norm_g": 1.0 + nrm(ks[9], (DEPTH, CONV_WIDTH), 0.02),
        "w_out": nrm(ks[10], (DEPTH, D_MODEL, D_MODEL), BETA * D_MODEL ** -0.5),
        "ln1_g": 1.0 + nrm(ks[11], (DEPTH, D_MODEL), 0.02),
        "ln1_b": nrm(ks[12], (DEPTH, D_MODEL), 0.02),
        "w_up": nrm(ks[13], (DEPTH, D_MODEL, 2 * D_FF), D_MODEL ** -0.5),
        "ffn_conv_w": nrm(ks[14], (DEPTH, FFN_CONV_KERNEL, 2 * D_FF), FFN_CONV_KERNEL ** -0.5),
        "ffn_conv_b": nrm(ks[15], (DEPTH, 2 * D_FF), 0.02),
        "w_down": nrm(ks[16], (DEPTH, D_FF, D_MODEL), BETA * D_FF ** -0.5),
        "ln2_g": 1.0 + nrm(ks[17], (DEPTH, D_MODEL), 0.02),
        "ln2_b": nrm(ks[18], (DEPTH, D_MODEL), 0.02),
    }


def reference(x, rel_table, w_in, b_in, conv_w, conv_b, conv_ln_g, conv_ln_b,
              attn_norm_g, conv_norm_g, w_out, ln1_g, ln1_b,
              w_up, ffn_conv_w, ffn_conv_b, w_down, ln2_g, ln2_b):
    for i in range(DEPTH):
        x = _layer(x, rel_table, w_in[i], b_in[i], conv_w[i], conv_b[i], conv_ln_g[i],
                   conv_ln_b[i], attn_norm_g[i], conv_norm_g[i], w_out[i], ln1_g[i],
                   ln1_b[i], w_up[i], ffn_conv_w[i], ffn_conv_b[i], w_down[i],
                   ln2_g[i], ln2_b[i])
    return x
```

```python
import math
from contextlib import ExitStack

import numpy as np
import concourse.bass as bass
import concourse.mybir as mybir
from concourse.bass_utils import run_bass_kernel_spmd

F32 = mybir.dt.float32
BF16 = mybir.dt.bfloat16
AF = mybir.ActivationFunctionType
ALU = mybir.AluOpType

D = 1024
S = 2048
NSEQ = 2
NCORE = 8
HD = 64
NH = 12
AW = 768
CW = 256
INW = 2816
DFF = 2816
CK = 31
ALPHA = 2.0 ** 0.25
EPS = 1e-5
MASKV = -30000.0
BRANCH = ((1, 1, 16), (4, 4, 4), (16, 16, 1))

DEBUG = False

PCOL = {}
_o = 0
for _n, _w in (("b_in", 22), ("convw", 62), ("conv_b", 2), ("cln_g", 2), ("cln_b", 2), ("cn_g", 2),
               ("an_g", 6), ("ln1_g", 8), ("ln1_b", 8), ("fw", 132), ("fb", 44), ("ln2_g", 8), ("ln2_b", 8)):
    PCOL[_n] = _o
    _o += _w
NPCOL = _o


def _cols(v):
    v = np.asarray(v, np.float32).reshape(-1, 128)
    return v.T


def pack_params(inp):
    pv = np.zeros((128, NPCOL), np.float32)

    def put(name, arr):
        pv[:, PCOL[name]:PCOL[name] + arr.shape[1]] = arr
    put("b_in", _cols(inp["b_in"][0]))
    cw = np.asarray(inp["conv_w"][0], np.float32)
    put("convw", np.concatenate([cw[:, 0:128].T, cw[:, 128:256].T], axis=1))
    put("conv_b", _cols(inp["conv_b"][0]))
    put("cln_g", _cols(inp["conv_ln_g"][0]))
    put("cln_b", _cols(inp["conv_ln_b"][0]))
    put("cn_g", _cols(inp["conv_norm_g"][0]))
    put("an_g", _cols(inp["attn_norm_g"][0]))
    put("ln1_g", _cols(inp["ln1_g"][0]))
    put("ln1_b", _cols(inp["ln1_b"][0]))
    fw = np.asarray(inp["ffn_conv_w"][0], np.float32)
    put("fw", np.concatenate([_cols(fw[j]) for j in range(3)], axis=1))
    put("fb", _cols(inp["ffn_conv_b"][0]))
    put("ln2_g", _cols(inp["ln2_g"][0]))
    put("ln2_b", _cols(inp["ln2_b"][0]))
    return pv


def t5_bucket_np(dist):
    dist = np.asarray(dist, np.int64)
    exact = 16
    d_f = np.maximum(dist, 1).astype(np.float32)
    large = exact + (np.log(d_f / np.float32(exact)) / np.float32(math.log(2048 / exact))
                     * np.float32(32 - exact)).astype(np.int32)
    large = np.minimum(large, 31)
    return np.where(dist < exact, dist, large)


def onehot_const():
    oh = np.zeros((33, 3 * 384), np.float32)
    for b, (d, _, _) in enumerate(BRANCH):
        for i in range(383):
            delta = i - 127
            if 0 <= delta <= 128:
                oh[int(t5_bucket_np(delta * d)), b * 384 + i] = 1.0
            else:
                oh[32, b * 384 + i] = MASKV
    return oh


class Tok:
    __slots__ = ("sem", "sid", "val")

    def __init__(self, sem, sid, val):
        self.sem, self.sid, self.val = sem, sid, val


class Eng:
    def __init__(self, nc, e, name):
        self.e = e
        self.name = name
        self.sem = nc.alloc_semaphore(name="es_" + name)
        self.sid = "E" + name
        self.n = 0
        self.seen = {}

    def wait(self, tok):
        if tok is None:
            return
        if self.seen.get(tok.sid, 0) >= tok.val:
            return
        self.e.wait_ge(tok.sem, tok.val)
        self.seen[tok.sid] = tok.val

    def sig(self, ins):
        self.n += 1
        ins.then_inc(self.sem, 1)
        return Tok(self.sem, self.sid, self.n)

    def last(self):
        return Tok(self.sem, self.sid, self.n) if self.n else None


class Buf:
    def __init__(self, name):
        self.name = name
        self.w = None
        self.r = {}
        self.dsem = None
        self.dn = 0


class K:
    def __init__(self):
        self.nc = nc = bass.Bass("TRN2", target_bir_lowering=False)
        self.PE = Eng(nc, nc.tensor, "pe")
        self.ACT = Eng(nc, nc.scalar, "act")
        self.DVE = Eng(nc, nc.vector, "dve")
        self.POOL = Eng(nc, nc.gpsimd, "pool")
        self.SP = Eng(nc, nc.sync, "sp")
        self.engs = [self.PE, self.ACT, self.DVE, self.POOL, self.SP]
        self.pe_pending = []
        self.bufs = {}
        self.dma_bufs = []
        self.nsem = 5
        self.stopped = False

    def buf(self, name):
        if name not in self.bufs:
            self.bufs[name] = Buf(name)
        return self.bufs[name]

    def deps(self, E, reads, writes):
        for b in reads:
            E.wait(b.w)
        for b in writes:
            assert b not in self.pe_pending, f"write to {b.name} with pending PE reads"
            if b.w is not None and b.w.sid != E.sid:
                E.wait(b.w)
            for t in b.r.values():
                if t.sid != E.sid:
                    E.wait(t)

    def done(self, tok, reads, writes):
        for b in reads:
            b.r[tok.sid] = tok
        for b in writes:
            b.w = tok
            b.r = {}

    def op(self, E, fn, reads=(), writes=()):
        if self.stopped:
            return None
        self.deps(E, reads, writes)
        ins = fn()
        tok = E.sig(ins)
        self.done(tok, reads, writes)
        return tok

    def dma(self, Q, out, in_, reads=(), writes=(), sbuf=None):
        if self.stopped:
            return None
        self.deps(Q, reads, writes)
        ins = Q.e.dma_start(out=out, in_=in_)
        b = sbuf if sbuf is not None else writes[0]
        if b.dsem is None:
            b.dsem = self.nc.alloc_semaphore(name="ds_" + b.name)
            self.nsem += 1
            self.dma_bufs.append(b)
        b.dn += 16
        ins.then_inc(b.dsem, 16)
        tok = Tok(b.dsem, "D" + b.name, b.dn)
        self.done(tok, reads, writes)
        return tok

    def mm(self, out, lhsT, rhs, start, stop, reads=(), wbuf=None, first=False, last=False, sig=False,
           transpose=False):
        PE = self.PE
        if self.stopped:
            return None
        for b in reads:
            PE.wait(b.w)
        if first and wbuf is not None:
            if wbuf.w is not None and wbuf.w.sid != PE.sid:
                PE.wait(wbuf.w)
            for t in wbuf.r.values():
                if t.sid != PE.sid:
                    PE.wait(t)
        if transpose:
            ins = self.nc.tensor.transpose(out, lhsT, rhs)
        else:
            ins = self.nc.tensor.matmul(out, lhsT, rhs, start=start, stop=stop)
        for b in reads:
            if b not in self.pe_pending:
                self.pe_pending.append(b)
        if last or sig:
            tok = PE.sig(ins)
            for b in self.pe_pending:
                b.r[tok.sid] = tok
            self.pe_pending = []
            if last and wbuf is not None:
                wbuf.w = tok
                wbuf.r = {}
            return tok
        return None

    def barrier(self):
        if self.stopped:
            return
        assert not self.pe_pending
        toks = [e.last() for e in self.engs]
        for b in self.dma_bufs:
            if b.dn:
                toks.append(Tok(b.dsem, "D" + b.name, b.dn))
        for e in self.engs:
            for t in toks:
                if t is not None and t.sid != e.sid:
                    e.wait(t)


def tok_ap(base, d, r, n, nblk=1):
    if d == 1:
        return slice(128 * n, 128 * (n + nblk))
    if d == 4:
        return slice(512 * n + r, 512 * (n + nblk), 4)
    return slice(r, S, 16)


class _Stop(Exception):
    pass


def build(stop=None, nseq=NSEQ):
    k = K()

    def stage(name):
        if stop == name and not k.stopped:
            k.barrier()
            k.stopped = True
    nc = k.nc
    PE, ACT, DVE, POOL, SP = k.PE, k.ACT, k.DVE, k.POOL, k.SP
    es = ExitStack()

    def dram(name, shape, dt, kind):
        return nc.dram_tensor(name, list(shape), dt, kind=kind).ap()

    xT = dram("xT", [NSEQ, D, S], F32, "ExternalInput")
    w_in = dram("w_in", [D, INW], F32, "ExternalInput")
    w_out = dram("w_out", [D, D], F32, "ExternalInput")
    w_up = dram("w_up", [D, 2 * DFF], F32, "ExternalInput")
    w_down = dram("w_down", [DFF, D], F32, "ExternalInput")
    pvec_d = dram("pvec", [128, NPCOL], F32, "ExternalInput")
    relx_d = dram("relx", [33, 12], F32, "ExternalInput")
    oh_d = dram("oh", [33, 3 * 384], F32, "ExternalInput")
    ident_d = dram("ident", [128, 128], F32, "ExternalInput")
    jmat_d = dram("jmat", [128, 128], F32, "ExternalInput")
    outT = dram("outT", [NSEQ, D, S], F32, "ExternalOutput")
    cdram = dram("cdram", [12, 3, 384], F32, "Internal")
    dnd = dram("dnd", [2, S], F32, "Internal")
    dnd2 = dram("dnd2", [2, S], F32, "Internal")
    if DEBUG:
        dbg_attn = dram("dbg_attn", [NSEQ, 128, 6, S], BF16, "ExternalOutput")
        dbg_un = dram("dbg_un", [NSEQ, 128, 2, S], BF16, "ExternalOutput")
        dbg_x1 = dram("dbg_x1", [NSEQ, 128, 8, S], BF16, "ExternalOutput")
        dbg_tb = dram("dbg_tb", [128, 3, 12, 256], F32, "ExternalOutput")
        dbg_act = dram("dbg_act", [128, 22, 1024], BF16, "ExternalOutput")

    w_in_v = w_in.rearrange("(kc p) n -> p kc n", p=128)
    w_out_v = w_out.rearrange("(kc p) n -> p kc n", p=128)
    w_up_v = w_up.rearrange("(kc p) n -> p kc n", p=128)
    w_down_v = w_down.rearrange("(kc p) n -> p kc n", p=128)

    _cnt = [0]

    def sb(name, shape, dt, stack=es):
        _cnt[0] += 1
        return stack.enter_context(nc.sbuf_tensor(f"{name}_{_cnt[0]}", list(shape), dt))

    PSA = es.enter_context(nc.psum_tensor("psA", [128, 1024], F32))
    PSBt = es.enter_context(nc.psum_tensor("psB", [128, 1024], F32))
    PS = [PSA[:, 0:512], PSA[:, 512:1024], PSBt[:, 0:512], PSBt[:, 512:1024]]
    PS += [es.enter_context(nc.psum_tensor(f"ps{i}", [128, 512], F32))[:, :] for i in range(4, 8)]
    PS2 = [PSA, PSBt]
    PSB = [k.buf(f"ps{i}") for i in range(8)]

    pvec = sb("pvec_sb", [128, NPCOL], F32)
    pder = sb("pder", [128, 64], F32)
    identb = sb("identb", [128, 128], BF16)
    onesb = sb("onesb", [128, 128], BF16)
    jmat = sb("jmat_sb", [128, 128], F32)
    attnT = sb("attnT", [128, 6, S], BF16)
    uN = sb("uN", [128, 2, S], BF16)
    halo = sb("halo", [128, 44, 2], F32)
    B_pvec, B_pder, B_ident, B_ones, B_j = (k.buf(n) for n in ("pvec", "pder", "ident", "ones", "jmat"))
    B_attn = [k.buf(f"attnT{c}") for c in range(6)]
    B_uN = k.buf("uN")
    B_halo = k.buf("halo")

    def pc(name, j=0, n=1):
        o = PCOL[name] + j
        return pvec[:, o:o + n]

    k.dma(SP, pvec[:], pvec_d, writes=[B_pvec])
    k.dma(POOL, identb[:], ident_d, writes=[B_ident])
    k.dma(SP, jmat[:], jmat_d, writes=[B_j])
    k.op(DVE, lambda: nc.vector.memset(onesb[:], 1.0), writes=[B_ones])
    k.op(DVE, lambda: nc.vector.tensor_scalar(out=pder[:, 0:62], in0=pc("convw", 0, 62), scalar1=0.5, scalar2=None,
                                              op0=ALU.mult), reads=[B_pvec], writes=[B_pder])
    k.op(DVE, lambda: nc.vector.tensor_scalar(out=pder[:, 62:64], in0=pc("b_in", 20, 2), scalar1=0.5, scalar2=None,
                                              op0=ALU.mult), reads=[B_pvec], writes=[B_pder])

    with ExitStack() as s0:
        relx = sb("relx_sb", [33, 12], F32, s0)
        ohs = sb("oh_sb", [33, 3 * 384], F32, s0)
        csb = sb("csb", [12, 3, 384], F32, s0)
        B_relx, B_oh, B_csb, B_cd = k.buf("relx"), k.buf("oh"), k.buf("csb"), k.buf("cdram")
        k.dma(SP, relx[:], relx_d, writes=[B_relx])
        k.dma(SP, ohs[:], oh_d, writes=[B_oh])
        for b in range(3):
            k.mm(PS[b][0:12, 0:384], relx[:, :], ohs[:, b * 384:(b + 1) * 384], True, True,
                 reads=[B_relx, B_oh], wbuf=PSB[b], first=True, last=True)
            k.op(DVE, lambda b=b: nc.vector.tensor_copy(out=csb[:, b, :], in_=PS[b][0:12, 0:384]),
                 reads=[PSB[b]], writes=[B_csb])
        k.dma(SP, cdram, csb[:], reads=[B_csb], writes=[B_cd])
        k.barrier()

    def layer_norm_tile(R, B_Rc, tsl, gname, bname, stat_ps, tmp, outb=None, B_outb=None):
        zb, zsq, B_zb, B_zsq = tmp["zb"], tmp["zsq"], tmp["B_zb"], tmp["B_zsq"]
        m, q, B_m, B_q = tmp["m"], tmp["q"], tmp["B_m"], tmp["B_q"]
        p1, p2 = stat_ps
        nb = len(zb)
        for oc in range(8):
            i = tmp["rot"][0] % nb
            tmp["rot"][0] += 1
            k.op(DVE, lambda: nc.vector.tensor_copy(out=zb[i][:], in_=R[:, oc, tsl]), reads=[B_Rc[oc]],
                 writes=[B_zb[i]])
            k.op(ACT, lambda: nc.scalar.activation(out=zsq[i][:], in_=R[:, oc, tsl], func=AF.Square),
                 reads=[B_Rc[oc]], writes=[B_zsq[i]])
            k.mm(PS[p1][:, :], onesb[:, :], zb[i][:], oc == 0, oc == 7, reads=[B_ones, B_zb[i]],
                 wbuf=PSB[p1], first=(oc == 0), last=(oc == 7), sig=True)
            k.mm(PS[p2][:, :], onesb[:, :], zsq[i][:], oc == 0, oc == 7, reads=[B_ones, B_zsq[i]],
                 wbuf=PSB[p2], first=(oc == 0), last=(oc == 7), sig=True)
        k.op(DVE, lambda: nc.vector.tensor_scalar(out=m[:], in0=PS[p1][:, :], scalar1=1.0 / D, scalar2=None,
                                                  op0=ALU.mult), reads=[PSB[p1]], writes=[B_m])
        k.op(DVE, lambda: nc.vector.tensor_tensor(out=q[:], in0=m[:], in1=m[:], op=ALU.mult),
             reads=[B_m], writes=[B_q])
        k.op(DVE, lambda: nc.vector.scalar_tensor_tensor(out=q[:], in0=PS[p2][:, :], scalar=1.0 / D, in1=q[:],
                                                         op0=ALU.mult, op1=ALU.subtract),
             reads=[PSB[p2], B_q], writes=[B_q])
        k.op(ACT, lambda: nc.scalar.activation(out=q[:], in_=q[:], func=AF.Ln, bias=EPS, scale=1.0),
             reads=[B_q], writes=[B_q])
        k.op(ACT, lambda: nc.scalar.activation(out=q[:], in_=q[:], func=AF.Exp, scale=-0.5),
             reads=[B_q], writes=[B_q])
        k.op(DVE, lambda: nc.vector.tensor_tensor(out=m[:], in0=m[:], in1=q[:], op=ALU.mult),
             reads=[B_m, B_q], writes=[B_m])
        def ln_mult(oc):
            k.op(DVE, lambda: nc.vector.tensor_tensor(out=R[:, oc, tsl], in0=R[:, oc, tsl], in1=q[:], op=ALU.mult),
                 reads=[B_Rc[oc], B_q], writes=[B_Rc[oc]])

        def ln_rest(oc):
            k.op(DVE, lambda: nc.vector.tensor_tensor(out=R[:, oc, tsl], in0=R[:, oc, tsl], in1=m[:],
                                                      op=ALU.subtract), reads=[B_Rc[oc], B_m], writes=[B_Rc[oc]])
            k.op(ACT, lambda: nc.scalar.activation(out=R[:, oc, tsl], in_=R[:, oc, tsl], func=AF.Identity,
                                                   bias=pc(bname, oc), scale=pc(gname, oc)),
                 reads=[B_Rc[oc], B_pvec], writes=[B_Rc[oc]])
            if outb is not None:
                k.op(ACT, lambda: nc.scalar.copy(out=outb[:, oc, tsl], in_=R[:, oc, tsl]),
                     reads=[B_Rc[oc]], writes=[B_outb[oc]])

        ln_mult(0)
        for oc in range(1, 8):
            ln_mult(oc)
            ln_rest(oc - 1)
        ln_rest(7)

    stage("setup")
    try:
      for s in range(nseq):
          with ExitStack() as sAB:
              Tb = sb("Tb", [128, 3, 12, 256], F32, sAB)
              xTb = sb("xTb", [128, 8, S], BF16, sAB)
              B_Tb = k.buf("Tb")
              B_x = [k.buf(f"xTb{i}") for i in range(8)]
              for kc in range(8):
                  k.dma(POOL, xTb[:, kc, :], xT[s, kc * 128:(kc + 1) * 128, :], writes=[B_x[kc]])

              with ExitStack() as sC:
                  convD = sb("convD", [128, 2, CK, 128], BF16, sC)
                  wc = sb("wc", [128, 8, 512], BF16, sC)
                  uB = sb("uB", [128, 2, S + 32], BF16, sC)
                  cv = sb("cv", [128, 2, S], F32, sC)
                  t1 = [sb(f"ct1_{i}", [128, 512], F32, sC) for i in range(2)]
                  t2 = [sb(f"ct2_{i}", [128, 512], F32, sC) for i in range(2)]
                  cb = [sb(f"cb_{i}", [128, 512], BF16, sC) for i in range(2)]
                  cq = [sb(f"cq_{i}", [128, 512], BF16, sC) for i in range(2)]
                  cm = sb("cm", [128, S], F32, sC)
                  cr = sb("cr", [128, S], F32, sC)
                  B_cD, B_wc, B_uB, B_cv = k.buf("convD"), k.buf("wc"), k.buf("uB"), k.buf("cv")
                  B_cDc = [k.buf("convD0"), k.buf("convD1")]
                  B_t1 = [k.buf(f"ct1_{i}") for i in range(2)]
                  B_t2 = [k.buf(f"ct2_{i}") for i in range(2)]
                  B_cb = [k.buf(f"cb_{i}") for i in range(2)]
                  B_cq = [k.buf(f"cq_{i}") for i in range(2)]
                  B_cm, B_cr = k.buf("cm"), k.buf("cr")

                  for kc in range(0, 8, 2):
                      k.dma(POOL, wc[:, kc:kc + 2, :], w_in_v[:, kc:kc + 2, 2304:2816], writes=[B_wc])
                  for cc in range(2):
                      for j in range(CK):
                          if cc == 0:
                              k.op(DVE, lambda: nc.vector.tensor_scalar(out=convD[:, cc, j, :], in0=identb[:, :],
                                                                        scalar1=pder[:, cc * CK + j:cc * CK + j + 1],
                                                                        scalar2=None, op0=ALU.mult),
                                   reads=[B_ident, B_pder], writes=[B_cDc[cc]])
                          else:
                              k.op(ACT, lambda: nc.scalar.activation(out=convD[:, cc, j, :], in_=identb[:, :],
                                                                     func=AF.Copy,
                                                                     scale=pder[:, cc * CK + j:cc * CK + j + 1]),
                                   reads=[B_ident, B_pder], writes=[B_cDc[cc]])
                  k.op(DVE, lambda: nc.vector.memset(uB[:, :, 0:30], 0.0), writes=[B_uB])
                  it = 0
                  for cc in range(2):
                      for tt in range(4):
                          tsl = slice(tt * 512, (tt + 1) * 512)
                          i = it % 2
                          pa, pg = 2 * i, 2 * i + 1
                          for kc in range(8):
                              k.mm(PS[pa][:, :], wc[:, kc, cc * 128:(cc + 1) * 128], xTb[:, kc, tsl], kc == 0, kc == 7,
                                   reads=[B_wc, B_x[kc]], wbuf=PSB[pa], first=(kc == 0), last=(kc == 7))
                          for kc in range(8):
                              k.mm(PS[pg][:, :], wc[:, kc, 256 + cc * 128:256 + (cc + 1) * 128], xTb[:, kc, tsl],
                                   kc == 0, kc == 7, reads=[B_wc, B_x[kc]], wbuf=PSB[pg], first=(kc == 0),
                                   last=(kc == 7))
                          k.op(ACT, lambda: nc.scalar.activation(out=t1[i][:], in_=PS[pg][:, :], func=AF.Tanh,
                                                                 bias=pder[:, 62 + cc:63 + cc], scale=0.5),
                               reads=[PSB[pg], B_pder], writes=[B_t1[i]])
                          k.op(ACT, lambda: nc.scalar.activation(out=t2[i][:], in_=PS[pa][:, :], func=AF.Identity,
                                                                 bias=pc("b_in", 18 + cc), scale=1.0),
                               reads=[PSB[pa], B_pvec], writes=[B_t2[i]])
                          k.op(DVE, lambda: nc.vector.scalar_tensor_tensor(
                              out=uB[:, cc, 30 + tt * 512:30 + (tt + 1) * 512], in0=t1[i][:], scalar=1.0,
                              in1=t2[i][:], op0=ALU.add, op1=ALU.mult),
                              reads=[B_t1[i], B_t2[i]], writes=[B_uB])
                          it += 1
                  for tt in range(4):
                      tsl = slice(tt * 512, (tt + 1) * 512)
                      for cc in range(2):
                          pb = (tt * 2 + cc) % 2
                          for j in range(CK):
                              k.mm(PS[pb][:, :], convD[:, cc, j, :], uB[:, cc, tt * 512 + j:tt * 512 + j + 512],
                                   j == 0, j == CK - 1, reads=[B_cDc[cc], B_uB], wbuf=PSB[pb], first=(j == 0),
                                   last=(j == CK - 1))
                          k.op(ACT, lambda: nc.scalar.activation(out=cv[:, cc, tsl], in_=PS[pb][:, :],
                                                                 func=AF.Identity, bias=pc("conv_b", cc), scale=1.0),
                               reads=[PSB[pb], B_pvec], writes=[B_cv])
                          k.op(ACT, lambda: nc.scalar.activation(out=cq[cc][:], in_=PS[pb][:, :], func=AF.Square,
                                                                 bias=pc("conv_b", cc), scale=1.0),
                               reads=[PSB[pb], B_pvec], writes=[B_cq[cc]])
                          k.op(DVE, lambda: nc.vector.tensor_copy(out=cb[cc][:], in_=cv[:, cc, tsl]),
                               reads=[B_cv], writes=[B_cb[cc]])
                      for cc in range(2):
                          k.mm(PS[2][:, :], onesb[:, :], cb[cc][:], cc == 0, cc == 1, reads=[B_ones, B_cb[cc]],
                               wbuf=PSB[2], first=(cc == 0), last=(cc == 1), sig=True)
                          k.mm(PS[3][:, :], onesb[:, :], cq[cc][:], cc == 0, cc == 1, reads=[B_ones, B_cq[cc]],
                               wbuf=PSB[3], first=(cc == 0), last=(cc == 1), sig=True)
                      k.op(DVE, lambda: nc.vector.tensor_scalar(out=cm[:, tsl], in0=PS[2][:, :], scalar1=1.0 / CW,
                                                                scalar2=None, op0=ALU.mult),
                           reads=[PSB[2]], writes=[B_cm])
                      k.op(DVE, lambda: nc.vector.tensor_tensor(out=cr[:, tsl], in0=cm[:, tsl], in1=cm[:, tsl],
                                                                op=ALU.mult), reads=[B_cm], writes=[B_cr])
                      k.op(DVE, lambda: nc.vector.scalar_tensor_tensor(out=cr[:, tsl], in0=PS[3][:, :],
                                                                       scalar=1.0 / CW, in1=cr[:, tsl],
                                                                       op0=ALU.mult, op1=ALU.subtract),
                           reads=[PSB[3], B_cr], writes=[B_cr])
                  with ExitStack() as sT:
                      Hk = [sb(f"Hk{i}", [128, 4, 2, 128], F32, sT) for i in range(2)]
                      B_Hk = [k.buf(f"Hk{i}") for i in range(2)]
                      it = 0
                      for b in range(3):
                          for h0 in range(0, 12, 4):
                              i = it % 2
                              if b < 2:
                                  src = bass.AP(tensor=cdram.tensor, offset=h0 * 3 * 384 + b * 384,
                                                ap=[[1, 128], [3 * 384, 4], [128, 2], [1, 128]])
                                  k.dma(SP, Hk[i][:], src, reads=[k.buf("cdram")], writes=[B_Hk[i]])
                              else:
                                  for part in range(2):
                                      src = bass.AP(tensor=cdram.tensor, offset=h0 * 3 * 384 + b * 384,
                                                    ap=[[1, 128], [3 * 384, 4], [1, 128]])
                                      k.dma(SP, Hk[i][:, :, part, :], src, reads=[k.buf("cdram")], writes=[B_Hk[i]])
                              for half in range(2):
                                  pb = (it * 2 + half) % 2
                                  k.mm(PS[pb][:, :], jmat[:, :],
                                       Hk[i][:, 2 * half:2 * half + 2, :, :].rearrange("p a b c -> p (a b c)"),
                                       True, True, reads=[B_j, B_Hk[i]], wbuf=PSB[pb], first=True, last=True)
                                  k.op(DVE, lambda: nc.vector.tensor_copy(
                                      out=Tb[:, b, h0 + 2 * half:h0 + 2 * half + 2, :].rearrange("p a b -> p (a b)"),
                                      in_=PS[pb][:, :]), reads=[PSB[pb]], writes=[B_Tb])
                              it += 1
                  if DEBUG and s == 0:
                      k.dma(SP, dbg_tb, Tb[:], reads=[B_Tb], writes=[k.buf("dbg_tb")])
                  stage("tables")

                  k.op(ACT, lambda: nc.scalar.activation(out=cr[:], in_=cr[:], func=AF.Ln, bias=EPS, scale=1.0),
                       reads=[B_cr], writes=[B_cr])
                  k.op(ACT, lambda: nc.scalar.activation(out=cr[:], in_=cr[:], func=AF.Exp, scale=-0.5),
                       reads=[B_cr], writes=[B_cr])
                  for cc in range(2):
                      k.op(DVE, lambda: nc.vector.tensor_tensor(out=cv[:, cc, :], in0=cv[:, cc, :], in1=cm[:],
                                                                op=ALU.subtract), reads=[B_cv, B_cm], writes=[B_cv])
                      k.op(DVE, lambda: nc.vector.tensor_tensor(out=cv[:, cc, :], in0=cv[:, cc, :], in1=cr[:],
                                                                op=ALU.mult), reads=[B_cv, B_cr], writes=[B_cv])
                  for cc in range(2):
                      k.op(ACT, lambda: nc.scalar.activation(out=cv[:, cc, :], in_=cv[:, cc, :], func=AF.Silu,
                                                             bias=pc("cln_b", cc), scale=pc("cln_g", cc)),
                           reads=[B_cv, B_pvec], writes=[B_cv])
                  for tt in range(4):
                      tsl = slice(tt * 512, (tt + 1) * 512)
                      for cc in range(2):
                          k.op(ACT, lambda: nc.scalar.activation(out=cq[cc][:], in_=cv[:, cc, tsl], func=AF.Square),
                               reads=[B_cv], writes=[B_cq[cc]])
                          k.mm(PS[3][:, :], onesb[:, :], cq[cc][:], cc == 0, cc == 1, reads=[B_ones, B_cq[cc]],
                               wbuf=PSB[3], first=(cc == 0), last=(cc == 1), sig=True)
                      k.op(ACT, lambda: nc.scalar.activation(out=cr[:, tsl], in_=PS[3][:, :], func=AF.Ln, bias=EPS,
                                                             scale=1.0 / CW), reads=[PSB[3]], writes=[B_cr])
                  k.op(ACT, lambda: nc.scalar.activation(out=cr[:], in_=cr[:], func=AF.Exp, scale=-0.5),
                       reads=[B_cr], writes=[B_cr])
                  for cc in range(2):
                      k.op(DVE, lambda: nc.vector.scalar_tensor_tensor(out=uN[:, cc, :], in0=cv[:, cc, :],
                                                                       scalar=pc("cn_g", cc), in1=cr[:],
                                                                       op0=ALU.mult, op1=ALU.mult),
                           reads=[B_cv, B_cr, B_pvec], writes=[B_uN])
                  k.barrier()
              if DEBUG:
                  k.dma(SP, dbg_un[s], uN[:], reads=[B_uN], writes=[k.buf("dbg_un")])
              stage("conv")

              with ExitStack() as sA:
                  NST = 4
                  wq = [sb(f"wq{i}", [128, 8, 384], BF16, sA) for i in range(2)]
                  qz = [sb(f"qz{i}", [128, S], BF16, sA) for i in range(2)]
                  kv = sb("kv", [128, 2, S], BF16, sA)
                  Vtok = sb("Vtok", [128, 96, 65], BF16, sA)
                  acc = [sb(f"acc{i}", [128, S], F32, sA) for i in range(2)]
                  denb = sb("denb", [128, S], F32, sA)
                  dq = sb("dq", [128, 2, 16], F32, sA)
                  stmp = [sb(f"stmp{i}", [128, 2, 256], F32, sA) for i in range(NST)]
                  PT = [sb(f"PT{i}", [128, 2, 256], BF16, sA) for i in range(NST)]
                  mtmp = [sb(f"mtmp{i}", [128, 512], F32, sA) for i in range(4)]
                  sq = [sb(f"asq{i}", [128, 512], BF16, sA) for i in range(2)]
                  rra = sb("rra", [128, S], F32, sA)
                  B_wq = [k.buf(f"wq{i}") for i in range(2)]
                  B_qz = [k.buf(f"qz{i}") for i in range(2)]
                  B_kv = [k.buf(f"kv{i}") for i in range(2)]
                  B_V = k.buf("Vtok")
                  B_Vg = [[k.buf(f"Vtok{b_}_{g_}") for g_ in range(4)] for b_ in range(3)]
                  B_acc = [k.buf(f"acc{i}") for i in range(2)]
                  B_denb = k.buf("denb")
                  B_dnd = k.buf("dnd")
                  B_dnd2 = k.buf("dnd2")
                  B_dq = k.buf("dq")
                  B_st = [k.buf(f"stmp{i}") for i in range(NST)]
                  B_PT = [k.buf(f"PT{i}") for i in range(NST)]
                  B_mt = [k.buf(f"mtmp{i}") for i in range(4)]
                  B_sq = [k.buf(f"asq{i}") for i in range(2)]
                  B_rra = k.buf("rra")

                  k.op(DVE, lambda: nc.vector.memset(Vtok[:, :, 64:65], 1.0), writes=[B_V])
                  k.op(POOL, lambda: nc.gpsimd.memset(qz[0][64:128, :], 0.0), writes=[B_qz[0]])
                  k.op(POOL, lambda: nc.gpsimd.memset(qz[1][0:64, :], 0.0), writes=[B_qz[1]])

                  def load_wq(c):
                      i = c % 2
                      for j in range(3):
                          k.dma(POOL, wq[i][:, :, j * 128:(j + 1) * 128],
                                w_in_v[:, :, j * AW + c * 128:j * AW + (c + 1) * 128], writes=[B_wq[i]])

                  evac_flip = [0]
                  mt_i = [0]

                  def project(c):
                      wi = c % 2
                      for j in range(3):
                          for tt in range(4):
                              tsl = slice(tt * 512, (tt + 1) * 512)
                              pb = (j * 4 + tt) % 4
                              for kc in range(8):
                                  k.mm(PS[pb][:, :], wq[wi][:, kc, j * 128:(j + 1) * 128], xTb[:, kc, tsl],
                                       kc == 0, kc == 7, reads=[B_wq[wi], B_x[kc]], wbuf=PSB[pb],
                                       first=(kc == 0), last=(kc == 7))
                              if j == 0:
                                  for hh in range(2):
                                      ps_ = slice(hh * 64, (hh + 1) * 64)
                                      k.op(ACT, lambda: nc.scalar.activation(
                                          out=qz[hh][ps_, tsl], in_=PS[pb][ps_, :], func=AF.Identity,
                                          bias=pvec[ps_, PCOL["b_in"] + c:PCOL["b_in"] + c + 1], scale=1.0),
                                          reads=[PSB[pb], B_pvec], writes=[B_qz[hh]])
                              else:
                                  k.op(ACT, lambda: nc.scalar.activation(out=kv[:, j - 1, tsl], in_=PS[pb][:, :],
                                                                         func=AF.Identity,
                                                                         bias=pc("b_in", j * 6 + c), scale=1.0),
                                       reads=[PSB[pb], B_pvec], writes=[B_kv[j - 1]])

                  def vtok(c):
                      for b, (d, nres, nblk) in enumerate(BRANCH):
                          for g in range(4):
                              pb = g % 4
                              pv = PS[pb][:, 0:256].bitcast(BF16)
                              for jj in range(4):
                                  blk = 4 * g + jj
                                  if d == 1:
                                      tsl = tok_ap(None, 1, 0, blk)
                                  elif d == 4:
                                      tsl = tok_ap(None, 4, blk // 4, blk % 4)
                                  else:
                                      tsl = tok_ap(None, 16, blk, 0)
                                  k.mm(pv[:, jj * 128:(jj + 1) * 128], kv[:, 1, tsl], identb[:, :], True, True,
                                       reads=[B_kv[1], B_ident], wbuf=PSB[pb], first=(jj == 0), last=(jj == 3),
                                       transpose=True)
                              src = pv.rearrange("p (a e) -> p a e", a=8)
                              dst = Vtok[:, (b * 16 + 4 * g) * 2:(b * 16 + 4 * g + 4) * 2, 0:64]
                              if evac_flip[0] % 2 == 0:
                                  k.op(DVE, lambda: nc.vector.tensor_copy(out=dst, in_=src), reads=[PSB[pb], B_V],
                                       writes=[B_Vg[b][g]])
                              else:
                                  k.op(ACT, lambda: nc.scalar.copy(out=dst, in_=src), reads=[PSB[pb], B_V],
                                       writes=[B_Vg[b][g]])
                              evac_flip[0] += 1

                  def post(c):
                      for hh in range(2):
                          k.dma(SP, dnd[hh:hh + 1, :], acc[hh][64:65, :], reads=[B_acc[hh]], writes=[B_dnd])
                      k.dma(SP, acc[0][64:128, :], acc[1][0:64, :], reads=[B_acc[1], B_dnd], writes=[B_acc[0]])
                      src = bass.AP(tensor=dnd.tensor, offset=0, ap=[[16, 128], [S, 2], [1, 16]])
                      k.dma(SP, dq[:], src, reads=[B_dnd], writes=[B_dq])
                      k.op(DVE, lambda: nc.vector.reciprocal(out=dq[:], in_=dq[:]), reads=[B_dq], writes=[B_dq])
                      dst = bass.AP(tensor=dnd2.tensor, offset=0, ap=[[16, 128], [S, 2], [1, 16]])
                      k.dma(SP, dst, dq[:], reads=[B_dq], writes=[B_dnd2])
                      for hh in range(2):
                          src = bass.AP(tensor=dnd2.tensor, offset=hh * S, ap=[[0, 64], [1, S]])
                          k.dma(SP, denb[hh * 64:(hh + 1) * 64, :], src, reads=[B_dnd2], writes=[B_denb])

                  def finalize(c):
                      k.op(DVE, lambda: nc.vector.tensor_tensor(out=attnT[:, c, :], in0=acc[0][:], in1=denb[:],
                                                                op=ALU.mult),
                           reads=[B_acc[0], B_denb], writes=[B_attn[c]])

                  def steps_for(c):
                      pend = []
                      gs = [0]

                      def flush(upto=None, obs=None):
                          keep = []
                          for item in pend:
                              at, ob_l, fn = item
                              if (upto is None and obs is None) or (upto is not None and at <= upto) or \
                                      (obs is not None and any(o in obs for o in ob_l)):
                                  fn()
                              else:
                                  keep.append(item)
                          pend[:] = keep

                      allsteps = []
                      for b, (d, nres, nblk) in enumerate(BRANCH):
                          if d == 16:
                              st_ = [[(r0, 0, 0, 128), (r0 + 1, 0, 128, 128)] for r0 in range(0, 16, 2)]
                          else:
                              st_ = [[(r, n, 0, 256 if n + 1 < nblk else 128)] for r in range(nres)
                                     for n in range(nblk)]
                          allsteps += [(b, d, subs) for subs in st_]

                      def group_of(d, r, n):
                          if d == 1:
                              return n // 4, n % 4
                          if d == 4:
                              return r, n
                          return r // 4, r % 4

                      def qk(si):
                          cur = si % NST
                          b, d, subs = allsteps[si]
                          nmm = 2 * len(subs)
                          i_ = 0
                          for hh in range(2):
                              for (r, n, co, N) in subs:
                                  k.mm(PS[cur][:, hh * 256 + co:hh * 256 + co + N],
                                       kv[:, 0, tok_ap(None, d, r, n)],
                                       qz[hh][:, tok_ap(None, d, r, n, 2 if N == 256 else 1)], True, True,
                                       reads=[B_kv[0], B_qz[hh]], wbuf=PSB[cur], first=(i_ == 0),
                                       last=(i_ == nmm - 1))
                                  i_ += 1

                      for si in range(min(NST - 1, len(allsteps))):
                          qk(si)
                      for si, (b, d, subs) in enumerate(allsteps):
                          cur = si % NST
                          NT = max(co + N for (_, _, co, N) in subs)
                          if si + NST - 1 < len(allsteps):
                              qk(si + NST - 1)
                          sv = PS[cur][:, :].rearrange("p (h n) -> p h n", h=2)[:, :, 0:NT]
                          k.op(DVE, lambda: nc.vector.scalar_tensor_tensor(
                              out=stmp[cur][:, :, 0:NT], in0=sv, scalar=0.125,
                              in1=Tb[:, b, 2 * c:2 * c + 2, 0:NT], op0=ALU.mult, op1=ALU.add),
                              reads=[PSB[cur], B_Tb], writes=[B_st[cur]])
                          k.op(ACT, lambda: nc.scalar.activation(out=PT[cur][:, :, 0:NT],
                                                                 in_=stmp[cur][:, :, 0:NT], func=AF.Exp),
                               reads=[B_st[cur]], writes=[B_PT[cur]])
                          batch = []
                          closing = []
                          for hh in range(2):
                              for (r, n, co, N) in subs:
                                  blk = n if d == 1 else (r * 4 + n if d == 4 else r)
                                  g, slot = group_of(d, r, n)
                                  ob = 4 + 2 * hh + (g % 2)
                                  vaug = Vtok[:, (b * 16 + blk) * 2 + hh, :]
                                  batch.append((PS[ob][0:65, slot * 128:(slot + 1) * 128], vaug,
                                                PT[cur][:, hh, co:co + 128], n == 0, True, ob,
                                                (slot == 0 and n == 0)))
                                  if slot == 3:
                                      closing.append((hh, g, ob))
                                  if N == 256:
                                      g2, slot2 = group_of(d, r, n + 1)
                                      ob2 = 4 + 2 * hh + (g2 % 2)
                                      batch.append((PS[ob2][0:65, slot2 * 128:(slot2 + 1) * 128], vaug,
                                                    PT[cur][:, hh, co + 128:co + 256], True, False, ob2,
                                                    (slot2 == 0)))
                          vb_ = []
                          for (r, n, co, N) in subs:
                              blk = n if d == 1 else (r * 4 + n if d == 4 else r)
                              if B_Vg[b][blk // 4] not in vb_:
                                  vb_.append(B_Vg[b][blk // 4])
                          opening = [ob_ for (_, _, _, _, _, ob_, fi_) in batch if fi_]
                          if opening:
                              flush(obs=opening)
                          for bi_, (o_, l_, r_, st_, sp_, ob_, fi_) in enumerate(batch):
                              k.mm(o_, l_, r_, st_, sp_, reads=vb_ + [B_PT[cur]], wbuf=PSB[ob_], first=fi_,
                                   last=False, sig=(bi_ == len(batch) - 1))
                          if closing and not k.stopped:
                              tokc = PE.last()
                              for (hh, g, ob) in closing:
                                  PSB[ob].w = tokc
                                  PSB[ob].r = {}
                                  PSB[ob].consumed = False
                                  if d == 1:
                                      a_ap = acc[hh][0:65, 512 * g:512 * (g + 1)]
                                      p_ap = PS[ob][0:65, :]
                                  elif d == 4:
                                      a_ap = acc[hh][0:65, g:S:4]
                                      p_ap = PS[ob][0:65, :]
                                  else:
                                      a_ap = acc[hh][0:65, :].rearrange("p (i r) -> p r i", r=16)[:, 4 * g:4 * g + 4, :]
                                      p_ap = PS[ob][0:65, :].rearrange("p (s i) -> p s i", s=4)

                                  def merge(a_ap=a_ap, p_ap=p_ap, ob=ob, hh=hh, b=b, d=d):
                                      if b == 0:
                                          k.op(ACT, lambda: nc.scalar.copy(out=a_ap, in_=p_ap), reads=[PSB[ob]],
                                               writes=[B_acc[hh]])
                                      else:
                                          mi = mt_i[0] % 4
                                          mt_i[0] += 1
                                          m_ap = mtmp[mi][0:65, :]
                                          if d == 16:
                                              m_ap = m_ap.rearrange("p (s i) -> p s i", s=4)
                                          k.op(ACT, lambda: nc.scalar.copy(out=m_ap, in_=p_ap), reads=[PSB[ob]],
                                               writes=[B_mt[mi]])
                                          k.op(POOL, lambda: nc.gpsimd.tensor_tensor(out=a_ap, in0=m_ap,
                                                                                    in1=a_ap, op=ALU.add),
                                               reads=[B_mt[mi], B_acc[hh]], writes=[B_acc[hh]])
                                  pend.append((gs[0] + 2, [ob], merge))
                          flush(upto=gs[0])
                          gs[0] += 1
                      flush()

                  load_wq(0)
                  for c in range(6):
                      if c + 1 < 6:
                          load_wq(c + 1)
                      project(c)
                      vtok(c)
                      if c > 0:
                          finalize(c - 1)
                      steps_for(c)
                      post(c)
                  finalize(5)
                  stage("attfin")
                  for tt in range(4):
                      tsl = slice(tt * 512, (tt + 1) * 512)
                      for c in range(6):
                          i = c % 2
                          k.op(ACT, lambda: nc.scalar.activation(out=sq[i][:], in_=attnT[:, c, tsl], func=AF.Square),
                               reads=[B_attn[c]], writes=[B_sq[i]])
                          k.mm(PS[0][:, :], onesb[:, :], sq[i][:], c == 0, c == 5, reads=[B_ones, B_sq[i]],
                               wbuf=PSB[0], first=(c == 0), last=(c == 5), sig=True)
                      k.op(ACT, lambda: nc.scalar.activation(out=rra[:, tsl], in_=PS[0][:, :], func=AF.Ln, bias=EPS,
                                                             scale=1.0 / AW), reads=[PSB[0]], writes=[B_rra])
                  k.op(ACT, lambda: nc.scalar.activation(out=rra[:], in_=rra[:], func=AF.Exp, scale=-0.5),
                       reads=[B_rra], writes=[B_rra])
                  for c in range(6):
                      k.op(DVE, lambda: nc.vector.scalar_tensor_tensor(out=attnT[:, c, :], in0=attnT[:, c, :],
                                                                       scalar=pc("an_g", c), in1=rra[:],
                                                                       op0=ALU.mult, op1=ALU.mult),
                           reads=[B_attn[c], B_rra, B_pvec], writes=[B_attn[c]])
                  if DEBUG:
                      k.dma(SP, dbg_attn[s], attnT[:], reads=B_attn, writes=[k.buf("dbg_attn")])
                  stage("attn")
                  k.barrier()
              k.barrier()

          with ExitStack() as sF:
              R = sb("R", [128, 8, 1024], F32, sF)
              x1b = sb("x1b", [128, 8, 1024], BF16, sF)
              zb = [sb(f"zb{i}", [128, 512], BF16, sF) for i in range(3)]
              zsq = [sb(f"zsq{i}", [128, 512], BF16, sF) for i in range(3)]
              lm = sb("lm", [128, 512], F32, sF)
              lq = sb("lq", [128, 512], F32, sF)
              wo = [sb(f"wo{i}", [128, 8, 128], BF16, sF) for i in range(3)]
              actT = sb("actT", [128, 22, 1024], BF16, sF)
              Hrow = [[sb(f"Hrow{g}_{i}", [128, 1026], F32, sF) for i in range(2)] for g in range(2)]
              Tt = [[sb(f"Tt{g}_{i}", [128, 1024], F32, sF) for i in range(2)] for g in range(2)]
              sgt = [sb(f"sgt{i}", [128, 1024], F32, sF) for i in range(2)]
              wu = [sb(f"wu{i}", [128, 8, 2, 128], BF16, sF) for i in range(3)]
              wd = [sb(f"wd{i}", [128, 22, 128], BF16, sF) for i in range(2)]
              B_Rt = [[k.buf(f"R{t}_{o}") for o in range(8)] for t in range(2)]
              B_x1t = [[k.buf(f"x1b{t}_{o}") for o in range(8)] for t in range(2)]
              B_wo = [k.buf(f"wo{i}") for i in range(3)]
              B_act = [k.buf(f"actT{i}") for i in range(22)]
              B_H = [[k.buf(f"Hrow{g}_{i}") for i in range(2)] for g in range(2)]
              B_Hh = [[k.buf(f"Hrowh{g}_{i}") for i in range(2)] for g in range(2)]
              B_hc = [k.buf(f"halo{i}") for i in range(44)]
              B_T = [[k.buf(f"Tt{g}_{i}") for i in range(2)] for g in range(2)]
              B_sg = [k.buf(f"sgt{i}") for i in range(2)]
              B_wu = [k.buf(f"wu{i}") for i in range(3)]
              B_wd = [k.buf(f"wd{i}") for i in range(2)]
              B_out = k.buf("outT")
              tmp = dict(zb=zb, zsq=zsq, B_zb=[k.buf(f"zb{i}") for i in range(3)],
                         B_zsq=[k.buf(f"zsq{i}") for i in range(3)],
                         m=lm, q=lq, B_m=k.buf("lm"), B_q=k.buf("lq"), rot=[0])
              MB = [0, 1, 4, 5, 6, 7]
              k.op(POOL, lambda: nc.gpsimd.memset(halo[:], 0.0), writes=[B_halo] + B_hc)
              wo_n = [0]
              wd_n = [0]

              def load_wo(oc):
                  i = wo_n[0] % 3
                  wo_n[0] += 1
                  k.dma(POOL, wo[i][:], w_out_v[:, :, oc * 128:(oc + 1) * 128], writes=[B_wo[i]])
                  return i

              def load_wu(fp):
                  i = fp % 3
                  k.dma(POOL, wu[i][:, :, 0, :], w_up_v[:, :, fp * 128:(fp + 1) * 128], writes=[B_wu[i]])
                  k.dma(POOL, wu[i][:, :, 1, :], w_up_v[:, :, DFF + fp * 128:DFF + (fp + 1) * 128],
                        writes=[B_wu[i]])

              def load_wd(oc):
                  i = wd_n[0] % 2
                  wd_n[0] += 1
                  k.dma(POOL, wd[i][:, 0:11, :], w_down_v[:, 0:11, oc * 128:(oc + 1) * 128], writes=[B_wd[i]])
                  k.dma(POOL, wd[i][:, 11:22, :], w_down_v[:, 11:22, oc * 128:(oc + 1) * 128], writes=[B_wd[i]])
                  return i

              mmi = 0
              for hf in range(2):
                  t0 = hf * 1024
                  for tt in range(2):
                      tl = slice(tt * 512, (tt + 1) * 512)
                      for kc in range(8):
                          k.dma(SP, R[:, kc, tl], xT[s, kc * 128:(kc + 1) * 128, t0 + tt * 512:t0 + (tt + 1) * 512],
                                writes=[B_Rt[tt][kc]])
                  for tt in range(2):
                      tl = slice(tt * 512, (tt + 1) * 512)
                      tg = slice(t0 + tt * 512, t0 + (tt + 1) * 512)
                      wq_ = [load_wo(0), load_wo(1)]
                      for oc in range(8):
                          if oc + 2 < 8:
                              wq_.append(load_wo(oc + 2))
                          wi = wq_[oc]
                          pb = MB[mmi % 6]
                          mmi += 1
                          for kc in range(8):
                              rhs = attnT[:, kc, tg] if kc < 6 else uN[:, kc - 6, tg]
                              rb = B_attn[kc] if kc < 6 else B_uN
                              k.mm(PS[pb][:, :], wo[wi][:, kc, :], rhs, kc == 0, kc == 7,
                                   reads=[B_wo[wi], rb], wbuf=PSB[pb], first=(kc == 0), last=(kc == 7))
                          k.op(DVE, lambda: nc.vector.scalar_tensor_tensor(out=R[:, oc, tl], in0=R[:, oc, tl],
                                                                           scalar=ALPHA, in1=PS[pb][:, :],
                                                                           op0=ALU.mult, op1=ALU.add),
                               reads=[B_Rt[tt][oc], PSB[pb]], writes=[B_Rt[tt][oc]])
                      layer_norm_tile(R, B_Rt[tt], tl, "ln1_g", "ln1_b", (2, 3), tmp, outb=x1b, B_outb=B_x1t[tt])
                  if DEBUG:
                      k.dma(SP, dbg_x1[s][:, :, t0:t0 + 1024], x1b[:], reads=B_x1t[0] + B_x1t[1],
                            writes=[k.buf("dbg_x1")])
                  if hf == 0:
                      stage("C0")

                  def silu_mul(fq):
                      bq = fq % 2
                      k.op(ACT, lambda: nc.scalar.activation(out=sgt[bq][:], in_=Tt[0][bq][:], func=AF.Silu),
                           reads=[B_T[0][bq]], writes=[B_sg[bq]])
                      k.op(DVE, lambda: nc.vector.tensor_tensor(out=actT[:, fq, :], in0=sgt[bq][:],
                                                                in1=Tt[1][bq][:], op=ALU.mult),
                           reads=[B_sg[bq], B_T[1][bq]], writes=[B_act[fq]])

                  load_wu(0)
                  load_wu(1)
                  for fp in range(22):
                      if fp + 2 < 22:
                          load_wu(fp + 2)
                      wi = fp % 3
                      bi = fp % 2
                      for gv in range(2):
                          ch = gv * 22 + fp
                          H = Hrow[gv][bi]
                          BH = B_H[gv][bi]
                          BHh = B_Hh[gv][bi]
                          k.op(POOL, lambda: nc.gpsimd.tensor_copy(out=H[:, 0:2], in_=halo[:, ch, :]),
                               reads=[B_hc[ch]], writes=[BHh])
                          for tt in range(2):
                              tl = slice(tt * 512, (tt + 1) * 512)
                              pb = MB[mmi % 6]
                              mmi += 1
                              for kc in range(8):
                                  k.mm(PS[pb][:, :], wu[wi][:, kc, gv, :], x1b[:, kc, tl], kc == 0, kc == 7,
                                       reads=[B_wu[wi], B_x1t[tt][kc]], wbuf=PSB[pb], first=(kc == 0), last=(kc == 7))
                              k.op(ACT, lambda: nc.scalar.copy(out=H[:, 2 + tt * 512:2 + (tt + 1) * 512],
                                                               in_=PS[pb][:, :]), reads=[PSB[pb]], writes=[BH])
                          k.op(ACT, lambda: nc.scalar.copy(out=halo[:, ch, :], in_=H[:, 1024:1026]),
                               reads=[BH], writes=[B_hc[ch]])
                          T = Tt[gv][bi]
                          BT = B_T[gv][bi]
                          k.op(ACT, lambda: nc.scalar.activation(out=T[:], in_=H[:, 2:1026], func=AF.Identity,
                                                                 bias=pc("fb", ch), scale=pc("fw", 2 * 44 + ch)),
                               reads=[BH, B_pvec], writes=[BT])
                          k.op(DVE, lambda: nc.vector.scalar_tensor_tensor(out=T[:], in0=H[:, 1:1025],
                                                                           scalar=pc("fw", 44 + ch), in1=T[:],
                                                                           op0=ALU.mult, op1=ALU.add),
                               reads=[BH, BHh, BT, B_pvec], writes=[BT])
                          k.op(DVE, lambda: nc.vector.scalar_tensor_tensor(out=T[:], in0=H[:, 0:1024],
                                                                           scalar=pc("fw", ch), in1=T[:],
                                                                           op0=ALU.mult, op1=ALU.add),
                               reads=[BH, BHh, BT, B_pvec], writes=[BT])
                      if fp >= 1:
                          silu_mul(fp - 1)
                      if fp == 19:
                          wdq = [load_wd(0)]
                      if fp == 20:
                          wdq.append(load_wd(1))
                  silu_mul(21)
                  if DEBUG and hf == 0 and s == 0:
                      k.dma(SP, dbg_act, actT[:], reads=B_act, writes=[k.buf("dbg_act")])
                  if hf == 0:
                      stage("D0")
                  for oc in range(8):
                      wi = wdq[oc]
                      for tt in range(2):
                          tl = slice(tt * 512, (tt + 1) * 512)
                          pb = MB[mmi % 6]
                          mmi += 1
                          for kc in range(22):
                              k.mm(PS[pb][:, :], wd[wi][:, kc, :], actT[:, kc, tl], kc == 0, kc == 21,
                                   reads=[B_wd[wi], B_act[kc]], wbuf=PSB[pb], first=(kc == 0), last=(kc == 21))
                          k.op(DVE, lambda: nc.vector.scalar_tensor_tensor(out=R[:, oc, tl], in0=R[:, oc, tl],
                                                                           scalar=ALPHA, in1=PS[pb][:, :],
                                                                           op0=ALU.mult, op1=ALU.add),
                               reads=[B_Rt[tt][oc], PSB[pb]], writes=[B_Rt[tt][oc]])
                      if oc + 2 < 8:
                          wdq.append(load_wd(oc + 2))
                  for tt in range(2):
                      tl = slice(tt * 512, (tt + 1) * 512)
                      layer_norm_tile(R, B_Rt[tt], tl, "ln2_g", "ln2_b", (2, 3), tmp)
                      for kc in range(8):
                          k.dma(SP, outT[s, kc * 128:(kc + 1) * 128, t0 + tt * 512:t0 + (tt + 1) * 512],
                                R[:, kc, tl], reads=[B_Rt[tt][kc]], writes=[], sbuf=B_out)
                  if hf == 0:
                      stage("E0")
              k.barrier()
    except _Stop:
        pass
    k.barrier()
    es.close()
    print("semaphores used:", k.nsem, "PE sigs", PE.n, "ACT", ACT.n, "DVE", DVE.n, "POOL", POOL.n)
    return nc


_NC_CACHE = {}


def kernel(**inp):
    x = np.asarray(inp["x"], np.float32)
    pv = pack_params(inp)
    relx = np.concatenate([np.asarray(inp["rel_table"], np.float32), np.ones((1, 12), np.float32)], axis=0)
    oh = onehot_const()
    ident = np.eye(128, dtype=np.float32)
    jmat = np.ascontiguousarray(np.eye(128, dtype=np.float32)[::-1])
    w_in = np.ascontiguousarray(np.asarray(inp["w_in"], np.float32)[0])
    w_out = np.ascontiguousarray(np.asarray(inp["w_out"], np.float32)[0])
    w_up = np.ascontiguousarray(np.asarray(inp["w_up"], np.float32)[0])
    w_down = np.ascontiguousarray(np.asarray(inp["w_down"], np.float32)[0])
    if "nc" not in _NC_CACHE:
        _NC_CACHE["nc"] = build()
    nc = _NC_CACHE["nc"]
    in_maps = []
    for cid in range(NCORE):
        xs = x[cid * NSEQ:(cid + 1) * NSEQ]
        xTs = np.ascontiguousarray(xs.transpose(0, 2, 1))
        in_maps.append({"xT": xTs, "w_in": w_in, "w_out": w_out, "w_up": w_up, "w_down": w_down,
                        "pvec": pv, "relx": relx, "oh": oh, "ident": ident, "jmat": jmat})
    res = run_bass_kernel_spmd(nc, in_maps, core_ids=list(range(NCORE)))
    outs = []
    for cid in range(NCORE):
        o = np.asarray(res.results[cid]["outT"], np.float32)
        outs.append(o.transpose(0, 2, 1))
    out = np.ascontiguousarray(np.concatenate(outs, axis=0))
    if DEBUG:
        kernel.last_results = res.results
    return out
```

```python
import math
from contextlib import ExitStack

import numpy as np
import concourse.bass as bass
import concourse.mybir as mybir
from concourse.bass_utils import run_bass_kernel_spmd

F32 = mybir.dt.float32
BF16 = mybir.dt.bfloat16
AF = mybir.ActivationFunctionType
ALU = mybir.AluOpType

D = 1024
S = 2048
NSEQ = 2
NCORE = 8
HD = 64
NH = 12
AW = 768
CW = 256
INW = 2816
DFF = 2816
CK = 31
ALPHA = 2.0 ** 0.25
EPS = 1e-5
MASKV = -30000.0
BRANCH = ((1, 1, 16), (4, 4, 4), (16, 16, 1))

DEBUG = False

PCOL = {}
_o = 0
for _n, _w in (("b_in", 22), ("convw", 62), ("conv_b", 2), ("cln_g", 2), ("cln_b", 2), ("cn_g", 2),
               ("an_g", 6), ("ln1_g", 8), ("ln1_b", 8), ("fw", 132), ("fb", 44), ("ln2_g", 8), ("ln2_b", 8)):
    PCOL[_n] = _o
    _o += _w
NPCOL = _o


def _cols(v):
    v = np.asarray(v, np.float32).reshape(-1, 128)
    return v.T


def pack_params(inp):
    pv = np.zeros((128, NPCOL), np.float32)

    def put(name, arr):
        pv[:, PCOL[name]:PCOL[name] + arr.shape[1]] = arr
    put("b_in", _cols(inp["b_in"][0]))
    cw = np.asarray(inp["conv_w"][0], np.float32)
    put("convw", np.concatenate([cw[:, 0:128].T, cw[:, 128:256].T], axis=1))
    put("conv_b", _cols(inp["conv_b"][0]))
    put("cln_g", _cols(inp["conv_ln_g"][0]))
    put("cln_b", _cols(inp["conv_ln_b"][0]))
    put("cn_g", _cols(inp["conv_norm_g"][0]))
    put("an_g", _cols(inp["attn_norm_g"][0]))
    put("ln1_g", _cols(inp["ln1_g"][0]))
    put("ln1_b", _cols(inp["ln1_b"][0]))
    fw = np.asarray(inp["ffn_conv_w"][0], np.float32)
    put("fw", np.concatenate([_cols(fw[j]) for j in range(3)], axis=1))
    put("fb", _cols(inp["ffn_conv_b"][0]))
    put("ln2_g", _cols(inp["ln2_g"][0]))
    put("ln2_b", _cols(inp["ln2_b"][0]))
    return pv


def t5_bucket_np(dist):
    dist = np.asarray(dist, np.int64)
    exact = 16
    d_f = np.maximum(dist, 1).astype(np.float32)
    large = exact + (np.log(d_f / np.float32(exact)) / np.float32(math.log(2048 / exact))
                     * np.float32(32 - exact)).astype(np.int32)
    large = np.minimum(large, 31)
    return np.where(dist < exact, dist, large)


def onehot_const():
    oh = np.zeros((33, 3 * 384), np.float32)
    for b, (d, _, _) in enumerate(BRANCH):
        for i in range(383):
            delta = i - 127
            if 0 <= delta <= 128:
                oh[int(t5_bucket_np(delta * d)), b * 384 + i] = 1.0
            else:
                oh[32, b * 384 + i] = MASKV
    return oh


class Tok:
    __slots__ = ("sem", "sid", "val")

    def __init__(self, sem, sid, val):
        self.sem, self.sid, self.val = sem, sid, val


class Eng:
    def __init__(self, nc, e, name):
        self.e = e
        self.name = name
        self.sem = nc.alloc_semaphore(name="es_" + name)
        self.sid = "E" + name
        self.n = 0
        self.seen = {}

    def wait(self, tok):
        if tok is None:
            return
        if self.seen.get(tok.sid, 0) >= tok.val:
            return
        self.e.wait_ge(tok.sem, tok.val)
        self.seen[tok.sid] = tok.val

    def sig(self, ins):
        self.n += 1
        ins.then_inc(self.sem, 1)
        return Tok(self.sem, self.sid, self.n)

    def last(self):
        return Tok(self.sem, self.sid, self.n) if self.n else None


class Buf:
    def __init__(self, name):
        self.name = name
        self.w = None
        self.r = {}
        self.dsem = None
        self.dn = 0


class K:
    def __init__(self):
        self.nc = nc = bass.Bass("TRN2", target_bir_lowering=False)
        self.PE = Eng(nc, nc.tensor, "pe")
        self.ACT = Eng(nc, nc.scalar, "act")
        self.DVE = Eng(nc, nc.vector, "dve")
        self.POOL = Eng(nc, nc.gpsimd, "pool")
        self.SP = Eng(nc, nc.sync, "sp")
        self.engs = [self.PE, self.ACT, self.DVE, self.POOL, self.SP]
        self.pe_pending = []
        self.bufs = {}
        self.dma_bufs = []
        self.nsem = 5
        self.stopped = False

    def buf(self, name):
        if name not in self.bufs:
            self.bufs[name] = Buf(name)
        return self.bufs[name]

    def deps(self, E, reads, writes):
        for b in reads:
            E.wait(b.w)
        for b in writes:
            assert b not in self.pe_pending, f"write to {b.name} with pending PE reads"
            if b.w is not None and b.w.sid != E.sid:
                E.wait(b.w)
            for t in b.r.values():
                if t.sid != E.sid:
                    E.wait(t)

    def done(self, tok, reads, writes):
        for b in reads:
            b.r[tok.sid] = tok
        for b in writes:
            b.w = tok
            b.r = {}

    def op(self, E, fn, reads=(), writes=()):
        if self.stopped:
            return None
        self.deps(E, reads, writes)
        ins = fn()
        tok = E.sig(ins)
        self.done(tok, reads, writes)
        return tok

    def dma(self, Q, out, in_, reads=(), writes=(), sbuf=None):
        if self.stopped:
            return None
        self.deps(Q, reads, writes)
        ins = Q.e.dma_start(out=out, in_=in_)
        b = sbuf if sbuf is not None else writes[0]
        if b.dsem is None:
            b.dsem = self.nc.alloc_semaphore(name="ds_" + b.name)
            self.nsem += 1
            self.dma_bufs.append(b)
        b.dn += 16
        ins.then_inc(b.dsem, 16)
        tok = Tok(b.dsem, "D" + b.name, b.dn)
        self.done(tok, reads, writes)
        return tok

    def mm(self, out, lhsT, rhs, start, stop, reads=(), wbuf=None, first=False, last=False, sig=False,
           transpose=False):
        PE = self.PE
        if self.stopped:
            return None
        for b in reads:
            PE.wait(b.w)
        if first and wbuf is not None:
            if wbuf.w is not None and wbuf.w.sid != PE.sid:
                PE.wait(wbuf.w)
            for t in wbuf.r.values():
                if t.sid != PE.sid:
                    PE.wait(t)
        if transpose:
            ins = self.nc.tensor.transpose(out, lhsT, rhs)
        else:
            ins = self.nc.tensor.matmul(out, lhsT, rhs, start=start, stop=stop)
        for b in reads:
            if b not in self.pe_pending:
                self.pe_pending.append(b)
        if last or sig:
            tok = PE.sig(ins)
            for b in self.pe_pending:
                b.r[tok.sid] = tok
            self.pe_pending = []
            if last and wbuf is not None:
                wbuf.w = tok
                wbuf.r = {}
            return tok
        return None

    def barrier(self):
        if self.stopped:
            return
        assert not self.pe_pending
        toks = [e.last() for e in self.engs]
        for b in self.dma_bufs:
            if b.dn:
                toks.append(Tok(b.dsem, "D" + b.name, b.dn))
        for e in self.engs:
            for t in toks:
                if t is not None and t.sid != e.sid:
                    e.wait(t)


def tok_ap(base, d, r, n, nblk=1):
    if d == 1:
        return slice(128 * n, 128 * (n + nblk))
    if d == 4:
        return slice(512 * n + r, 512 * (n + nblk), 4)
    return slice(r, S, 16)


class _Stop(Exception):
    pass


def build(stop=None, nseq=NSEQ):
    k = K()

    def stage(name):
        if stop == name and not k.stopped:
            k.barrier()
            k.stopped = True
    nc = k.nc
    PE, ACT, DVE, POOL, SP = k.PE, k.ACT, k.DVE, k.POOL, k.SP
    es = ExitStack()

    def dram(name, shape, dt, kind):
        return nc.dram_tensor(name, list(shape), dt, kind=kind).ap()

    xT = dram("xT", [NSEQ, D, S], F32, "ExternalInput")
    w_in = dram("w_in", [D, INW], F32, "ExternalInput")
    w_out = dram("w_out", [D, D], F32, "ExternalInput")
    w_up = dram("w_up", [D, 2 * DFF], F32, "ExternalInput")
    w_down = dram("w_down", [DFF, D], F32, "ExternalInput")
    pvec_d = dram("pvec", [128, NPCOL], F32, "ExternalInput")
    relx_d = dram("relx", [33, 12], F32, "ExternalInput")
    oh_d = dram("oh", [33, 3 * 384], F32, "ExternalInput")
    ident_d = dram("ident", [128, 128], F32, "ExternalInput")
    jmat_d = dram("jmat", [128, 128], F32, "ExternalInput")
    outT = dram("outT", [NSEQ, D, S], F32, "ExternalOutput")
    cdram = dram("cdram", [12, 3, 384], F32, "Internal")
    dnd = dram("dnd", [2, S], F32, "Internal")
    dnd2 = dram("dnd2", [2, S], F32, "Internal")
    if DEBUG:
        dbg_attn = dram("dbg_attn", [NSEQ, 128, 6, S], BF16, "ExternalOutput")
        dbg_un = dram("dbg_un", [NSEQ, 128, 2, S], BF16, "ExternalOutput")
        dbg_x1 = dram("dbg_x1", [NSEQ, 128, 8, S], BF16, "ExternalOutput")
        dbg_tb = dram("dbg_tb", [128, 3, 12, 256], F32, "ExternalOutput")
        dbg_act = dram("dbg_act", [128, 22, 1024], BF16, "ExternalOutput")

    w_in_v = w_in.rearrange("(kc p) n -> p kc n", p=128)
    w_out_v = w_out.rearrange("(kc p) n -> p kc n", p=128)
    w_up_v = w_up.rearrange("(kc p) n -> p kc n", p=128)
    w_down_v = w_down.rearrange("(kc p) n -> p kc n", p=128)

    _cnt = [0]

    def sb(name, shape, dt, stack=es):
        _cnt[0] += 1
        return stack.enter_context(nc.sbuf_tensor(f"{name}_{_cnt[0]}", list(shape), dt))

    PSA = es.enter_context(nc.psum_tensor("psA", [128, 1024], F32))
    PSBt = es.enter_context(nc.psum_tensor("psB", [128, 1024], F32))
    PS = [PSA[:, 0:512], PSA[:, 512:1024], PSBt[:, 0:512], PSBt[:, 512:1024]]
    PS += [es.enter_context(nc.psum_tensor(f"ps{i}", [128, 512], F32))[:, :] for i in range(4, 8)]
    PS2 = [PSA, PSBt]
    PSB = [k.buf(f"ps{i}") for i in range(8)]

    pvec = sb("pvec_sb", [128, NPCOL], F32)
    pder = sb("pder", [128, 64], F32)
    identb = sb("identb", [128, 128], BF16)
    onesb = sb("onesb", [128, 128], BF16)
    jmat = sb("jmat_sb", [128, 128], F32)
    attnT = sb("attnT", [128, 6, S], BF16)
    uN = sb("uN", [128, 2, S], BF16)
    halo = sb("halo", [128, 44, 2], F32)
    B_pvec, B_pder, B_ident, B_ones, B_j = (k.buf(n) for n in ("pvec", "pder", "ident", "ones", "jmat"))
    B_attn = [k.buf(f"attnT{c}") for c in range(6)]
    B_uN = k.buf("uN")
    B_halo = k.buf("halo")

    def pc(name, j=0, n=1):
        o = PCOL[name] + j
        return pvec[:, o:o + n]

    k.dma(SP, pvec[:], pvec_d, writes=[B_pvec])
    k.dma(POOL, identb[:], ident_d, writes=[B_ident])
    k.dma(SP, jmat[:], jmat_d, writes=[B_j])
    k.op(DVE, lambda: nc.vector.memset(onesb[:], 1.0), writes=[B_ones])
    k.op(DVE, lambda: nc.vector.tensor_scalar(out=pder[:, 0:62], in0=pc("convw", 0, 62), scalar1=0.5, scalar2=None,
                                              op0=ALU.mult), reads=[B_pvec], writes=[B_pder])
    k.op(DVE, lambda: nc.vector.tensor_scalar(out=pder[:, 62:64], in0=pc("b_in", 20, 2), scalar1=0.5, scalar2=None,
                                              op0=ALU.mult), reads=[B_pvec], writes=[B_pder])

    with ExitStack() as s0:
        relx = sb("relx_sb", [33, 12], F32, s0)
        ohs = sb("oh_sb", [33, 3 * 384], F32, s0)
        csb = sb("csb", [12, 3, 384], F32, s0)
        B_relx, B_oh, B_csb, B_cd = k.buf("relx"), k.buf("oh"), k.buf("csb"), k.buf("cdram")
        k.dma(SP, relx[:], relx_d, writes=[B_relx])
        k.dma(SP, ohs[:], oh_d, writes=[B_oh])
        for b in range(3):
            k.mm(PS[b][0:12, 0:384], relx[:, :], ohs[:, b * 384:(b + 1) * 384], True, True,
                 reads=[B_relx, B_oh], wbuf=PSB[b], first=True, last=True)
            k.op(DVE, lambda b=b: nc.vector.tensor_copy(out=csb[:, b, :], in_=PS[b][0:12, 0:384]),
                 reads=[PSB[b]], writes=[B_csb])
        k.dma(SP, cdram, csb[:], reads=[B_csb], writes=[B_cd])
        k.barrier()

    def layer_norm_tile(R, B_Rc, tsl, gname, bname, stat_ps, tmp, outb=None, B_outb=None):
        zb, zsq, B_zb, B_zsq = tmp["zb"], tmp["zsq"], tmp["B_zb"], tmp["B_zsq"]
        m, q, B_m, B_q = tmp["m"], tmp["q"], tmp["B_m"], tmp["B_q"]
        p1, p2 = stat_ps
        nb = len(zb)
        for oc in range(8):
            i = tmp["rot"][0] % nb
            tmp["rot"][0] += 1
            k.op(DVE, lambda: nc.vector.tensor_copy(out=zb[i][:], in_=R[:, oc, tsl]), reads=[B_Rc[oc]],
                 writes=[B_zb[i]])
            k.op(ACT, lambda: nc.scalar.activation(out=zsq[i][:], in_=R[:, oc, tsl], func=AF.Square),
                 reads=[B_Rc[oc]], writes=[B_zsq[i]])
            k.mm(PS[p1][:, :], onesb[:, :], zb[i][:], oc == 0, oc == 7, reads=[B_ones, B_zb[i]],
                 wbuf=PSB[p1], first=(oc == 0), last=(oc == 7), sig=True)
            k.mm(PS[p2][:, :], onesb[:, :], zsq[i][:], oc == 0, oc == 7, reads=[B_ones, B_zsq[i]],
                 wbuf=PSB[p2], first=(oc == 0), last=(oc == 7), sig=True)
        k.op(DVE, lambda: nc.vector.tensor_scalar(out=m[:], in0=PS[p1][:, :], scalar1=1.0 / D, scalar2=None,
                                                  op0=ALU.mult), reads=[PSB[p1]], writes=[B_m])
        k.op(DVE, lambda: nc.vector.tensor_tensor(out=q[:], in0=m[:], in1=m[:], op=ALU.mult),
             reads=[B_m], writes=[B_q])
        k.op(DVE, lambda: nc.vector.scalar_tensor_tensor(out=q[:], in0=PS[p2][:, :], scalar=1.0 / D, in1=q[:],
                                                         op0=ALU.mult, op1=ALU.subtract),
             reads=[PSB[p2], B_q], writes=[B_q])
        k.op(ACT, lambda: nc.scalar.activation(out=q[:], in_=q[:], func=AF.Ln, bias=EPS, scale=1.0),
             reads=[B_q], writes=[B_q])
        k.op(ACT, lambda: nc.scalar.activation(out=q[:], in_=q[:], func=AF.Exp, scale=-0.5),
             reads=[B_q], writes=[B_q])
        k.op(DVE, lambda: nc.vector.tensor_tensor(out=m[:], in0=m[:], in1=q[:], op=ALU.mult),
             reads=[B_m, B_q], writes=[B_m])
        def ln_mult(oc):
            k.op(DVE, lambda: nc.vector.tensor_tensor(out=R[:, oc, tsl], in0=R[:, oc, tsl], in1=q[:], op=ALU.mult),
                 reads=[B_Rc[oc], B_q], writes=[B_Rc[oc]])

        def ln_rest(oc):
            k.op(DVE, lambda: nc.vector.tensor_tensor(out=R[:, oc, tsl], in0=R[:, oc, tsl], in1=m[:],
                                                      op=ALU.subtract), reads=[B_Rc[oc], B_m], writes=[B_Rc[oc]])
            k.op(ACT, lambda: nc.scalar.activation(out=R[:, oc, tsl], in_=R[:, oc, tsl], func=AF.Identity,
                                                   bias=pc(bname, oc), scale=pc(gname, oc)),
                 reads=[B_Rc[oc], B_pvec], writes=[B_Rc[oc]])
            if outb is not None:
                k.op(ACT, lambda: nc.scalar.copy(out=outb[:, oc, tsl], in_=R[:, oc, tsl]),
                     reads=[B_Rc[oc]], writes=[B_outb[oc]])

        ln_mult(0)
        for oc in range(1, 8):
            ln_mult(oc)
            ln_rest(oc - 1)
        ln_rest(7)

    stage("setup")
    try:
      for s in range(nseq):
          with ExitStack() as sAB:
              Tb = sb("Tb", [128, 3, 12, 256], F32, sAB)
              xTb = sb("xTb", [128, 8, S], BF16, sAB)
              B_Tb = k.buf("Tb")
              B_x = [k.buf(f"xTb{i}") for i in range(8)]
              for kc in range(8):
                  k.dma(POOL, xTb[:, kc, :], xT[s, kc * 128:(kc + 1) * 128, :], writes=[B_x[kc]])

              with ExitStack() as sC:
                  convD = sb("convD", [128, 2, CK, 128], BF16, sC)
                  wc = sb("wc", [128, 8, 512], BF16, sC)
                  uB = sb("uB", [128, 2, S + 32], BF16, sC)
                  cv = sb("cv", [128, 2, S], F32, sC)
                  t1 = [sb(f"ct1_{i}", [128, 512], F32, sC) for i in range(2)]
                  t2 = [sb(f"ct2_{i}", [128, 512], F32, sC) for i in range(2)]
                  cb = [sb(f"cb_{i}", [128, 512], BF16, sC) for i in range(2)]
                  cq = [sb(f"cq_{i}", [128, 512], BF16, sC) for i in range(2)]
                  cm = sb("cm", [128, S], F32, sC)
                  cr = sb("cr", [128, S], F32, sC)
                  B_cD, B_wc, B_uB, B_cv = k.buf("convD"), k.buf("wc"), k.buf("uB"), k.buf("cv")
                  B_cDc = [k.buf("convD0"), k.buf("convD1")]
                  B_t1 = [k.buf(f"ct1_{i}") for i in range(2)]
                  B_t2 = [k.buf(f"ct2_{i}") for i in range(2)]
                  B_cb = [k.buf(f"cb_{i}") for i in range(2)]
                  B_cq = [k.buf(f"cq_{i}") for i in range(2)]
                  B_cm, B_cr = k.buf("cm"), k.buf("cr")

                  for kc in range(0, 8, 2):
                      k.dma(POOL, wc[:, kc:kc + 2, :], w_in_v[:, kc:kc + 2, 2304:2816], writes=[B_wc])
                  for cc in range(2):
                      for j in range(CK):
                          if cc == 0:
                              k.op(DVE, lambda: nc.vector.tensor_scalar(out=convD[:, cc, j, :], in0=identb[:, :],
                                                                        scalar1=pder[:, cc * CK + j:cc * CK + j + 1],
                                                                        scalar2=None, op0=ALU.mult),
                                   reads=[B_ident, B_pder], writes=[B_cDc[cc]])
                          else:
                              k.op(ACT, lambda: nc.scalar.activation(out=convD[:, cc, j, :], in_=identb[:, :],
                                                                     func=AF.Copy,
                                                                     scale=pder[:, cc * CK + j:cc * CK + j + 1]),
                                   reads=[B_ident, B_pder], writes=[B_cDc[cc]])
                  k.op(DVE, lambda: nc.vector.memset(uB[:, :, 0:30], 0.0), writes=[B_uB])
                  it = 0
                  for cc in range(2):
                      for tt in range(4):
                          tsl = slice(tt * 512, (tt + 1) * 512)
                          i = it % 2
                          pa, pg = 2 * i, 2 * i + 1
                          for kc in range(8):
                              k.mm(PS[pa][:, :], wc[:, kc, cc * 128:(cc + 1) * 128], xTb[:, kc, tsl], kc == 0, kc == 7,
                                   reads=[B_wc, B_x[kc]], wbuf=PSB[pa], first=(kc == 0), last=(kc == 7))
                          for kc in range(8):
                              k.mm(PS[pg][:, :], wc[:, kc, 256 + cc * 128:256 + (cc + 1) * 128], xTb[:, kc, tsl],
                                   kc == 0, kc == 7, reads=[B_wc, B_x[kc]], wbuf=PSB[pg], first=(kc == 0),
                                   last=(kc == 7))
                          k.op(ACT, lambda: nc.scalar.activation(out=t1[i][:], in_=PS[pg][:, :], func=AF.Tanh,
                                                                 bias=pder[:, 62 + cc:63 + cc], scale=0.5),
                               reads=[PSB[pg], B_pder], writes=[B_t1[i]])
                          k.op(ACT, lambda: nc.scalar.activation(out=t2[i][:], in_=PS[pa][:, :], func=AF.Identity,
                                                                 bias=pc("b_in", 18 + cc), scale=1.0),
                               reads=[PSB[pa], B_pvec], writes=[B_t2[i]])
                          k.op(DVE, lambda: nc.vector.scalar_tensor_tensor(
                              out=uB[:, cc, 30 + tt * 512:30 + (tt + 1) * 512], in0=t1[i][:], scalar=1.0,
                              in1=t2[i][:], op0=ALU.add, op1=ALU.mult),
                              reads=[B_t1[i], B_t2[i]], writes=[B_uB])
                          it += 1
                  for tt in range(4):
                      tsl = slice(tt * 512, (tt + 1) * 512)
                      for cc in range(2):
                          pb = (tt * 2 + cc) % 2
                          for j in range(CK):
                              k.mm(PS[pb][:, :], convD[:, cc, j, :], uB[:, cc, tt * 512 + j:tt * 512 + j + 512],
                                   j == 0, j == CK - 1, reads=[B_cDc[cc], B_uB], wbuf=PSB[pb], first=(j == 0),
                                   last=(j == CK - 1))
                          k.op(ACT, lambda: nc.scalar.activation(out=cv[:, cc, tsl], in_=PS[pb][:, :],
                                                                 func=AF.Identity, bias=pc("conv_b", cc), scale=1.0),
                               reads=[PSB[pb], B_pvec], writes=[B_cv])
                          k.op(ACT, lambda: nc.scalar.activation(out=cq[cc][:], in_=PS[pb][:, :], func=AF.Square,
                                                                 bias=pc("conv_b", cc), scale=1.0),
                               reads=[PSB[pb], B_pvec], writes=[B_cq[cc]])
                          k.op(DVE, lambda: nc.vector.tensor_copy(out=cb[cc][:], in_=cv[:, cc, tsl]),
                               reads=[B_cv], writes=[B_cb[cc]])
                      for cc in range(2):
                          k.mm(PS[2][:, :], onesb[:, :], cb[cc][:], cc == 0, cc == 1, reads=[B_ones, B_cb[cc]],
                               wbuf=PSB[2], first=(cc == 0), last=(cc == 1), sig=True)
                          k.mm(PS[3][:, :], onesb[:, :], cq[cc][:], cc == 0, cc == 1, reads=[B_ones, B_cq[cc]],
                               wbuf=PSB[3], first=(cc == 0), last=(cc == 1), sig=True)
                      k.op(DVE, lambda: nc.vector.tensor_scalar(out=cm[:, tsl], in0=PS[2][:, :], scalar1=1.0 / CW,
                                                                scalar2=None, op0=ALU.mult),
                           reads=[PSB[2]], writes=[B_cm])
                      k.op(DVE, lambda: nc.vector.tensor_tensor(out=cr[:, tsl], in0=cm[:, tsl], in1=cm[:, tsl],
                                                                op=ALU.mult), reads=[B_cm], writes=[B_cr])
                      k.op(DVE, lambda: nc.vector.scalar_tensor_tensor(out=cr[:, tsl], in0=PS[3][:, :],
                                                                       scalar=1.0 / CW, in1=cr[:, tsl],
                                                                       op0=ALU.mult, op1=ALU.subtract),
                           reads=[PSB[3], B_cr], writes=[B_cr])
                  with ExitStack() as sT:
                      Hk = [sb(f"Hk{i}", [128, 4, 2, 128], F32, sT) for i in range(2)]
                      B_Hk = [k.buf(f"Hk{i}") for i in range(2)]
                      it = 0
                      for b in range(3):
                          for h0 in range(0, 12, 4):
                              i = it % 2
                              if b < 2:
                                  src = bass.AP(tensor=cdram.tensor, offset=h0 * 3 * 384 + b * 384,
                                                ap=[[1, 128], [3 * 384, 4], [128, 2], [1, 128]])
                                  k.dma(SP, Hk[i][:], src, reads=[k.buf("cdram")], writes=[B_Hk[i]])
                              else:
                                  for part in range(2):
                                      src = bass.AP(tensor=cdram.tensor, offset=h0 * 3 * 384 + b * 384,
                                                    ap=[[1, 128], [3 * 384, 4], [1, 128]])
                                      k.dma(SP, Hk[i][:, :, part, :], src, reads=[k.buf("cdram")], writes=[B_Hk[i]])
                              for half in range(2):
                                  pb = (it * 2 + half) % 2
                                  k.mm(PS[pb][:, :], jmat[:, :],
                                       Hk[i][:, 2 * half:2 * half + 2, :, :].rearrange("p a b c -> p (a b c)"),
                                       True, True, reads=[B_j, B_Hk[i]], wbuf=PSB[pb], first=True, last=True)
                                  k.op(DVE, lambda: nc.vector.tensor_copy(
                                      out=Tb[:, b, h0 + 2 * half:h0 + 2 * half + 2, :].rearrange("p a b -> p (a b)"),
                                      in_=PS[pb][:, :]), reads=[PSB[pb]], writes=[B_Tb])
                              it += 1
                  if DEBUG and s == 0:
                      k.dma(SP, dbg_tb, Tb[:], reads=[B_Tb], writes=[k.buf("dbg_tb")])
                  stage("tables")

                  k.op(ACT, lambda: nc.scalar.activation(out=cr[:], in_=cr[:], func=AF.Ln, bias=EPS, scale=1.0),
                       reads=[B_cr], writes=[B_cr])
                  k.op(ACT, lambda: nc.scalar.activation(out=cr[:], in_=cr[:], func=AF.Exp, scale=-0.5),
                       reads=[B_cr], writes=[B_cr])
                  for cc in range(2):
                      k.op(DVE, lambda: nc.vector.tensor_tensor(out=cv[:, cc, :], in0=cv[:, cc, :], in1=cm[:],
                                                                op=ALU.subtract), reads=[B_cv, B_cm], writes=[B_cv])
                      k.op(DVE, lambda: nc.vector.tensor_tensor(out=cv[:, cc, :], in0=cv[:, cc, :], in1=cr[:],
                                                                op=ALU.mult), reads=[B_cv, B_cr], writes=[B_cv])
                  for cc in range(2):
                      k.op(ACT, lambda: nc.scalar.activation(out=cv[:, cc, :], in_=cv[:, cc, :], func=AF.Silu,
                                                             bias=pc("cln_b", cc), scale=pc("cln_g", cc)),
                           reads=[B_cv, B_pvec], writes=[B_cv])
                  for tt in range(4):
                      tsl = slice(tt * 512, (tt + 1) * 512)
                      for cc in range(2):
                          k.op(ACT, lambda: nc.scalar.activation(out=cq[cc][:], in_=cv[:, cc, tsl], func=AF.Square),
                               reads=[B_cv], writes=[B_cq[cc]])
                          k.mm(PS[3][:, :], onesb[:, :], cq[cc][:], cc == 0, cc == 1, reads=[B_ones, B_cq[cc]],
                               wbuf=PSB[3], first=(cc == 0), last=(cc == 1), sig=True)
                      k.op(ACT, lambda: nc.scalar.activation(out=cr[:, tsl], in_=PS[3][:, :], func=AF.Ln, bias=EPS,
                                                             scale=1.0 / CW), reads=[PSB[3]], writes=[B_cr])
                  k.op(ACT, lambda: nc.scalar.activation(out=cr[:], in_=cr[:], func=AF.Exp, scale=-0.5),
                       reads=[B_cr], writes=[B_cr])
                  for cc in range(2):
                      k.op(DVE, lambda: nc.vector.scalar_tensor_tensor(out=uN[:, cc, :], in0=cv[:, cc, :],
                                                                       scalar=pc("cn_g", cc), in1=cr[:],
                                                                       op0=ALU.mult, op1=ALU.mult),
                           reads=[B_cv, B_cr, B_pvec], writes=[B_uN])
                  k.barrier()
              if DEBUG:
                  k.dma(SP, dbg_un[s], uN[:], reads=[B_uN], writes=[k.buf("dbg_un")])
              stage("conv")

              with ExitStack() as sA:
                  NST = 4
                  wq = [sb(f"wq{i}", [128, 8, 384], BF16, sA) for i in range(2)]
                  qz = [sb(f"qz{i}", [128, S], BF16, sA) for i in range(2)]
                  kv = sb("kv", [128, 2, S], BF16, sA)
                  Vtok = sb("Vtok", [128, 96, 65], BF16, sA)
                  acc = [sb(f"acc{i}", [128, S], F32, sA) for i in range(2)]
                  denb = sb("denb", [128, S], F32, sA)
                  dq = sb("dq", [128, 2, 16], F32, sA)
                  stmp = [sb(f"stmp{i}", [128, 2, 256], F32, sA) for i in range(NST)]
                  PT = [sb(f"PT{i}", [128, 2, 256], BF16, sA) for i in range(NST)]
                  mtmp = [sb(f"mtmp{i}", [128, 512], F32, sA) for i in range(4)]
                  sq = [sb(f"asq{i}", [128, 512], BF16, sA) for i in range(2)]
                  rra = sb("rra", [128, S], F32, sA)
                  B_wq = [k.buf(f"wq{i}") for i in range(2)]
                  B_qz = [k.buf(f"qz{i}") for i in range(2)]
                  B_kv = [k.buf(f"kv{i}") for i in range(2)]
                  B_V = k.buf("Vtok")
                  B_Vg = [[k.buf(f"Vtok{b_}_{g_}") for g_ in range(4)] for b_ in range(3)]
                  B_acc = [k.buf(f"acc{i}") for i in range(2)]
                  B_denb = k.buf("denb")
                  B_dnd = k.buf("dnd")
                  B_dnd2 = k.buf("dnd2")
                  B_dq = k.buf("dq")
                  B_st = [k.buf(f"stmp{i}") for i in range(NST)]
                  B_PT = [k.buf(f"PT{i}") for i in range(NST)]
                  B_mt = [k.buf(f"mtmp{i}") for i in range(4)]
                  B_sq = [k.buf(f"asq{i}") for i in range(2)]
                  B_rra = k.buf("rra")

                  k.op(DVE, lambda: nc.vector.memset(Vtok[:, :, 64:65], 1.0), writes=[B_V])
                  k.op(POOL, lambda: nc.gpsimd.memset(qz[0][64:128, :], 0.0), writes=[B_qz[0]])
                  k.op(POOL, lambda: nc.gpsimd.memset(qz[1][0:64, :], 0.0), writes=[B_qz[1]])

                  def load_wq(c):
                      i = c % 2
                      for j in range(3):
                          k.dma(POOL, wq[i][:, :, j * 128:(j + 1) * 128],
                                w_in_v[:, :, j * AW + c * 128:j * AW + (c + 1) * 128], writes=[B_wq[i]])

                  evac_flip = [0]
                  mt_i = [0]

                  def project(c):
                      wi = c % 2
                      for j in range(3):
                          for tt in range(4):
                              tsl = slice(tt * 512, (tt + 1) * 512)
                              pb = (j * 4 + tt) % 4
                              for kc in range(8):
                                  k.mm(PS[pb][:, :], wq[wi][:, kc, j * 128:(j + 1) * 128], xTb[:, kc, tsl],
                                       kc == 0, kc == 7, reads=[B_wq[wi], B_x[kc]], wbuf=PSB[pb],
                                       first=(kc == 0), last=(kc == 7))
                              if j == 0:
                                  for hh in range(2):
                                      ps_ = slice(hh * 64, (hh + 1) * 64)
                                      k.op(ACT, lambda: nc.scalar.activation(
                                          out=qz[hh][ps_, tsl], in_=PS[pb][ps_, :], func=AF.Identity,
                                          bias=pvec[ps_, PCOL["b_in"] + c:PCOL["b_in"] + c + 1], scale=1.0),
                                          reads=[PSB[pb], B_pvec], writes=[B_qz[hh]])
                              else:
                                  k.op(ACT, lambda: nc.scalar.activation(out=kv[:, j - 1, tsl], in_=PS[pb][:, :],
                                                                         func=AF.Identity,
                                                                         bias=pc("b_in", j * 6 + c), scale=1.0),
                                       reads=[PSB[pb], B_pvec], writes=[B_kv[j - 1]])

                  def vtok(c):
                      for b, (d, nres, nblk) in enumerate(BRANCH):
                          for g in range(4):
                              pb = g % 4
                              pv = PS[pb][:, 0:256].bitcast(BF16)
                              for jj in range(4):
                                  blk = 4 * g + jj
                                  if d == 1:
                                      tsl = tok_ap(None, 1, 0, blk)
                                  elif d == 4:
                                      tsl = tok_ap(None, 4, blk // 4, blk % 4)
                                  else:
                                      tsl = tok_ap(None, 16, blk, 0)
                                  k.mm(pv[:, jj * 128:(jj + 1) * 128], kv[:, 1, tsl], identb[:, :], True, True,
                                       reads=[B_kv[1], B_ident], wbuf=PSB[pb], first=(jj == 0), last=(jj == 3),
                                       transpose=True)
                              src = pv.rearrange("p (a e) -> p a e", a=8)
                              dst = Vtok[:, (b * 16 + 4 * g) * 2:(b * 16 + 4 * g + 4) * 2, 0:64]
                              if evac_flip[0] % 2 == 0:
                                  k.op(DVE, lambda: nc.vector.tensor_copy(out=dst, in_=src), reads=[PSB[pb], B_V],
                                       writes=[B_Vg[b][g]])
                              else:
                                  k.op(ACT, lambda: nc.scalar.copy(out=dst, in_=src), reads=[PSB[pb], B_V],
                                       writes=[B_Vg[b][g]])
                              evac_flip[0] += 1

                  def post(c):
                      for hh in range(2):
                          k.dma(SP, dnd[hh:hh + 1, :], acc[hh][64:65, :], reads=[B_acc[hh]], writes=[B_dnd])
                      k.dma(SP, acc[0][64:128, :], acc[1][0:64, :], reads=[B_acc[1], B_dnd], writes=[B_acc[0]])
                      src = bass.AP(tensor=dnd.tensor, offset=0, ap=[[16, 128], [S, 2], [1, 16]])
                      k.dma(SP, dq[:], src, reads=[B_dnd], writes=[B_dq])
                      k.op(DVE, lambda: nc.vector.reciprocal(out=dq[:], in_=dq[:]), reads=[B_dq], writes=[B_dq])
                      dst = bass.AP(tensor=dnd2.tensor, offset=0, ap=[[16, 128], [S, 2], [1, 16]])
                      k.dma(SP, dst, dq[:], reads=[B_dq], writes=[B_dnd2])
                      for hh in range(2):
                          src = bass.AP(tensor=dnd2.tensor, offset=hh * S, ap=[[0, 64], [1, S]])
                          k.dma(SP, denb[hh * 64:(hh + 1) * 64, :], src, reads=[B_dnd2], writes=[B_denb])

                  def finalize(c):
                      k.op(DVE, lambda: nc.vector.tensor_tensor(out=attnT[:, c, :], in0=acc[0][:], in1=denb[:],
                                                                op=ALU.mult),
                           reads=[B_acc[0], B_denb], writes=[B_attn[c]])

                  def steps_for(c):
                      pend = []
                      gs = [0]

                      def flush(upto=None, obs=None):
                          keep = []
                          for item in pend:
                              at, ob_l, fn = item
                              if (upto is None and obs is None) or (upto is not None and at <= upto) or \
                                      (obs is not None and any(o in obs for o in ob_l)):
                                  fn()
                              else:
                                  keep.append(item)
                          pend[:] = keep

                      allsteps = []
                      for b, (d, nres, nblk) in enumerate(BRANCH):
                          if d == 16:
                              st_ = [[(r0, 0, 0, 128), (r0 + 1, 0, 128, 128)] for r0 in range(0, 16, 2)]
                          else:
                              st_ = [[(r, n, 0, 256 if n + 1 < nblk else 128)] for r in range(nres)
                                     for n in range(nblk)]
                          allsteps += [(b, d, subs) for subs in st_]

                      def group_of(d, r, n):
                          if d == 1:
                              return n // 4, n % 4
                          if d == 4:
                              return r, n
                          return r // 4, r % 4

                      def qk(si):
                          cur = si % NST
                          b, d, subs = allsteps[si]
                          nmm = 2 * len(subs)
                          i_ = 0
                          for hh in range(2):
                              for (r, n, co, N) in subs:
                                  k.mm(PS[cur][:, hh * 256 + co:hh * 256 + co + N],
                                       kv[:, 0, tok_ap(None, d, r, n)],
                                       qz[hh][:, tok_ap(None, d, r, n, 2 if N == 256 else 1)], True, True,
                                       reads=[B_kv[0], B_qz[hh]], wbuf=PSB[cur], first=(i_ == 0),
                                       last=(i_ == nmm - 1))
                                  i_ += 1

                      for si in range(min(NST - 1, len(allsteps))):
                          qk(si)
                      for si, (b, d, subs) in enumerate(allsteps):
                          cur = si % NST
                          NT = max(co + N for (_, _, co, N) in subs)
                          if si + NST - 1 < len(allsteps):
                              qk(si + NST - 1)
                          sv = PS[cur][:, :].rearrange("p (h n) -> p h n", h=2)[:, :, 0:NT]
                          k.op(DVE, lambda: nc.vector.scalar_tensor_tensor(
                              out=stmp[cur][:, :, 0:NT], in0=sv, scalar=0.125,
                              in1=Tb[:, b, 2 * c:2 * c + 2, 0:NT], op0=ALU.mult, op1=ALU.add),
                              reads=[PSB[cur], B_Tb], writes=[B_st[cur]])
                          k.op(ACT, lambda: nc.scalar.activation(out=PT[cur][:, :, 0:NT],
                                                                 in_=stmp[cur][:, :, 0:NT], func=AF.Exp),
                               reads=[B_st[cur]], writes=[B_PT[cur]])
                          batch = []
                          closing = []
                          for hh in range(2):
                              for (r, n, co, N) in subs:
                                  blk = n if d == 1 else (r * 4 + n if d == 4 else r)
                                  g, slot = group_of(d, r, n)
                                  ob = 4 + 2 * hh + (g % 2)
                                  vaug = Vtok[:, (b * 16 + blk) * 2 + hh, :]
                                  batch.append((PS[ob][0:65, slot * 128:(slot + 1) * 128], vaug,
                                                PT[cur][:, hh, co:co + 128], n == 0, True, ob,
                                                (slot == 0 and n == 0)))
                                  if slot == 3:
                                      closing.append((hh, g, ob))
                                  if N == 256:
                                      g2, slot2 = group_of(d, r, n + 1)
                                      ob2 = 4 + 2 * hh + (g2 % 2)
                                      batch.append((PS[ob2][0:65, slot2 * 128:(slot2 + 1) * 128], vaug,
                                                    PT[cur][:, hh, co + 128:co + 256], True, False, ob2,
                                                    (slot2 == 0)))
                          vb_ = []
                          for (r, n, co, N) in subs:
                              blk = n if d == 1 else (r * 4 + n if d == 4 else r)
                              if B_Vg[b][blk // 4] not in vb_:
                                  vb_.append(B_Vg[b][blk // 4])
                          opening = [ob_ for (_, _, _, _, _, ob_, fi_) in batch if fi_]
                          if opening:
                              flush(obs=opening)
                          for bi_, (o_, l_, r_, st_, sp_, ob_, fi_) in enumerate(batch):
                              k.mm(o_, l_, r_, st_, sp_, reads=vb_ + [B_PT[cur]], wbuf=PSB[ob_], first=fi_,
                                   last=False, sig=(bi_ == len(batch) - 1))
                          if closing and not k.stopped:
                              tokc = PE.last()
                              for (hh, g, ob) in closing:
                                  PSB[ob].w = tokc
                                  PSB[ob].r = {}
                                  PSB[ob].consumed = False
                                  if d == 1:
                                      a_ap = acc[hh][0:65, 512 * g:512 * (g + 1)]
                                      p_ap = PS[ob][0:65, :]
                                  elif d == 4:
                                      a_ap = acc[hh][0:65, g:S:4]
                                      p_ap = PS[ob][0:65, :]
                                  else:
                                      a_ap = acc[hh][0:65, :].rearrange("p (i r) -> p r i", r=16)[:, 4 * g:4 * g + 4, :]
                                      p_ap = PS[ob][0:65, :].rearrange("p (s i) -> p s i", s=4)

                                  def merge(a_ap=a_ap, p_ap=p_ap, ob=ob, hh=hh, b=b, d=d):
                                      if b == 0:
                                          k.op(ACT, lambda: nc.scalar.copy(out=a_ap, in_=p_ap), reads=[PSB[ob]],
                                               writes=[B_acc[hh]])
                                      else:
                                          mi = mt_i[0] % 4
                                          mt_i[0] += 1
                                          m_ap = mtmp[mi][0:65, :]
                                          if d == 16:
                                              m_ap = m_ap.rearrange("p (s i) -> p s i", s=4)
                                          k.op(ACT, lambda: nc.scalar.copy(out=m_ap, in_=p_ap), reads=[PSB[ob]],
                                               writes=[B_mt[mi]])
                                          k.op(POOL, lambda: nc.gpsimd.tensor_tensor(out=a_ap, in0=m_ap,
                                                                                    in1=a_ap, op=ALU.add),
                                               reads=[B_mt[mi], B_acc[hh]], writes=[B_acc[hh]])
                                  pend.append((gs[0] + 2, [ob], merge))
                          flush(upto=gs[0])
                          gs[0] += 1
                      flush()

                  load_wq(0)
                  for c in range(6):
                      if c + 1 < 6:
                          load_wq(c + 1)
                      project(c)
                      vtok(c)
                      if c > 0:
                          finalize(c - 1)
                      steps_for(c)
                      post(c)
                  finalize(5)
                  stage("attfin")
                  for tt in range(4):
                      tsl = slice(tt * 512, (tt + 1) * 512)
                      for c in range(6):
                          i = c % 2
                          k.op(ACT, lambda: nc.scalar.activation(out=sq[i][:], in_=attnT[:, c, tsl], func=AF.Square),
                               reads=[B_attn[c]], writes=[B_sq[i]])
                          k.mm(PS[0][:, :], onesb[:, :], sq[i][:], c == 0, c == 5, reads=[B_ones, B_sq[i]],
                               wbuf=PSB[0], first=(c == 0), last=(c == 5), sig=True)
                      k.op(ACT, lambda: nc.scalar.activation(out=rra[:, tsl], in_=PS[0][:, :], func=AF.Ln, bias=EPS,
                                                             scale=1.0 / AW), reads=[PSB[0]], writes=[B_rra])
                  k.op(ACT, lambda: nc.scalar.activation(out=rra[:], in_=rra[:], func=AF.Exp, scale=-0.5),
                       reads=[B_rra], writes=[B_rra])
                  for c in range(6):
                      k.op(DVE, lambda: nc.vector.scalar_tensor_tensor(out=attnT[:, c, :], in0=attnT[:, c, :],
                                                                       scalar=pc("an_g", c), in1=rra[:],
                                                                       op0=ALU.mult, op1=ALU.mult),
                           reads=[B_attn[c], B_rra, B_pvec], writes=[B_attn[c]])
                  if DEBUG:
                      k.dma(SP, dbg_attn[s], attnT[:], reads=B_attn, writes=[k.buf("dbg_attn")])
                  stage("attn")
                  k.barrier()
              k.barrier()

          with ExitStack() as sF:
              R = sb("R", [128, 8, 1024], F32, sF)
              x1b = sb("x1b", [128, 8, 1024], BF16, sF)
              zb = [sb(f"zb{i}", [128, 512], BF16, sF) for i in range(3)]
              zsq = [sb(f"zsq{i}", [128, 512], BF16, sF) for i in range(3)]
              lm = sb("lm", [128, 512], F32, sF)
              lq = sb("lq", [128, 512], F32, sF)
              wo = [sb(f"wo{i}", [128, 8, 128], BF16, sF) for i in range(3)]
              actT = sb("actT", [128, 22, 1024], BF16, sF)
              Hrow = [[sb(f"Hrow{g}_{i}", [128, 1026], F32, sF) for i in range(2)] for g in range(2)]
              Tt = [[sb(f"Tt{g}_{i}", [128, 1024], F32, sF) for i in range(2)] for g in range(2)]
              sgt = [sb(f"sgt{i}", [128, 1024], F32, sF) for i in range(2)]
              wu = [sb(f"wu{i}", [128, 8, 2, 128], BF16, sF) for i in range(3)]
              wd = [sb(f"wd{i}", [128, 22, 128], BF16, sF) for i in range(2)]
              B_Rt = [[k.buf(f"R{t}_{o}") for o in range(8)] for t in range(2)]
              B_x1t = [[k.buf(f"x1b{t}_{o}") for o in range(8)] for t in range(2)]
              B_wo = [k.buf(f"wo{i}") for i in range(3)]
              B_act = [k.buf(f"actT{i}") for i in range(22)]
              B_H = [[k.buf(f"Hrow{g}_{i}") for i in range(2)] for g in range(2)]
              B_Hh = [[k.buf(f"Hrowh{g}_{i}") for i in range(2)] for g in range(2)]
              B_hc = [k.buf(f"halo{i}") for i in range(44)]
              B_T = [[k.buf(f"Tt{g}_{i}") for i in range(2)] for g in range(2)]
              B_sg = [k.buf(f"sgt{i}") for i in range(2)]
              B_wu = [k.buf(f"wu{i}") for i in range(3)]
              B_wd = [k.buf(f"wd{i}") for i in range(2)]
              B_out = k.buf("outT")
              tmp = dict(zb=zb, zsq=zsq, B_zb=[k.buf(f"zb{i}") for i in range(3)],
                         B_zsq=[k.buf(f"zsq{i}") for i in range(3)],
                         m=lm, q=lq, B_m=k.buf("lm"), B_q=k.buf("lq"), rot=[0])
              MB = [0, 1, 4, 5, 6, 7]
              k.op(POOL, lambda: nc.gpsimd.memset(halo[:], 0.0), writes=[B_halo] + B_hc)
              wo_n = [0]
              wd_n = [0]

              def load_wo(oc):
                  i = wo_n[0] % 3
                  wo_n[0] += 1
                  k.dma(POOL, wo[i][:], w_out_v[:, :, oc * 128:(oc + 1) * 128], writes=[B_wo[i]])
                  return i

              def load_wu(fp):
                  i = fp % 3
                  k.dma(POOL, wu[i][:, :, 0, :], w_up_v[:, :, fp * 128:(fp + 1) * 128], writes=[B_wu[i]])
                  k.dma(POOL, wu[i][:, :, 1, :], w_up_v[:, :, DFF + fp * 128:DFF + (fp + 1) * 128],
                        writes=[B_wu[i]])

              def load_wd(oc):
                  i = wd_n[0] % 2
                  wd_n[0] += 1
                  k.dma(POOL, wd[i][:, 0:11, :], w_down_v[:, 0:11, oc * 128:(oc + 1) * 128], writes=[B_wd[i]])
                  k.dma(POOL, wd[i][:, 11:22, :], w_down_v[:, 11:22, oc * 128:(oc + 1) * 128], writes=[B_wd[i]])
                  return i

              mmi = 0
              for hf in range(2):
                  t0 = hf * 1024
                  for tt in range(2):
                      tl = slice(tt * 512, (tt + 1) * 512)
                      for kc in range(8):
                          k.dma(SP, R[:, kc, tl], xT[s, kc * 128:(kc + 1) * 128, t0 + tt * 512:t0 + (tt + 1) * 512],
                                writes=[B_Rt[tt][kc]])
                  for tt in range(2):
                      tl = slice(tt * 512, (tt + 1) * 512)
                      tg = slice(t0 + tt * 512, t0 + (tt + 1) * 512)
                      wq_ = [load_wo(0), load_wo(1)]
                      for oc in range(8):
                          if oc + 2 < 8:
                              wq_.append(load_wo(oc + 2))
                          wi = wq_[oc]
                          pb = MB[mmi % 6]
                          mmi += 1
                          for kc in range(8):
                              rhs = attnT[:, kc, tg] if kc < 6 else uN[:, kc - 6, tg]
                              rb = B_attn[kc] if kc < 6 else B_uN
                              k.mm(PS[pb][:, :], wo[wi][:, kc, :], rhs, kc == 0, kc == 7,
                                   reads=[B_wo[wi], rb], wbuf=PSB[pb], first=(kc == 0), last=(kc == 7))
                          k.op(DVE, lambda: nc.vector.scalar_tensor_tensor(out=R[:, oc, tl], in0=R[:, oc, tl],
                                                                           scalar=ALPHA, in1=PS[pb][:, :],
                                                                           op0=ALU.mult, op1=ALU.add),
                               reads=[B_Rt[tt][oc], PSB[pb]], writes=[B_Rt[tt][oc]])
                      layer_norm_tile(R, B_Rt[tt], tl, "ln1_g", "ln1_b", (2, 3), tmp, outb=x1b, B_outb=B_x1t[tt])
                  if DEBUG:
                      k.dma(SP, dbg_x1[s][:, :, t0:t0 + 1024], x1b[:], reads=B_x1t[0] + B_x1t[1],
                            writes=[k.buf("dbg_x1")])
                  if hf == 0:
                      stage("C0")

                  def silu_mul(fq):
                      bq = fq % 2
                      k.op(ACT, lambda: nc.scalar.activation(out=sgt[bq][:], in_=Tt[0][bq][:], func=AF.Silu),
                           reads=[B_T[0][bq]], writes=[B_sg[bq]])
                      k.op(DVE, lambda: nc.vector.tensor_tensor(out=actT[:, fq, :], in0=sgt[bq][:],
                                                                in1=Tt[1][bq][:], op=ALU.mult),
                           reads=[B_sg[bq], B_T[1][bq]], writes=[B_act[fq]])

                  load_wu(0)
                  load_wu(1)
                  for fp in range(22):
                      if fp + 2 < 22:
                          load_wu(fp + 2)
                      wi = fp % 3
                      bi = fp % 2
                      for gv in range(2):
                          ch = gv * 22 + fp
                          H = Hrow[gv][bi]
                          BH = B_H[gv][bi]
                          BHh = B_Hh[gv][bi]
                          k.op(POOL, lambda: nc.gpsimd.tensor_copy(out=H[:, 0:2], in_=halo[:, ch, :]),
                               reads=[B_hc[ch]], writes=[BHh])
                          for tt in range(2):
                              tl = slice(tt * 512, (tt + 1) * 512)
                              pb = MB[mmi % 6]
                              mmi += 1
                              for kc in range(8):
                                  k.mm(PS[pb][:, :], wu[wi][:, kc, gv, :], x1b[:, kc, tl], kc == 0, kc == 7,
                                       reads=[B_wu[wi], B_x1t[tt][kc]], wbuf=PSB[pb], first=(kc == 0), last=(kc == 7))
                              k.op(ACT, lambda: nc.scalar.copy(out=H[:, 2 + tt * 512:2 + (tt + 1) * 512],
                                                               in_=PS[pb][:, :]), reads=[PSB[pb]], writes=[BH])
                      for gv in range(2):
                          ch = gv * 22 + fp
                          H = Hrow[gv][bi]
                          BH = B_H[gv][bi]
                          T = Tt[gv][bi]
                          BT = B_T[gv][bi]
                          k.op(ACT, lambda: nc.scalar.copy(out=halo[:, ch, :], in_=H[:, 1024:1026]),
                               reads=[BH], writes=[B_hc[ch]])
                          k.op(ACT, lambda: nc.scalar.activation(out=T[:], in_=H[:, 2:1026], func=AF.Identity,
                                                                 bias=pc("fb", ch), scale=pc("fw", 2 * 44 + ch)),
                               reads=[BH, B_pvec], writes=[BT])
                      for tap, wcol in ((1, 44), (0, 0)):
                          for gv in range(2):
                              ch = gv * 22 + fp
                              H = Hrow[gv][bi]
                              T = Tt[gv][bi]
                              k.op(DVE, lambda: nc.vector.scalar_tensor_tensor(out=T[:], in0=H[:, tap:tap + 1024],
                                                                               scalar=pc("fw", wcol + ch), in1=T[:],
                                                                               op0=ALU.mult, op1=ALU.add),
                                   reads=[B_H[gv][bi], B_Hh[gv][bi], B_T[gv][bi], B_pvec], writes=[B_T[gv][bi]])
                      if fp >= 1:
                          silu_mul(fp - 1)
                      if fp == 19:
                          wdq = [load_wd(0)]
                      if fp == 20:
                          wdq.append(load_wd(1))
                  silu_mul(21)
                  if DEBUG and hf == 0 and s == 0:
                      k.dma(SP, dbg_act, actT[:], reads=B_act, writes=[k.buf("dbg_act")])
                  if hf == 0:
                      stage("D0")
                  for oc in range(8):
                      wi = wdq[oc]
                      for tt in range(2):
                          tl = slice(tt * 512, (tt + 1) * 512)
                          pb = MB[mmi % 6]
                          mmi += 1
                          for kc in range(22):
                              k.mm(PS[pb][:, :], wd[wi][:, kc, :], actT[:, kc, tl], kc == 0, kc == 21,
                                   reads=[B_wd[wi], B_act[kc]], wbuf=PSB[pb], first=(kc == 0), last=(kc == 21))
                          k.op(DVE, lambda: nc.vector.scalar_tensor_tensor(out=R[:, oc, tl], in0=R[:, oc, tl],
                                                                           scalar=ALPHA, in1=PS[pb][:, :],
                                                                           op0=ALU.mult, op1=ALU.add),
                               reads=[B_Rt[tt][oc], PSB[pb]], writes=[B_Rt[tt][oc]])
                      if oc + 2 < 8:
                          wdq.append(load_wd(oc + 2))
                  for tt in range(2):
                      tl = slice(tt * 512, (tt + 1) * 512)
                      layer_norm_tile(R, B_Rt[tt], tl, "ln2_g", "ln2_b", (2, 3), tmp)
                      for kc in range(8):
                          k.dma(SP, outT[s, kc * 128:(kc + 1) * 128, t0 + tt * 512:t0 + (tt + 1) * 512],
                                R[:, kc, tl], reads=[B_Rt[tt][kc]], writes=[], sbuf=B_out)
                  if hf == 0:
                      stage("E0")
              k.barrier()
    except _Stop:
        pass
    k.barrier()
    es.close()
    print("semaphores used:", k.nsem, "PE sigs", PE.n, "ACT", ACT.n, "DVE", DVE.n, "POOL", POOL.n)
    return nc


_NC_CACHE = {}


def kernel(**inp):
    x = np.asarray(inp["x"], np.float32)
    pv = pack_params(inp)
    relx = np.concatenate([np.asarray(inp["rel_table"], np.float32), np.ones((1, 12), np.float32)], axis=0)
    oh = onehot_const()
    ident = np.eye(128, dtype=np.float32)
    jmat = np.ascontiguousarray(np.eye(128, dtype=np.float32)[::-1])
    w_in = np.ascontiguousarray(np.asarray(inp["w_in"], np.float32)[0])
    w_out = np.ascontiguousarray(np.asarray(inp["w_out"], np.float32)[0])
    w_up = np.ascontiguousarray(np.asarray(inp["w_up"], np.float32)[0])
    w_down = np.ascontiguousarray(np.asarray(inp["w_down"], np.float32)[0])
    if "nc" not in _NC_CACHE:
        _NC_CACHE["nc"] = build()
    nc = _NC_CACHE["nc"]
    in_maps = []
    for cid in range(NCORE):
        xs = x[cid * NSEQ:(cid + 1) * NSEQ]
        xTs = np.ascontiguousarray(xs.transpose(0, 2, 1))
        in_maps.append({"xT": xTs, "w_in": w_in, "w_out": w_out, "w_up": w_up, "w_down": w_down,
                        "pvec": pv, "relx": relx, "oh": oh, "ident": ident, "jmat": jmat})
    res = run_bass_kernel_spmd(nc, in_maps, core_ids=list(range(NCORE)))
    outs = []
    for cid in range(NCORE):
        o = np.asarray(res.results[cid]["outT"], np.float32)
        outs.append(o.transpose(0, 2, 1))
    out = np.ascontiguousarray(np.concatenate(outs, axis=0))
    if DEBUG:
        kernel.last_results = res.results
    return out
```

```python
import math
from contextlib import ExitStack

import numpy as np
import concourse.bass as bass
import concourse.mybir as mybir
from concourse.bass_utils import run_bass_kernel_spmd

F32 = mybir.dt.float32
BF16 = mybir.dt.bfloat16
AF = mybir.ActivationFunctionType
ALU = mybir.AluOpType

D = 1024
S = 2048
NSEQ = 2
NCORE = 8
HD = 64
NH = 12
AW = 768
CW = 256
INW = 2816
DFF = 2816
CK = 31
ALPHA = 2.0 ** 0.25
EPS = 1e-5
MASKV = -30000.0
BRANCH = ((1, 1, 16), (4, 4, 4), (16, 16, 1))

DEBUG = False

PCOL = {}
_o = 0
for _n, _w in (("b_in", 22), ("convw", 62), ("conv_b", 2), ("cln_g", 2), ("cln_b", 2), ("cn_g", 2),
               ("an_g", 6), ("ln1_g", 8), ("ln1_b", 8), ("fw", 132), ("fb", 44), ("ln2_g", 8), ("ln2_b", 8)):
    PCOL[_n] = _o
    _o += _w
NPCOL = _o


def _cols(v):
    v = np.asarray(v, np.float32).reshape(-1, 128)
    return v.T


def pack_params(inp):
    pv = np.zeros((128, NPCOL), np.float32)

    def put(name, arr):
        pv[:, PCOL[name]:PCOL[name] + arr.shape[1]] = arr
    put("b_in", _cols(inp["b_in"][0]))
    cw = np.asarray(inp["conv_w"][0], np.float32)
    put("convw", np.concatenate([cw[:, 0:128].T, cw[:, 128:256].T], axis=1))
    put("conv_b", _cols(inp["conv_b"][0]))
    put("cln_g", _cols(inp["conv_ln_g"][0]))
    put("cln_b", _cols(inp["conv_ln_b"][0]))
    put("cn_g", _cols(inp["conv_norm_g"][0]))
    put("an_g", _cols(inp["attn_norm_g"][0]))
    put("ln1_g", _cols(inp["ln1_g"][0]))
    put("ln1_b", _cols(inp["ln1_b"][0]))
    fw = np.asarray(inp["ffn_conv_w"][0], np.float32)
    put("fw", np.concatenate([_cols(fw[j]) for j in range(3)], axis=1))
    put("fb", _cols(inp["ffn_conv_b"][0]))
    put("ln2_g", _cols(inp["ln2_g"][0]))
    put("ln2_b", _cols(inp["ln2_b"][0]))
    return pv


def t5_bucket_np(dist):
    dist = np.asarray(dist, np.int64)
    exact = 16
    d_f = np.maximum(dist, 1).astype(np.float32)
    large = exact + (np.log(d_f / np.float32(exact)) / np.float32(math.log(2048 / exact))
                     * np.float32(32 - exact)).astype(np.int32)
    large = np.minimum(large, 31)
    return np.where(dist < exact, dist, large)


def onehot_const():
    oh = np.zeros((33, 3 * 384), np.float32)
    for b, (d, _, _) in enumerate(BRANCH):
        for i in range(383):
            delta = i - 127
            if 0 <= delta <= 128:
                oh[int(t5_bucket_np(delta * d)), b * 384 + i] = 1.0
            else:
                oh[32, b * 384 + i] = MASKV
    return oh


class Tok:
    __slots__ = ("sem", "sid", "val")

    def __init__(self, sem, sid, val):
        self.sem, self.sid, self.val = sem, sid, val


class Eng:
    def __init__(self, nc, e, name):
        self.e = e
        self.name = name
        self.sem = nc.alloc_semaphore(name="es_" + name)
        self.sid = "E" + name
        self.n = 0
        self.seen = {}

    def wait(self, tok):
        if tok is None:
            return
        if self.seen.get(tok.sid, 0) >= tok.val:
            return
        self.e.wait_ge(tok.sem, tok.val)
        self.seen[tok.sid] = tok.val

    def sig(self, ins):
        self.n += 1
        ins.then_inc(self.sem, 1)
        return Tok(self.sem, self.sid, self.n)

    def last(self):
        return Tok(self.sem, self.sid, self.n) if self.n else None


class Buf:
    def __init__(self, name):
        self.name = name
        self.w = None
        self.r = {}
        self.dsem = None
        self.dn = 0


class K:
    def __init__(self):
        self.nc = nc = bass.Bass("TRN2", target_bir_lowering=False)
        self.PE = Eng(nc, nc.tensor, "pe")
        self.ACT = Eng(nc, nc.scalar, "act")
        self.DVE = Eng(nc, nc.vector, "dve")
        self.POOL = Eng(nc, nc.gpsimd, "pool")
        self.SP = Eng(nc, nc.sync, "sp")
        self.engs = [self.PE, self.ACT, self.DVE, self.POOL, self.SP]
        self.pe_pending = []
        self.bufs = {}
        self.dma_bufs = []
        self.nsem = 5
        self.stopped = False

    def buf(self, name):
        if name not in self.bufs:
            self.bufs[name] = Buf(name)
        return self.bufs[name]

    def deps(self, E, reads, writes):
        for b in reads:
            E.wait(b.w)
        for b in writes:
            assert b not in self.pe_pending, f"write to {b.name} with pending PE reads"
            if b.w is not None and b.w.sid != E.sid:
                E.wait(b.w)
            for t in b.r.values():
                if t.sid != E.sid:
                    E.wait(t)

    def done(self, tok, reads, writes):
        for b in reads:
            b.r[tok.sid] = tok
        for b in writes:
            b.w = tok
            b.r = {}

    def op(self, E, fn, reads=(), writes=()):
        if self.stopped:
            return None
        self.deps(E, reads, writes)
        ins = fn()
        tok = E.sig(ins)
        self.done(tok, reads, writes)
        return tok

    def dma(self, Q, out, in_, reads=(), writes=(), sbuf=None):
        if self.stopped:
            return None
        self.deps(Q, reads, writes)
        ins = Q.e.dma_start(out=out, in_=in_)
        b = sbuf if sbuf is not None else writes[0]
        if b.dsem is None:
            b.dsem = self.nc.alloc_semaphore(name="ds_" + b.name)
            self.nsem += 1
            self.dma_bufs.append(b)
        b.dn += 16
        ins.then_inc(b.dsem, 16)
        tok = Tok(b.dsem, "D" + b.name, b.dn)
        self.done(tok, reads, writes)
        return tok

    def mm(self, out, lhsT, rhs, start, stop, reads=(), wbuf=None, first=False, last=False, sig=False,
           transpose=False):
        PE = self.PE
        if self.stopped:
            return None
        for b in reads:
            PE.wait(b.w)
        if first and wbuf is not None:
            if wbuf.w is not None and wbuf.w.sid != PE.sid:
                PE.wait(wbuf.w)
            for t in wbuf.r.values():
                if t.sid != PE.sid:
                    PE.wait(t)
        if transpose:
            ins = self.nc.tensor.transpose(out, lhsT, rhs)
        else:
            ins = self.nc.tensor.matmul(out, lhsT, rhs, start=start, stop=stop)
        for b in reads:
            if b not in self.pe_pending:
                self.pe_pending.append(b)
        if last or sig:
            tok = PE.sig(ins)
            for b in self.pe_pending:
                b.r[tok.sid] = tok
            self.pe_pending = []
            if last and wbuf is not None:
                wbuf.w = tok
                wbuf.r = {}
            return tok
        return None

    def barrier(self):
        if self.stopped:
            return
        assert not self.pe_pending
        toks = [e.last() for e in self.engs]
        for b in self.dma_bufs:
            if b.dn:
                toks.append(Tok(b.dsem, "D" + b.name, b.dn))
        for e in self.engs:
            for t in toks:
                if t is not None and t.sid != e.sid:
                    e.wait(t)


def tok_ap(base, d, r, n, nblk=1):
    if d == 1:
        return slice(128 * n, 128 * (n + nblk))
    if d == 4:
        return slice(512 * n + r, 512 * (n + nblk), 4)
    return slice(r, S, 16)


class _Stop(Exception):
    pass


def build(stop=None, nseq=NSEQ):
    k = K()

    def stage(name):
        if stop == name and not k.stopped:
            k.barrier()
            k.stopped = True
    nc = k.nc
    PE, ACT, DVE, POOL, SP = k.PE, k.ACT, k.DVE, k.POOL, k.SP
    es = ExitStack()

    def dram(name, shape, dt, kind):
        return nc.dram_tensor(name, list(shape), dt, kind=kind).ap()

    xT = dram("xT", [NSEQ, D, S], F32, "ExternalInput")
    w_in = dram("w_in", [D, INW], F32, "ExternalInput")
    w_out = dram("w_out", [D, D], F32, "ExternalInput")
    w_up = dram("w_up", [D, 2 * DFF], F32, "ExternalInput")
    w_down = dram("w_down", [DFF, D], F32, "ExternalInput")
    pvec_d = dram("pvec", [128, NPCOL], F32, "ExternalInput")
    relx_d = dram("relx", [33, 12], F32, "ExternalInput")
    oh_d = dram("oh", [33, 3 * 384], F32, "ExternalInput")
    ident_d = dram("ident", [128, 128], F32, "ExternalInput")
    jmat_d = dram("jmat", [128, 128], F32, "ExternalInput")
    outT = dram("outT", [NSEQ, D, S], F32, "ExternalOutput")
    cdram = dram("cdram", [12, 3, 384], F32, "Internal")
    dnd = dram("dnd", [2, S], F32, "Internal")
    dnd2 = dram("dnd2", [2, S], F32, "Internal")
    if DEBUG:
        dbg_attn = dram("dbg_attn", [NSEQ, 128, 6, S], BF16, "ExternalOutput")
        dbg_un = dram("dbg_un", [NSEQ, 128, 2, S], BF16, "ExternalOutput")
        dbg_x1 = dram("dbg_x1", [NSEQ, 128, 8, S], BF16, "ExternalOutput")
        dbg_tb = dram("dbg_tb", [128, 3, 12, 256], F32, "ExternalOutput")
        dbg_act = dram("dbg_act", [128, 22, 1024], BF16, "ExternalOutput")

    w_in_v = w_in.rearrange("(kc p) n -> p kc n", p=128)
    w_out_v = w_out.rearrange("(kc p) n -> p kc n", p=128)
    w_up_v = w_up.rearrange("(kc p) n -> p kc n", p=128)
    w_down_v = w_down.rearrange("(kc p) n -> p kc n", p=128)

    _cnt = [0]

    def sb(name, shape, dt, stack=es):
        _cnt[0] += 1
        return stack.enter_context(nc.sbuf_tensor(f"{name}_{_cnt[0]}", list(shape), dt))

    PSA = es.enter_context(nc.psum_tensor("psA", [128, 1024], F32))
    PSBt = es.enter_context(nc.psum_tensor("psB", [128, 1024], F32))
    PS = [PSA[:, 0:512], PSA[:, 512:1024], PSBt[:, 0:512], PSBt[:, 512:1024]]
    PS += [es.enter_context(nc.psum_tensor(f"ps{i}", [128, 512], F32))[:, :] for i in range(4, 8)]
    PS2 = [PSA, PSBt]
    PSB = [k.buf(f"ps{i}") for i in range(8)]

    pvec = sb("pvec_sb", [128, NPCOL], F32)
    pder = sb("pder", [128, 64], F32)
    identb = sb("identb", [128, 128], BF16)
    onesb = sb("onesb", [128, 128], BF16)
    jmat = sb("jmat_sb", [128, 128], F32)
    attnT = sb("attnT", [128, 6, S], BF16)
    uN = sb("uN", [128, 2, S], BF16)
    halo = sb("halo", [128, 44, 2], F32)
    B_pvec, B_pder, B_ident, B_ones, B_j = (k.buf(n) for n in ("pvec", "pder", "ident", "ones", "jmat"))
    B_attn = [k.buf(f"attnT{c}") for c in range(6)]
    B_uN = k.buf("uN")
    B_halo = k.buf("halo")

    def pc(name, j=0, n=1):
        o = PCOL[name] + j
        return pvec[:, o:o + n]

    k.dma(SP, pvec[:], pvec_d, writes=[B_pvec])
    k.dma(POOL, identb[:], ident_d, writes=[B_ident])
    k.dma(SP, jmat[:], jmat_d, writes=[B_j])
    k.op(DVE, lambda: nc.vector.memset(onesb[:], 1.0), writes=[B_ones])
    k.op(DVE, lambda: nc.vector.tensor_scalar(out=pder[:, 0:62], in0=pc("convw", 0, 62), scalar1=0.5, scalar2=None,
                                              op0=ALU.mult), reads=[B_pvec], writes=[B_pder])
    k.op(DVE, lambda: nc.vector.tensor_scalar(out=pder[:, 62:64], in0=pc("b_in", 20, 2), scalar1=0.5, scalar2=None,
                                              op0=ALU.mult), reads=[B_pvec], writes=[B_pder])

    with ExitStack() as s0:
        relx = sb("relx_sb", [33, 12], F32, s0)
        ohs = sb("oh_sb", [33, 3 * 384], F32, s0)
        csb = sb("csb", [12, 3, 384], F32, s0)
        B_relx, B_oh, B_csb, B_cd = k.buf("relx"), k.buf("oh"), k.buf("csb"), k.buf("cdram")
        k.dma(SP, relx[:], relx_d, writes=[B_relx])
        k.dma(SP, ohs[:], oh_d, writes=[B_oh])
        for b in range(3):
            k.mm(PS[b][0:12, 0:384], relx[:, :], ohs[:, b * 384:(b + 1) * 384], True, True,
                 reads=[B_relx, B_oh], wbuf=PSB[b], first=True, last=True)
            k.op(DVE, lambda b=b: nc.vector.tensor_copy(out=csb[:, b, :], in_=PS[b][0:12, 0:384]),
                 reads=[PSB[b]], writes=[B_csb])
        k.dma(SP, cdram, csb[:], reads=[B_csb], writes=[B_cd])
        k.barrier()

    def ln_stats_chunk(R, B_Rc, tsl, oc, stat_ps, tmp):
        zb, zsq, B_zb, B_zsq = tmp["zb"], tmp["zsq"], tmp["B_zb"], tmp["B_zsq"]
        p1, p2 = stat_ps
        i = tmp["rot"][0] % len(zb)
        tmp["rot"][0] += 1
        k.op(DVE, lambda: nc.vector.tensor_copy(out=zb[i][:], in_=R[:, oc, tsl]), reads=[B_Rc[oc]],
             writes=[B_zb[i]])
        k.op(ACT, lambda: nc.scalar.activation(out=zsq[i][:], in_=R[:, oc, tsl], func=AF.Square),
             reads=[B_Rc[oc]], writes=[B_zsq[i]])

        def mms():
            k.mm(PS[p1][:, :], onesb[:, :], zb[i][:], oc == 0, oc == 7, reads=[B_ones, B_zb[i]],
                 wbuf=PSB[p1], first=(oc == 0), last=(oc == 7), sig=True)
            k.mm(PS[p2][:, :], onesb[:, :], zsq[i][:], oc == 0, oc == 7, reads=[B_ones, B_zsq[i]],
                 wbuf=PSB[p2], first=(oc == 0), last=(oc == 7), sig=True)
        return mms

    def layer_norm_tile(R, B_Rc, tsl, gname, bname, stat_ps, tmp, outb=None, B_outb=None, skip_stats=False):
        zb, zsq, B_zb, B_zsq = tmp["zb"], tmp["zsq"], tmp["B_zb"], tmp["B_zsq"]
        m, q, B_m, B_q = tmp["m"], tmp["q"], tmp["B_m"], tmp["B_q"]
        p1, p2 = stat_ps
        nb = len(zb)
        for oc in range(0 if skip_stats else 8):
            i = tmp["rot"][0] % nb
            tmp["rot"][0] += 1
            k.op(DVE, lambda: nc.vector.tensor_copy(out=zb[i][:], in_=R[:, oc, tsl]), reads=[B_Rc[oc]],
                 writes=[B_zb[i]])
            k.op(ACT, lambda: nc.scalar.activation(out=zsq[i][:], in_=R[:, oc, tsl], func=AF.Square),
                 reads=[B_Rc[oc]], writes=[B_zsq[i]])
            k.mm(PS[p1][:, :], onesb[:, :], zb[i][:], oc == 0, oc == 7, reads=[B_ones, B_zb[i]],
                 wbuf=PSB[p1], first=(oc == 0), last=(oc == 7), sig=True)
            k.mm(PS[p2][:, :], onesb[:, :], zsq[i][:], oc == 0, oc == 7, reads=[B_ones, B_zsq[i]],
                 wbuf=PSB[p2], first=(oc == 0), last=(oc == 7), sig=True)
        k.op(DVE, lambda: nc.vector.tensor_scalar(out=m[:], in0=PS[p1][:, :], scalar1=1.0 / D, scalar2=None,
                                                  op0=ALU.mult), reads=[PSB[p1]], writes=[B_m])
        k.op(DVE, lambda: nc.vector.tensor_tensor(out=q[:], in0=m[:], in1=m[:], op=ALU.mult),
             reads=[B_m], writes=[B_q])
        k.op(DVE, lambda: nc.vector.scalar_tensor_tensor(out=q[:], in0=PS[p2][:, :], scalar=1.0 / D, in1=q[:],
                                                         op0=ALU.mult, op1=ALU.subtract),
             reads=[PSB[p2], B_q], writes=[B_q])
        k.op(ACT, lambda: nc.scalar.activation(out=q[:], in_=q[:], func=AF.Ln, bias=EPS, scale=1.0),
             reads=[B_q], writes=[B_q])
        k.op(ACT, lambda: nc.scalar.activation(out=q[:], in_=q[:], func=AF.Exp, scale=-0.5),
             reads=[B_q], writes=[B_q])
        k.op(DVE, lambda: nc.vector.tensor_tensor(out=m[:], in0=m[:], in1=q[:], op=ALU.mult),
             reads=[B_m, B_q], writes=[B_m])
        def ln_mult(oc):
            k.op(DVE, lambda: nc.vector.tensor_tensor(out=R[:, oc, tsl], in0=R[:, oc, tsl], in1=q[:], op=ALU.mult),
                 reads=[B_Rc[oc], B_q], writes=[B_Rc[oc]])

        def ln_rest(oc):
            k.op(DVE, lambda: nc.vector.tensor_tensor(out=R[:, oc, tsl], in0=R[:, oc, tsl], in1=m[:],
                                                      op=ALU.subtract), reads=[B_Rc[oc], B_m], writes=[B_Rc[oc]])
            k.op(ACT, lambda: nc.scalar.activation(out=R[:, oc, tsl], in_=R[:, oc, tsl], func=AF.Identity,
                                                   bias=pc(bname, oc), scale=pc(gname, oc)),
                 reads=[B_Rc[oc], B_pvec], writes=[B_Rc[oc]])
            if outb is not None:
                k.op(ACT, lambda: nc.scalar.copy(out=outb[:, oc, tsl], in_=R[:, oc, tsl]),
                     reads=[B_Rc[oc]], writes=[B_outb[oc]])

        ln_mult(0)
        for oc in range(1, 8):
            ln_mult(oc)
            ln_rest(oc - 1)
        ln_rest(7)

    stage("setup")
    try:
      for s in range(nseq):
          with ExitStack() as sAB:
              Tb = sb("Tb", [128, 3, 12, 256], F32, sAB)
              xTb = sb("xTb", [128, 8, S], BF16, sAB)
              B_Tb = k.buf("Tb")
              B_x = [k.buf(f"xTb{i}") for i in range(8)]
              for kc in range(8):
                  k.dma(POOL, xTb[:, kc, :], xT[s, kc * 128:(kc + 1) * 128, :], writes=[B_x[kc]])

              with ExitStack() as sC:
                  convD = sb("convD", [128, 2, CK, 128], BF16, sC)
                  wc = sb("wc", [128, 8, 512], BF16, sC)
                  uB = sb("uB", [128, 2, S + 32], BF16, sC)
                  cv = sb("cv", [128, 2, S], F32, sC)
                  t1 = [sb(f"ct1_{i}", [128, 512], F32, sC) for i in range(2)]
                  t2 = [sb(f"ct2_{i}", [128, 512], F32, sC) for i in range(2)]
                  cb = [sb(f"cb_{i}", [128, 512], BF16, sC) for i in range(2)]
                  cq = [sb(f"cq_{i}", [128, 512], BF16, sC) for i in range(2)]
                  cm = sb("cm", [128, S], F32, sC)
                  cr = sb("cr", [128, S], F32, sC)
                  B_cD, B_wc, B_uB, B_cv = k.buf("convD"), k.buf("wc"), k.buf("uB"), k.buf("cv")
                  B_cDc = [k.buf("convD0"), k.buf("convD1")]
                  B_t1 = [k.buf(f"ct1_{i}") for i in range(2)]
                  B_t2 = [k.buf(f"ct2_{i}") for i in range(2)]
                  B_cb = [k.buf(f"cb_{i}") for i in range(2)]
                  B_cq = [k.buf(f"cq_{i}") for i in range(2)]
                  B_cm, B_cr = k.buf("cm"), k.buf("cr")

                  for kc in range(0, 8, 2):
                      k.dma(POOL, wc[:, kc:kc + 2, :], w_in_v[:, kc:kc + 2, 2304:2816], writes=[B_wc])
                  for cc in range(2):
                      for j in range(CK):
                          if cc == 0:
                              k.op(DVE, lambda: nc.vector.tensor_scalar(out=convD[:, cc, j, :], in0=identb[:, :],
                                                                        scalar1=pder[:, cc * CK + j:cc * CK + j + 1],
                                                                        scalar2=None, op0=ALU.mult),
                                   reads=[B_ident, B_pder], writes=[B_cDc[cc]])
                          else:
                              k.op(ACT, lambda: nc.scalar.activation(out=convD[:, cc, j, :], in_=identb[:, :],
                                                                     func=AF.Copy,
                                                                     scale=pder[:, cc * CK + j:cc * CK + j + 1]),
                                   reads=[B_ident, B_pder], writes=[B_cDc[cc]])
                  k.op(DVE, lambda: nc.vector.memset(uB[:, :, 0:30], 0.0), writes=[B_uB])
                  it = 0
                  for cc in range(2):
                      for tt in range(4):
                          tsl = slice(tt * 512, (tt + 1) * 512)
                          i = it % 2
                          pa, pg = 2 * i, 2 * i + 1
                          for kc in range(8):
                              k.mm(PS[pa][:, :], wc[:, kc, cc * 128:(cc + 1) * 128], xTb[:, kc, tsl], kc == 0, kc == 7,
                                   reads=[B_wc, B_x[kc]], wbuf=PSB[pa], first=(kc == 0), last=(kc == 7))
                          for kc in range(8):
                              k.mm(PS[pg][:, :], wc[:, kc, 256 + cc * 128:256 + (cc + 1) * 128], xTb[:, kc, tsl],
                                   kc == 0, kc == 7, reads=[B_wc, B_x[kc]], wbuf=PSB[pg], first=(kc == 0),
                                   last=(kc == 7))
                          k.op(ACT, lambda: nc.scalar.activation(out=t1[i][:], in_=PS[pg][:, :], func=AF.Tanh,
                                                                 bias=pder[:, 62 + cc:63 + cc], scale=0.5),
                               reads=[PSB[pg], B_pder], writes=[B_t1[i]])
                          k.op(ACT, lambda: nc.scalar.activation(out=t2[i][:], in_=PS[pa][:, :], func=AF.Identity,
                                                                 bias=pc("b_in", 18 + cc), scale=1.0),
                               reads=[PSB[pa], B_pvec], writes=[B_t2[i]])
                          k.op(DVE, lambda: nc.vector.scalar_tensor_tensor(
                              out=uB[:, cc, 30 + tt * 512:30 + (tt + 1) * 512], in0=t1[i][:], scalar=1.0,
                              in1=t2[i][:], op0=ALU.add, op1=ALU.mult),
                              reads=[B_t1[i], B_t2[i]], writes=[B_uB])
                          it += 1
                  for tt in range(4):
                      tsl = slice(tt * 512, (tt + 1) * 512)
                      for cc in range(2):
                          pb = (tt * 2 + cc) % 2
                          for j in range(CK):
                              k.mm(PS[pb][:, :], convD[:, cc, j, :], uB[:, cc, tt * 512 + j:tt * 512 + j + 512],
                                   j == 0, j == CK - 1, reads=[B_cDc[cc], B_uB], wbuf=PSB[pb], first=(j == 0),
                                   last=(j == CK - 1))
                          k.op(ACT, lambda: nc.scalar.activation(out=cv[:, cc, tsl], in_=PS[pb][:, :],
                                                                 func=AF.Identity, bias=pc("conv_b", cc), scale=1.0),
                               reads=[PSB[pb], B_pvec], writes=[B_cv])
                          k.op(ACT, lambda: nc.scalar.activation(out=cq[cc][:], in_=PS[pb][:, :], func=AF.Square,
                                                                 bias=pc("conv_b", cc), scale=1.0),
                               reads=[PSB[pb], B_pvec], writes=[B_cq[cc]])
                          k.op(DVE, lambda: nc.vector.tensor_copy(out=cb[cc][:], in_=cv[:, cc, tsl]),
                               reads=[B_cv], writes=[B_cb[cc]])
                      for cc in range(2):
                          k.mm(PS[2][:, :], onesb[:, :], cb[cc][:], cc == 0, cc == 1, reads=[B_ones, B_cb[cc]],
                               wbuf=PSB[2], first=(cc == 0), last=(cc == 1), sig=True)
                          k.mm(PS[3][:, :], onesb[:, :], cq[cc][:], cc == 0, cc == 1, reads=[B_ones, B_cq[cc]],
                               wbuf=PSB[3], first=(cc == 0), last=(cc == 1), sig=True)
                      k.op(DVE, lambda: nc.vector.tensor_scalar(out=cm[:, tsl], in0=PS[2][:, :], scalar1=1.0 / CW,
                                                                scalar2=None, op0=ALU.mult),
                           reads=[PSB[2]], writes=[B_cm])
                      k.op(DVE, lambda: nc.vector.tensor_tensor(out=cr[:, tsl], in0=cm[:, tsl], in1=cm[:, tsl],
                                                                op=ALU.mult), reads=[B_cm], writes=[B_cr])
                      k.op(DVE, lambda: nc.vector.scalar_tensor_tensor(out=cr[:, tsl], in0=PS[3][:, :],
                                                                       scalar=1.0 / CW, in1=cr[:, tsl],
                                                                       op0=ALU.mult, op1=ALU.subtract),
                           reads=[PSB[3], B_cr], writes=[B_cr])
                  with ExitStack() as sT:
                      Hk = [sb(f"Hk{i}", [128, 4, 2, 128], F32, sT) for i in range(2)]
                      B_Hk = [k.buf(f"Hk{i}") for i in range(2)]
                      it = 0
                      for b in range(3):
                          for h0 in range(0, 12, 4):
                              i = it % 2
                              if b < 2:
                                  src = bass.AP(tensor=cdram.tensor, offset=h0 * 3 * 384 + b * 384,
                                                ap=[[1, 128], [3 * 384, 4], [128, 2], [1, 128]])
                                  k.dma(SP, Hk[i][:], src, reads=[k.buf("cdram")], writes=[B_Hk[i]])
                              else:
                                  for part in range(2):
                                      src = bass.AP(tensor=cdram.tensor, offset=h0 * 3 * 384 + b * 384,
                                                    ap=[[1, 128], [3 * 384, 4], [1, 128]])
                                      k.dma(SP, Hk[i][:, :, part, :], src, reads=[k.buf("cdram")], writes=[B_Hk[i]])
                              for half in range(2):
                                  pb = (it * 2 + half) % 2
                                  k.mm(PS[pb][:, :], jmat[:, :],
                                       Hk[i][:, 2 * half:2 * half + 2, :, :].rearrange("p a b c -> p (a b c)"),
                                       True, True, reads=[B_j, B_Hk[i]], wbuf=PSB[pb], first=True, last=True)
                                  k.op(DVE, lambda: nc.vector.tensor_copy(
                                      out=Tb[:, b, h0 + 2 * half:h0 + 2 * half + 2, :].rearrange("p a b -> p (a b)"),
                                      in_=PS[pb][:, :]), reads=[PSB[pb]], writes=[B_Tb])
                              it += 1
                  if DEBUG and s == 0:
                      k.dma(SP, dbg_tb, Tb[:], reads=[B_Tb], writes=[k.buf("dbg_tb")])
                  stage("tables")

                  k.op(ACT, lambda: nc.scalar.activation(out=cr[:], in_=cr[:], func=AF.Ln, bias=EPS, scale=1.0),
                       reads=[B_cr], writes=[B_cr])
                  k.op(ACT, lambda: nc.scalar.activation(out=cr[:], in_=cr[:], func=AF.Exp, scale=-0.5),
                       reads=[B_cr], writes=[B_cr])
                  for cc in range(2):
                      k.op(DVE, lambda: nc.vector.tensor_tensor(out=cv[:, cc, :], in0=cv[:, cc, :], in1=cm[:],
                                                                op=ALU.subtract), reads=[B_cv, B_cm], writes=[B_cv])
                      k.op(DVE, lambda: nc.vector.tensor_tensor(out=cv[:, cc, :], in0=cv[:, cc, :], in1=cr[:],
                                                                op=ALU.mult), reads=[B_cv, B_cr], writes=[B_cv])
                  for cc in range(2):
                      k.op(ACT, lambda: nc.scalar.activation(out=cv[:, cc, :], in_=cv[:, cc, :], func=AF.Silu,
                                                             bias=pc("cln_b", cc), scale=pc("cln_g", cc)),
                           reads=[B_cv, B_pvec], writes=[B_cv])
                  for tt in range(4):
                      tsl = slice(tt * 512, (tt + 1) * 512)
                      for cc in range(2):
                          k.op(ACT, lambda: nc.scalar.activation(out=cq[cc][:], in_=cv[:, cc, tsl], func=AF.Square),
                               reads=[B_cv], writes=[B_cq[cc]])
                          k.mm(PS[3][:, :], onesb[:, :], cq[cc][:], cc == 0, cc == 1, reads=[B_ones, B_cq[cc]],
                               wbuf=PSB[3], first=(cc == 0), last=(cc == 1), sig=True)
                      k.op(ACT, lambda: nc.scalar.activation(out=cr[:, tsl], in_=PS[3][:, :], func=AF.Ln, bias=EPS,
                                                             scale=1.0 / CW), reads=[PSB[3]], writes=[B_cr])
                  k.op(ACT, lambda: nc.scalar.activation(out=cr[:], in_=cr[:], func=AF.Exp, scale=-0.5),
                       reads=[B_cr], writes=[B_cr])
                  for cc in range(2):
                      k.op(DVE, lambda: nc.vector.scalar_tensor_tensor(out=uN[:, cc, :], in0=cv[:, cc, :],
                                                                       scalar=pc("cn_g", cc), in1=cr[:],
                                                                       op0=ALU.mult, op1=ALU.mult),
                           reads=[B_cv, B_cr, B_pvec], writes=[B_uN])
                  k.barrier()
              if DEBUG:
                  k.dma(SP, dbg_un[s], uN[:], reads=[B_uN], writes=[k.buf("dbg_un")])
              stage("conv")

              with ExitStack() as sA:
                  NST = 4
                  wq = [sb(f"wq{i}", [128, 8, 384], BF16, sA) for i in range(2)]
                  qz = [sb(f"qz{i}", [128, S], BF16, sA) for i in range(2)]
                  kv = sb("kv", [128, 2, S], BF16, sA)
                  Vtok = sb("Vtok", [128, 96, 65], BF16, sA)
                  acc = [sb(f"acc{i}", [128, S], F32, sA) for i in range(2)]
                  denb = sb("denb", [128, S], F32, sA)
                  dq = sb("dq", [128, 2, 16], F32, sA)
                  stmp = [sb(f"stmp{i}", [128, 2, 256], F32, sA) for i in range(NST)]
                  PT = [sb(f"PT{i}", [128, 2, 256], BF16, sA) for i in range(NST)]
                  mtmp = [sb(f"mtmp{i}", [128, 512], F32, sA) for i in range(4)]
                  sq = [sb(f"asq{i}", [128, 512], BF16, sA) for i in range(2)]
                  rra = sb("rra", [128, S], F32, sA)
                  B_wq = [k.buf(f"wq{i}") for i in range(2)]
                  B_qz = [k.buf(f"qz{i}") for i in range(2)]
                  B_kv = [k.buf(f"kv{i}") for i in range(2)]
                  B_V = k.buf("Vtok")
                  B_Vg = [[k.buf(f"Vtok{b_}_{g_}") for g_ in range(4)] for b_ in range(3)]
                  B_acc = [k.buf(f"acc{i}") for i in range(2)]
                  B_denb = k.buf("denb")
                  B_dnd = k.buf("dnd")
                  B_dnd2 = k.buf("dnd2")
                  B_dq = k.buf("dq")
                  B_st = [k.buf(f"stmp{i}") for i in range(NST)]
                  B_PT = [k.buf(f"PT{i}") for i in range(NST)]
                  B_mt = [k.buf(f"mtmp{i}") for i in range(4)]
                  B_sq = [k.buf(f"asq{i}") for i in range(2)]
                  B_rra = k.buf("rra")

                  k.op(DVE, lambda: nc.vector.memset(Vtok[:, :, 64:65], 1.0), writes=[B_V])
                  k.op(POOL, lambda: nc.gpsimd.memset(qz[0][64:128, :], 0.0), writes=[B_qz[0]])
                  k.op(POOL, lambda: nc.gpsimd.memset(qz[1][0:64, :], 0.0), writes=[B_qz[1]])

                  def load_wq(c):
                      i = c % 2
                      for j in range(3):
                          k.dma(POOL, wq[i][:, :, j * 128:(j + 1) * 128],
                                w_in_v[:, :, j * AW + c * 128:j * AW + (c + 1) * 128], writes=[B_wq[i]])

                  evac_flip = [0]
                  mt_i = [0]

                  def project(c):
                      wi = c % 2
                      for j in range(3):
                          for tt in range(4):
                              tsl = slice(tt * 512, (tt + 1) * 512)
                              pb = (j * 4 + tt) % 4
                              for kc in range(8):
                                  k.mm(PS[pb][:, :], wq[wi][:, kc, j * 128:(j + 1) * 128], xTb[:, kc, tsl],
                                       kc == 0, kc == 7, reads=[B_wq[wi], B_x[kc]], wbuf=PSB[pb],
                                       first=(kc == 0), last=(kc == 7))
                              if j == 0:
                                  for hh in range(2):
                                      ps_ = slice(hh * 64, (hh + 1) * 64)
                                      k.op(ACT, lambda: nc.scalar.activation(
                                          out=qz[hh][ps_, tsl], in_=PS[pb][ps_, :], func=AF.Identity,
                                          bias=pvec[ps_, PCOL["b_in"] + c:PCOL["b_in"] + c + 1], scale=1.0),
                                          reads=[PSB[pb], B_pvec], writes=[B_qz[hh]])
                              else:
                                  k.op(ACT, lambda: nc.scalar.activation(out=kv[:, j - 1, tsl], in_=PS[pb][:, :],
                                                                         func=AF.Identity,
                                                                         bias=pc("b_in", j * 6 + c), scale=1.0),
                                       reads=[PSB[pb], B_pvec], writes=[B_kv[j - 1]])

                  def vtok(c):
                      for b, (d, nres, nblk) in enumerate(BRANCH):
                          for g in range(4):
                              pb = g % 4
                              pv = PS[pb][:, 0:256].bitcast(BF16)
                              for jj in range(4):
                                  blk = 4 * g + jj
                                  if d == 1:
                                      tsl = tok_ap(None, 1, 0, blk)
                                  elif d == 4:
                                      tsl = tok_ap(None, 4, blk // 4, blk % 4)
                                  else:
                                      tsl = tok_ap(None, 16, blk, 0)
                                  k.mm(pv[:, jj * 128:(jj + 1) * 128], kv[:, 1, tsl], identb[:, :], True, True,
                                       reads=[B_kv[1], B_ident], wbuf=PSB[pb], first=(jj == 0), last=(jj == 3),
                                       transpose=True)
                              src = pv.rearrange("p (a e) -> p a e", a=8)
                              dst = Vtok[:, (b * 16 + 4 * g) * 2:(b * 16 + 4 * g + 4) * 2, 0:64]
                              if evac_flip[0] % 2 == 0:
                                  k.op(DVE, lambda: nc.vector.tensor_copy(out=dst, in_=src), reads=[PSB[pb], B_V],
                                       writes=[B_Vg[b][g]])
                              else:
                                  k.op(ACT, lambda: nc.scalar.copy(out=dst, in_=src), reads=[PSB[pb], B_V],
                                       writes=[B_Vg[b][g]])
                              evac_flip[0] += 1

                  def post(c):
                      for hh in range(2):
                          k.dma(SP, dnd[hh:hh + 1, :], acc[hh][64:65, :], reads=[B_acc[hh]], writes=[B_dnd])
                      k.dma(SP, acc[0][64:128, :], acc[1][0:64, :], reads=[B_acc[1], B_dnd], writes=[B_acc[0]])
                      src = bass.AP(tensor=dnd.tensor, offset=0, ap=[[16, 128], [S, 2], [1, 16]])
                      k.dma(SP, dq[:], src, reads=[B_dnd], writes=[B_dq])
                      k.op(DVE, lambda: nc.vector.reciprocal(out=dq[:], in_=dq[:]), reads=[B_dq], writes=[B_dq])
                      dst = bass.AP(tensor=dnd2.tensor, offset=0, ap=[[16, 128], [S, 2], [1, 16]])
                      k.dma(SP, dst, dq[:], reads=[B_dq], writes=[B_dnd2])
                      for hh in range(2):
                          src = bass.AP(tensor=dnd2.tensor, offset=hh * S, ap=[[0, 64], [1, S]])
                          k.dma(SP, denb[hh * 64:(hh + 1) * 64, :], src, reads=[B_dnd2], writes=[B_denb])

                  def finalize(c):
                      k.op(DVE, lambda: nc.vector.tensor_tensor(out=attnT[:, c, :], in0=acc[0][:], in1=denb[:],
                                                                op=ALU.mult),
                           reads=[B_acc[0], B_denb], writes=[B_attn[c]])

                  def steps_for(c):
                      pend = []
                      gs = [0]

                      def flush(upto=None, obs=None):
                          keep = []
                          for item in pend:
                              at, ob_l, fn = item
                              if (upto is None and obs is None) or (upto is not None and at <= upto) or \
                                      (obs is not None and any(o in obs for o in ob_l)):
                                  fn()
                              else:
                                  keep.append(item)
                          pend[:] = keep

                      allsteps = []
                      for b, (d, nres, nblk) in enumerate(BRANCH):
                          if d == 16:
                              st_ = [[(r0, 0, 0, 128), (r0 + 1, 0, 128, 128)] for r0 in range(0, 16, 2)]
                          else:
                              st_ = [[(r, n, 0, 256 if n + 1 < nblk else 128)] for r in range(nres)
                                     for n in range(nblk)]
                          allsteps += [(b, d, subs) for subs in st_]

                      def group_of(d, r, n):
                          if d == 1:
                              return n // 4, n % 4
                          if d == 4:
                              return r, n
                          return r // 4, r % 4

                      def qk(si):
                          cur = si % NST
                          b, d, subs = allsteps[si]
                          nmm = 2 * len(subs)
                          i_ = 0
                          for hh in range(2):
                              for (r, n, co, N) in subs:
                                  k.mm(PS[cur][:, hh * 256 + co:hh * 256 + co + N],
                                       kv[:, 0, tok_ap(None, d, r, n)],
                                       qz[hh][:, tok_ap(None, d, r, n, 2 if N == 256 else 1)], True, True,
                                       reads=[B_kv[0], B_qz[hh]], wbuf=PSB[cur], first=(i_ == 0),
                                       last=(i_ == nmm - 1))
                                  i_ += 1

                      for si in range(min(NST - 1, len(allsteps))):
                          qk(si)
                      for si, (b, d, subs) in enumerate(allsteps):
                          cur = si % NST
                          NT = max(co + N for (_, _, co, N) in subs)
                          if si + NST - 1 < len(allsteps):
                              qk(si + NST - 1)
                          sv = PS[cur][:, :].rearrange("p (h n) -> p h n", h=2)[:, :, 0:NT]
                          k.op(DVE, lambda: nc.vector.scalar_tensor_tensor(
                              out=stmp[cur][:, :, 0:NT], in0=sv, scalar=0.125,
                              in1=Tb[:, b, 2 * c:2 * c + 2, 0:NT], op0=ALU.mult, op1=ALU.add),
                              reads=[PSB[cur], B_Tb], writes=[B_st[cur]])
                          k.op(ACT, lambda: nc.scalar.activation(out=PT[cur][:, :, 0:NT],
                                                                 in_=stmp[cur][:, :, 0:NT], func=AF.Exp),
                               reads=[B_st[cur]], writes=[B_PT[cur]])
                          batch = []
                          closing = []
                          for hh in range(2):
                              for (r, n, co, N) in subs:
                                  blk = n if d == 1 else (r * 4 + n if d == 4 else r)
                                  g, slot = group_of(d, r, n)
                                  ob = 4 + 2 * hh + (g % 2)
                                  vaug = Vtok[:, (b * 16 + blk) * 2 + hh, :]
                                  batch.append((PS[ob][0:65, slot * 128:(slot + 1) * 128], vaug,
                                                PT[cur][:, hh, co:co + 128], n == 0, True, ob,
                                                (slot == 0 and n == 0)))
                                  if slot == 3:
                                      closing.append((hh, g, ob))
                                  if N == 256:
                                      g2, slot2 = group_of(d, r, n + 1)
                                      ob2 = 4 + 2 * hh + (g2 % 2)
                                      batch.append((PS[ob2][0:65, slot2 * 128:(slot2 + 1) * 128], vaug,
                                                    PT[cur][:, hh, co + 128:co + 256], True, False, ob2,
                                                    (slot2 == 0)))
                          vb_ = []
                          for (r, n, co, N) in subs:
                              blk = n if d == 1 else (r * 4 + n if d == 4 else r)
                              if B_Vg[b][blk // 4] not in vb_:
                                  vb_.append(B_Vg[b][blk // 4])
                          opening = [ob_ for (_, _, _, _, _, ob_, fi_) in batch if fi_]
                          if opening:
                              flush(obs=opening)
                          for bi_, (o_, l_, r_, st_, sp_, ob_, fi_) in enumerate(batch):
                              k.mm(o_, l_, r_, st_, sp_, reads=vb_ + [B_PT[cur]], wbuf=PSB[ob_], first=fi_,
                                   last=False, sig=(bi_ == len(batch) - 1))
                          if closing and not k.stopped:
                              tokc = PE.last()
                              for (hh, g, ob) in closing:
                                  PSB[ob].w = tokc
                                  PSB[ob].r = {}
                                  PSB[ob].consumed = False
                                  if d == 1:
                                      a_ap = acc[hh][0:65, 512 * g:512 * (g + 1)]
                                      p_ap = PS[ob][0:65, :]
                                  elif d == 4:
                                      a_ap = acc[hh][0:65, g:S:4]
                                      p_ap = PS[ob][0:65, :]
                                  else:
                                      a_ap = acc[hh][0:65, :].rearrange("p (i r) -> p r i", r=16)[:, 4 * g:4 * g + 4, :]
                                      p_ap = PS[ob][0:65, :].rearrange("p (s i) -> p s i", s=4)

                                  def merge(a_ap=a_ap, p_ap=p_ap, ob=ob, hh=hh, b=b, d=d):
                                      if b == 0:
                                          k.op(ACT, lambda: nc.scalar.copy(out=a_ap, in_=p_ap), reads=[PSB[ob]],
                                               writes=[B_acc[hh]])
                                      else:
                                          mi = mt_i[0] % 4
                                          mt_i[0] += 1
                                          m_ap = mtmp[mi][0:65, :]
                                          if d == 16:
                                              m_ap = m_ap.rearrange("p (s i) -> p s i", s=4)
                                          k.op(ACT, lambda: nc.scalar.copy(out=m_ap, in_=p_ap), reads=[PSB[ob]],
                                               writes=[B_mt[mi]])
                                          k.op(POOL, lambda: nc.gpsimd.tensor_tensor(out=a_ap, in0=m_ap,
                                                                                    in1=a_ap, op=ALU.add),
                                               reads=[B_mt[mi], B_acc[hh]], writes=[B_acc[hh]])
                                  pend.append((gs[0] + 2, [ob], merge))
                          flush(upto=gs[0])
                          gs[0] += 1
                      flush()

                  load_wq(0)
                  for c in range(6):
                      if c + 1 < 6:
                          load_wq(c + 1)
                      project(c)
                      vtok(c)
                      if c > 0:
                          finalize(c - 1)
                      steps_for(c)
                      post(c)
                  finalize(5)
                  stage("attfin")
                  for tt in range(4):
                      tsl = slice(tt * 512, (tt + 1) * 512)
                      for c in range(6):
                          i = c % 2
                          k.op(ACT, lambda: nc.scalar.activation(out=sq[i][:], in_=attnT[:, c, tsl], func=AF.Square),
                               reads=[B_attn[c]], writes=[B_sq[i]])
                          k.mm(PS[0][:, :], onesb[:, :], sq[i][:], c == 0, c == 5, reads=[B_ones, B_sq[i]],
                               wbuf=PSB[0], first=(c == 0), last=(c == 5), sig=True)
                      k.op(ACT, lambda: nc.scalar.activation(out=rra[:, tsl], in_=PS[0][:, :], func=AF.Ln, bias=EPS,
                                                             scale=1.0 / AW), reads=[PSB[0]], writes=[B_rra])
                  k.op(ACT, lambda: nc.scalar.activation(out=rra[:], in_=rra[:], func=AF.Exp, scale=-0.5),
                       reads=[B_rra], writes=[B_rra])
                  for c in range(6):
                      k.op(DVE, lambda: nc.vector.scalar_tensor_tensor(out=attnT[:, c, :], in0=attnT[:, c, :],
                                                                       scalar=pc("an_g", c), in1=rra[:],
                                                                       op0=ALU.mult, op1=ALU.mult),
                           reads=[B_attn[c], B_rra, B_pvec], writes=[B_attn[c]])
                  if DEBUG:
                      k.dma(SP, dbg_attn[s], attnT[:], reads=B_attn, writes=[k.buf("dbg_attn")])
                  stage("attn")
                  k.barrier()
              k.barrier()

          with ExitStack() as sF:
              R = sb("R", [128, 8, 1024], F32, sF)
              x1b = sb("x1b", [128, 8, 1024], BF16, sF)
              zb = [sb(f"zb{i}", [128, 512], BF16, sF) for i in range(3)]
              zsq = [sb(f"zsq{i}", [128, 512], BF16, sF) for i in range(3)]
              lm = sb("lm", [128, 512], F32, sF)
              lq = sb("lq", [128, 512], F32, sF)
              wo = [sb(f"wo{i}", [128, 8, 128], BF16, sF) for i in range(3)]
              actT = sb("actT", [128, 22, 1024], BF16, sF)
              Hrow = [[sb(f"Hrow{g}_{i}", [128, 1026], F32, sF) for i in range(2)] for g in range(2)]
              Tt = [[sb(f"Tt{g}_{i}", [128, 1024], F32, sF) for i in range(2)] for g in range(2)]
              sgt = [sb(f"sgt{i}", [128, 1024], F32, sF) for i in range(2)]
              wu = [sb(f"wu{i}", [128, 8, 2, 128], BF16, sF) for i in range(3)]
              wd = [sb(f"wd{i}", [128, 22, 128], BF16, sF) for i in range(2)]
              B_Rt = [[k.buf(f"R{t}_{o}") for o in range(8)] for t in range(2)]
              B_x1t = [[k.buf(f"x1b{t}_{o}") for o in range(8)] for t in range(2)]
              B_wo = [k.buf(f"wo{i}") for i in range(3)]
              B_act = [k.buf(f"actT{i}") for i in range(22)]
              B_H = [[k.buf(f"Hrow{g}_{i}") for i in range(2)] for g in range(2)]
              B_Hh = [[k.buf(f"Hrowh{g}_{i}") for i in range(2)] for g in range(2)]
              B_hc = [k.buf(f"halo{i}") for i in range(44)]
              B_T = [[k.buf(f"Tt{g}_{i}") for i in range(2)] for g in range(2)]
              B_sg = [k.buf(f"sgt{i}") for i in range(2)]
              B_wu = [k.buf(f"wu{i}") for i in range(3)]
              B_wd = [k.buf(f"wd{i}") for i in range(2)]
              B_out = k.buf("outT")
              tmp = dict(zb=zb, zsq=zsq, B_zb=[k.buf(f"zb{i}") for i in range(3)],
                         B_zsq=[k.buf(f"zsq{i}") for i in range(3)],
                         m=lm, q=lq, B_m=k.buf("lm"), B_q=k.buf("lq"), rot=[0])
              MB = [0, 1, 4, 5, 6, 7]
              k.op(POOL, lambda: nc.gpsimd.memset(halo[:], 0.0), writes=[B_halo] + B_hc)
              wo_n = [0]
              wd_n = [0]

              def load_wo(oc):
                  i = wo_n[0] % 3
                  wo_n[0] += 1
                  k.dma(POOL, wo[i][:], w_out_v[:, :, oc * 128:(oc + 1) * 128], writes=[B_wo[i]])
                  return i

              def load_wu(fp):
                  i = fp % 3
                  k.dma(POOL, wu[i][:, :, 0, :], w_up_v[:, :, fp * 128:(fp + 1) * 128], writes=[B_wu[i]])
                  k.dma(POOL, wu[i][:, :, 1, :], w_up_v[:, :, DFF + fp * 128:DFF + (fp + 1) * 128],
                        writes=[B_wu[i]])

              def load_wd(oc):
                  i = wd_n[0] % 2
                  wd_n[0] += 1
                  k.dma(POOL, wd[i][:, 0:11, :], w_down_v[:, 0:11, oc * 128:(oc + 1) * 128], writes=[B_wd[i]])
                  k.dma(POOL, wd[i][:, 11:22, :], w_down_v[:, 11:22, oc * 128:(oc + 1) * 128], writes=[B_wd[i]])
                  return i

              mmi = 0
              for hf in range(2):
                  t0 = hf * 1024
                  for tt in range(2):
                      tl = slice(tt * 512, (tt + 1) * 512)
                      for kc in range(8):
                          k.dma(SP, R[:, kc, tl], xT[s, kc * 128:(kc + 1) * 128, t0 + tt * 512:t0 + (tt + 1) * 512],
                                writes=[B_Rt[tt][kc]])
                  for tt in range(2):
                      tl = slice(tt * 512, (tt + 1) * 512)
                      tg = slice(t0 + tt * 512, t0 + (tt + 1) * 512)
                      wq_ = [load_wo(0), load_wo(1)]
                      for oc in range(8):
                          if oc + 2 < 8:
                              wq_.append(load_wo(oc + 2))
                          wi = wq_[oc]
                          pb = MB[mmi % 6]
                          mmi += 1
                          for kc in range(8):
                              rhs = attnT[:, kc, tg] if kc < 6 else uN[:, kc - 6, tg]
                              rb = B_attn[kc] if kc < 6 else B_uN
                              k.mm(PS[pb][:, :], wo[wi][:, kc, :], rhs, kc == 0, kc == 7,
                                   reads=[B_wo[wi], rb], wbuf=PSB[pb], first=(kc == 0), last=(kc == 7))
                          k.op(DVE, lambda: nc.vector.scalar_tensor_tensor(out=R[:, oc, tl], in0=R[:, oc, tl],
                                                                           scalar=ALPHA, in1=PS[pb][:, :],
                                                                           op0=ALU.mult, op1=ALU.add),
                               reads=[B_Rt[tt][oc], PSB[pb]], writes=[B_Rt[tt][oc]])
                      layer_norm_tile(R, B_Rt[tt], tl, "ln1_g", "ln1_b", (2, 3), tmp, outb=x1b, B_outb=B_x1t[tt])
                  if DEBUG:
                      k.dma(SP, dbg_x1[s][:, :, t0:t0 + 1024], x1b[:], reads=B_x1t[0] + B_x1t[1],
                            writes=[k.buf("dbg_x1")])
                  if hf == 0:
                      stage("C0")

                  def silu_mul(fq):
                      bq = fq % 2
                      k.op(ACT, lambda: nc.scalar.activation(out=sgt[bq][:], in_=Tt[0][bq][:], func=AF.Silu),
                           reads=[B_T[0][bq]], writes=[B_sg[bq]])
                      k.op(DVE, lambda: nc.vector.tensor_tensor(out=actT[:, fq, :], in0=sgt[bq][:],
                                                                in1=Tt[1][bq][:], op=ALU.mult),
                           reads=[B_sg[bq], B_T[1][bq]], writes=[B_act[fq]])

                  load_wu(0)
                  load_wu(1)
                  for fp in range(22):
                      if fp + 2 < 22:
                          load_wu(fp + 2)
                      wi = fp % 3
                      bi = fp % 2
                      for gv in range(2):
                          ch = gv * 22 + fp
                          H = Hrow[gv][bi]
                          BH = B_H[gv][bi]
                          BHh = B_Hh[gv][bi]
                          k.op(POOL, lambda: nc.gpsimd.tensor_copy(out=H[:, 0:2], in_=halo[:, ch, :]),
                               reads=[B_hc[ch]], writes=[BHh])
                          for tt in range(2):
                              tl = slice(tt * 512, (tt + 1) * 512)
                              pb = MB[mmi % 6]
                              mmi += 1
                              for kc in range(8):
                                  k.mm(PS[pb][:, :], wu[wi][:, kc, gv, :], x1b[:, kc, tl], kc == 0, kc == 7,
                                       reads=[B_wu[wi], B_x1t[tt][kc]], wbuf=PSB[pb], first=(kc == 0), last=(kc == 7))
                              k.op(ACT, lambda: nc.scalar.copy(out=H[:, 2 + tt * 512:2 + (tt + 1) * 512],
                                                               in_=PS[pb][:, :]), reads=[PSB[pb]], writes=[BH])
                      for gv in range(2):
                          ch = gv * 22 + fp
                          H = Hrow[gv][bi]
                          BH = B_H[gv][bi]
                          T = Tt[gv][bi]
                          BT = B_T[gv][bi]
                          k.op(ACT, lambda: nc.scalar.copy(out=halo[:, ch, :], in_=H[:, 1024:1026]),
                               reads=[BH], writes=[B_hc[ch]])
                          k.op(ACT, lambda: nc.scalar.activation(out=T[:], in_=H[:, 2:1026], func=AF.Identity,
                                                                 bias=pc("fb", ch), scale=pc("fw", 2 * 44 + ch)),
                               reads=[BH, B_pvec], writes=[BT])
                      for tap, wcol in ((1, 44), (0, 0)):
                          for gv in range(2):
                              ch = gv * 22 + fp
                              H = Hrow[gv][bi]
                              T = Tt[gv][bi]
                              k.op(DVE, lambda: nc.vector.scalar_tensor_tensor(out=T[:], in0=H[:, tap:tap + 1024],
                                                                               scalar=pc("fw", wcol + ch), in1=T[:],
                                                                               op0=ALU.mult, op1=ALU.add),
                                   reads=[B_H[gv][bi], B_Hh[gv][bi], B_T[gv][bi], B_pvec], writes=[B_T[gv][bi]])
                      if fp >= 1:
                          silu_mul(fp - 1)
                      if fp == 19:
                          wdq = [load_wd(0)]
                      if fp == 20:
                          wdq.append(load_wd(1))
                  silu_mul(21)
                  if DEBUG and hf == 0 and s == 0:
                      k.dma(SP, dbg_act, actT[:], reads=B_act, writes=[k.buf("dbg_act")])
                  if hf == 0:
                      stage("D0")
                  MBE = [4, 5, 6, 7]
                  SPE = [(2, 3), (0, 1)]
                  pend = []
                  for oc in range(8):
                      wi = wdq[oc]
                      for tt in range(2):
                          tl = slice(tt * 512, (tt + 1) * 512)
                          pb = MBE[mmi % 4]
                          mmi += 1
                          for kc in range(22):
                              k.mm(PS[pb][:, :], wd[wi][:, kc, :], actT[:, kc, tl], kc == 0, kc == 21,
                                   reads=[B_wd[wi], B_act[kc]], wbuf=PSB[pb], first=(kc == 0), last=(kc == 21))
                          for f in pend:
                              f()
                          pend = []
                          k.op(DVE, lambda: nc.vector.scalar_tensor_tensor(out=R[:, oc, tl], in0=R[:, oc, tl],
                                                                           scalar=ALPHA, in1=PS[pb][:, :],
                                                                           op0=ALU.mult, op1=ALU.add),
                               reads=[B_Rt[tt][oc], PSB[pb]], writes=[B_Rt[tt][oc]])
                          pend.append(ln_stats_chunk(R, B_Rt[tt], tl, oc, SPE[tt], tmp))
                      if oc + 2 < 8:
                          wdq.append(load_wd(oc + 2))
                  for f in pend:
                      f()
                  for tt in range(2):
                      tl = slice(tt * 512, (tt + 1) * 512)
                      layer_norm_tile(R, B_Rt[tt], tl, "ln2_g", "ln2_b", SPE[tt], tmp, skip_stats=True)
                      for kc in range(8):
                          k.dma(SP, outT[s, kc * 128:(kc + 1) * 128, t0 + tt * 512:t0 + (tt + 1) * 512],
                                R[:, kc, tl], reads=[B_Rt[tt][kc]], writes=[], sbuf=B_out)
                  if hf == 0:
                      stage("E0")
              k.barrier()
    except _Stop:
        pass
    k.barrier()
    es.close()
    print("semaphores used:", k.nsem, "PE sigs", PE.n, "ACT", ACT.n, "DVE", DVE.n, "POOL", POOL.n)
    return nc


_NC_CACHE = {}


def kernel(**inp):
    x = np.asarray(inp["x"], np.float32)
    pv = pack_params(inp)
    relx = np.concatenate([np.asarray(inp["rel_table"], np.float32), np.ones((1, 12), np.float32)], axis=0)
    oh = onehot_const()
    ident = np.eye(128, dtype=np.float32)
    jmat = np.ascontiguousarray(np.eye(128, dtype=np.float32)[::-1])
    w_in = np.ascontiguousarray(np.asarray(inp["w_in"], np.float32)[0])
    w_out = np.ascontiguousarray(np.asarray(inp["w_out"], np.float32)[0])
    w_up = np.ascontiguousarray(np.asarray(inp["w_up"], np.float32)[0])
    w_down = np.ascontiguousarray(np.asarray(inp["w_down"], np.float32)[0])
    if "nc" not in _NC_CACHE:
        _NC_CACHE["nc"] = build()
    nc = _NC_CACHE["nc"]
    in_maps = []
    for cid in range(NCORE):
        xs = x[cid * NSEQ:(cid + 1) * NSEQ]
        xTs = np.ascontiguousarray(xs.transpose(0, 2, 1))
        in_maps.append({"xT": xTs, "w_in": w_in, "w_out": w_out, "w_up": w_up, "w_down": w_down,
                        "pvec": pv, "relx": relx, "oh": oh, "ident": ident, "jmat": jmat})
    res = run_bass_kernel_spmd(nc, in_maps, core_ids=list(range(NCORE)))
    outs = []
    for cid in range(NCORE):
        o = np.asarray(res.results[cid]["outT"], np.float32)
        outs.append(o.transpose(0, 2, 1))
    out = np.ascontiguousarray(np.concatenate(outs, axis=0))
    if DEBUG:
        kernel.last_results = res.results
    return out
```

```python
import math
from contextlib import ExitStack

import numpy as np
import concourse.bass as bass
import concourse.mybir as mybir
from concourse.bass_utils import run_bass_kernel_spmd

F32 = mybir.dt.float32
BF16 = mybir.dt.bfloat16
AF = mybir.ActivationFunctionType
ALU = mybir.AluOpType

D = 1024
S = 2048
NSEQ = 2
NCORE = 8
HD = 64
NH = 12
AW = 768
CW = 256
INW = 2816
DFF = 2816
CK = 31
ALPHA = 2.0 ** 0.25
EPS = 1e-5
MASKV = -30000.0
BRANCH = ((1, 1, 16), (4, 4, 4), (16, 16, 1))

DEBUG = False

PCOL = {}
_o = 0
for _n, _w in (("b_in", 22), ("convw", 62), ("conv_b", 2), ("cln_g", 2), ("cln_b", 2), ("cn_g", 2),
               ("an_g", 6), ("ln1_g", 8), ("ln1_b", 8), ("fw", 132), ("fb", 44), ("ln2_g", 8), ("ln2_b", 8)):
    PCOL[_n] = _o
    _o += _w
NPCOL = _o


def _cols(v):
    v = np.asarray(v, np.float32).reshape(-1, 128)
    return v.T


def pack_params(inp):
    pv = np.zeros((128, NPCOL), np.float32)

    def put(name, arr):
        pv[:, PCOL[name]:PCOL[name] + arr.shape[1]] = arr
    put("b_in", _cols(inp["b_in"][0]))
    cw = np.asarray(inp["conv_w"][0], np.float32)
    put("convw", np.concatenate([cw[:, 0:128].T, cw[:, 128:256].T], axis=1))
    put("conv_b", _cols(inp["conv_b"][0]))
    put("cln_g", _cols(inp["conv_ln_g"][0]))
    put("cln_b", _cols(inp["conv_ln_b"][0]))
    put("cn_g", _cols(inp["conv_norm_g"][0]))
    put("an_g", _cols(inp["attn_norm_g"][0]))
    put("ln1_g", _cols(inp["ln1_g"][0]))
    put("ln1_b", _cols(inp["ln1_b"][0]))
    fw = np.asarray(inp["ffn_conv_w"][0], np.float32)
    put("fw", np.concatenate([_cols(fw[j]) for j in range(3)], axis=1))
    put("fb", _cols(inp["ffn_conv_b"][0]))
    put("ln2_g", _cols(inp["ln2_g"][0]))
    put("ln2_b", _cols(inp["ln2_b"][0]))
    return pv


def t5_bucket_np(dist):
    dist = np.asarray(dist, np.int64)
    exact = 16
    d_f = np.maximum(dist, 1).astype(np.float32)
    large = exact + (np.log(d_f / np.float32(exact)) / np.float32(math.log(2048 / exact))
                     * np.float32(32 - exact)).astype(np.int32)
    large = np.minimum(large, 31)
    return np.where(dist < exact, dist, large)


def onehot_const():
    oh = np.zeros((33, 3 * 384), np.float32)
    for b, (d, _, _) in enumerate(BRANCH):
        for i in range(383):
            delta = i - 127
            if 0 <= delta <= 128:
                oh[int(t5_bucket_np(delta * d)), b * 384 + i] = 1.0
            else:
                oh[32, b * 384 + i] = MASKV
    return oh


class Tok:
    __slots__ = ("sem", "sid", "val")

    def __init__(self, sem, sid, val):
        self.sem, self.sid, self.val = sem, sid, val


class Eng:
    def __init__(self, nc, e, name):
        self.e = e
        self.name = name
        self.sem = nc.alloc_semaphore(name="es_" + name)
        self.sid = "E" + name
        self.n = 0
        self.seen = {}

    def wait(self, tok):
        if tok is None:
            return
        if self.seen.get(tok.sid, 0) >= tok.val:
            return
        self.e.wait_ge(tok.sem, tok.val)
        self.seen[tok.sid] = tok.val

    def sig(self, ins):
        self.n += 1
        ins.then_inc(self.sem, 1)
        return Tok(self.sem, self.sid, self.n)

    def last(self):
        return Tok(self.sem, self.sid, self.n) if self.n else None


class Buf:
    def __init__(self, name):
        self.name = name
        self.w = None
        self.r = {}
        self.dsem = None
        self.dn = 0


class K:
    def __init__(self):
        self.nc = nc = bass.Bass("TRN2", target_bir_lowering=False)
        self.PE = Eng(nc, nc.tensor, "pe")
        self.ACT = Eng(nc, nc.scalar, "act")
        self.DVE = Eng(nc, nc.vector, "dve")
        self.POOL = Eng(nc, nc.gpsimd, "pool")
        self.SP = Eng(nc, nc.sync, "sp")
        self.engs = [self.PE, self.ACT, self.DVE, self.POOL, self.SP]
        self.pe_pending = []
        self.bufs = {}
        self.dma_bufs = []
        self.nsem = 5
        self.stopped = False

    def buf(self, name):
        if name not in self.bufs:
            self.bufs[name] = Buf(name)
        return self.bufs[name]

    def deps(self, E, reads, writes):
        for b in reads:
            E.wait(b.w)
        for b in writes:
            assert b not in self.pe_pending, f"write to {b.name} with pending PE reads"
            if b.w is not None and b.w.sid != E.sid:
                E.wait(b.w)
            for t in b.r.values():
                if t.sid != E.sid:
                    E.wait(t)

    def done(self, tok, reads, writes):
        for b in reads:
            b.r[tok.sid] = tok
        for b in writes:
            b.w = tok
            b.r = {}

    def op(self, E, fn, reads=(), writes=()):
        if self.stopped:
            return None
        self.deps(E, reads, writes)
        ins = fn()
        tok = E.sig(ins)
        self.done(tok, reads, writes)
        return tok

    def dma(self, Q, out, in_, reads=(), writes=(), sbuf=None):
        if self.stopped:
            return None
        self.deps(Q, reads, writes)
        ins = Q.e.dma_start(out=out, in_=in_)
        b = sbuf if sbuf is not None else writes[0]
        if b.dsem is None:
            b.dsem = self.nc.alloc_semaphore(name="ds_" + b.name)
            self.nsem += 1
            self.dma_bufs.append(b)
        b.dn += 16
        ins.then_inc(b.dsem, 16)
        tok = Tok(b.dsem, "D" + b.name, b.dn)
        self.done(tok, reads, writes)
        return tok

    def mm(self, out, lhsT, rhs, start, stop, reads=(), wbuf=None, first=False, last=False, sig=False,
           transpose=False):
        PE = self.PE
        if self.stopped:
            return None
        for b in reads:
            PE.wait(b.w)
        if first and wbuf is not None:
            if wbuf.w is not None and wbuf.w.sid != PE.sid:
                PE.wait(wbuf.w)
            for t in wbuf.r.values():
                if t.sid != PE.sid:
                    PE.wait(t)
        if transpose:
            ins = self.nc.tensor.transpose(out, lhsT, rhs)
        else:
            ins = self.nc.tensor.matmul(out, lhsT, rhs, start=start, stop=stop)
        for b in reads:
            if b not in self.pe_pending:
                self.pe_pending.append(b)
        if last or sig:
            tok = PE.sig(ins)
            for b in self.pe_pending:
                b.r[tok.sid] = tok
            self.pe_pending = []
            if last and wbuf is not None:
                wbuf.w = tok
                wbuf.r = {}
            return tok
        return None

    def barrier(self):
        if self.stopped:
            return
        assert not self.pe_pending
        toks = [e.last() for e in self.engs]
        for b in self.dma_bufs:
            if b.dn:
                toks.append(Tok(b.dsem, "D" + b.name, b.dn))
        for e in self.engs:
            for t in toks:
                if t is not None and t.sid != e.sid:
                    e.wait(t)


def tok_ap(base, d, r, n, nblk=1):
    if d == 1:
        return slice(128 * n, 128 * (n + nblk))
    if d == 4:
        return slice(512 * n + r, 512 * (n + nblk), 4)
    return slice(r, S, 16)


class _Stop(Exception):
    pass


def build(stop=None, nseq=NSEQ):
    k = K()

    def stage(name):
        if stop == name and not k.stopped:
            k.barrier()
            k.stopped = True
    nc = k.nc
    PE, ACT, DVE, POOL, SP = k.PE, k.ACT, k.DVE, k.POOL, k.SP
    es = ExitStack()

    def dram(name, shape, dt, kind):
        return nc.dram_tensor(name, list(shape), dt, kind=kind).ap()

    xT = dram("xT", [NSEQ, D, S], F32, "ExternalInput")
    w_in = dram("w_in", [D, INW], F32, "ExternalInput")
    w_out = dram("w_out", [D, D], F32, "ExternalInput")
    w_up = dram("w_up", [D, 2 * DFF], F32, "ExternalInput")
    w_down = dram("w_down", [DFF, D], F32, "ExternalInput")
    pvec_d = dram("pvec", [128, NPCOL], F32, "ExternalInput")
    relx_d = dram("relx", [33, 12], F32, "ExternalInput")
    oh_d = dram("oh", [33, 3 * 384], F32, "ExternalInput")
    ident_d = dram("ident", [128, 128], F32, "ExternalInput")
    jmat_d = dram("jmat", [128, 128], F32, "ExternalInput")
    outT = dram("outT", [NSEQ, D, S], F32, "ExternalOutput")
    cdram = dram("cdram", [12, 3, 384], F32, "Internal")
    dnd = dram("dnd", [2, S], F32, "Internal")
    dnd2 = dram("dnd2", [2, S], F32, "Internal")
    if DEBUG:
        dbg_attn = dram("dbg_attn", [NSEQ, 128, 6, S], BF16, "ExternalOutput")
        dbg_un = dram("dbg_un", [NSEQ, 128, 2, S], BF16, "ExternalOutput")
        dbg_x1 = dram("dbg_x1", [NSEQ, 128, 8, S], BF16, "ExternalOutput")
        dbg_tb = dram("dbg_tb", [128, 3, 12, 256], F32, "ExternalOutput")
        dbg_act = dram("dbg_act", [128, 22, 1024], BF16, "ExternalOutput")

    w_in_v = w_in.rearrange("(kc p) n -> p kc n", p=128)
    w_out_v = w_out.rearrange("(kc p) n -> p kc n", p=128)
    w_up_v = w_up.rearrange("(kc p) n -> p kc n", p=128)
    w_down_v = w_down.rearrange("(kc p) n -> p kc n", p=128)

    _cnt = [0]

    def sb(name, shape, dt, stack=es):
        _cnt[0] += 1
        return stack.enter_context(nc.sbuf_tensor(f"{name}_{_cnt[0]}", list(shape), dt))

    PSA = es.enter_context(nc.psum_tensor("psA", [128, 1024], F32))
    PSBt = es.enter_context(nc.psum_tensor("psB", [128, 1024], F32))
    PS = [PSA[:, 0:512], PSA[:, 512:1024], PSBt[:, 0:512], PSBt[:, 512:1024]]
    PS += [es.enter_context(nc.psum_tensor(f"ps{i}", [128, 512], F32))[:, :] for i in range(4, 8)]
    PS2 = [PSA, PSBt]
    PSB = [k.buf(f"ps{i}") for i in range(8)]

    pvec = sb("pvec_sb", [128, NPCOL], F32)
    pder = sb("pder", [128, 64], F32)
    identb = sb("identb", [128, 128], BF16)
    onesb = sb("onesb", [128, 128], BF16)
    jmat = sb("jmat_sb", [128, 128], F32)
    attnT = sb("attnT", [128, 6, S], BF16)
    uN = sb("uN", [128, 2, S], BF16)
    halo = sb("halo", [128, 44, 2], F32)
    B_pvec, B_pder, B_ident, B_ones, B_j = (k.buf(n) for n in ("pvec", "pder", "ident", "ones", "jmat"))
    B_attn = [k.buf(f"attnT{c}") for c in range(6)]
    B_uN = k.buf("uN")
    B_halo = k.buf("halo")

    def pc(name, j=0, n=1):
        o = PCOL[name] + j
        return pvec[:, o:o + n]

    k.dma(SP, pvec[:], pvec_d, writes=[B_pvec])
    k.dma(POOL, identb[:], ident_d, writes=[B_ident])
    k.dma(SP, jmat[:], jmat_d, writes=[B_j])
    k.op(DVE, lambda: nc.vector.memset(onesb[:], 1.0), writes=[B_ones])
    k.op(DVE, lambda: nc.vector.tensor_scalar(out=pder[:, 0:62], in0=pc("convw", 0, 62), scalar1=0.5, scalar2=None,
                                              op0=ALU.mult), reads=[B_pvec], writes=[B_pder])
    k.op(DVE, lambda: nc.vector.tensor_scalar(out=pder[:, 62:64], in0=pc("b_in", 20, 2), scalar1=0.5, scalar2=None,
                                              op0=ALU.mult), reads=[B_pvec], writes=[B_pder])

    with ExitStack() as s0:
        relx = sb("relx_sb", [33, 12], F32, s0)
        ohs = sb("oh_sb", [33, 3 * 384], F32, s0)
        csb = sb("csb", [12, 3, 384], F32, s0)
        B_relx, B_oh, B_csb, B_cd = k.buf("relx"), k.buf("oh"), k.buf("csb"), k.buf("cdram")
        k.dma(SP, relx[:], relx_d, writes=[B_relx])
        k.dma(SP, ohs[:], oh_d, writes=[B_oh])
        for b in range(3):
            k.mm(PS[b][0:12, 0:384], relx[:, :], ohs[:, b * 384:(b + 1) * 384], True, True,
                 reads=[B_relx, B_oh], wbuf=PSB[b], first=True, last=True)
            k.op(DVE, lambda b=b: nc.vector.tensor_copy(out=csb[:, b, :], in_=PS[b][0:12, 0:384]),
                 reads=[PSB[b]], writes=[B_csb])
        k.dma(SP, cdram, csb[:], reads=[B_csb], writes=[B_cd])
        k.barrier()

    def ln_stats_chunk(R, B_Rc, tsl, oc, stat_ps, tmp):
        zb, zsq, B_zb, B_zsq = tmp["zb"], tmp["zsq"], tmp["B_zb"], tmp["B_zsq"]
        p1, p2 = stat_ps
        i = tmp["rot"][0] % len(zb)
        tmp["rot"][0] += 1
        k.op(DVE, lambda: nc.vector.tensor_copy(out=zb[i][:], in_=R[:, oc, tsl]), reads=[B_Rc[oc]],
             writes=[B_zb[i]])
        k.op(ACT, lambda: nc.scalar.activation(out=zsq[i][:], in_=R[:, oc, tsl], func=AF.Square),
             reads=[B_Rc[oc]], writes=[B_zsq[i]])

        def mms():
            k.mm(PS[p1][:, :], onesb[:, :], zb[i][:], oc == 0, oc == 7, reads=[B_ones, B_zb[i]],
                 wbuf=PSB[p1], first=(oc == 0), last=(oc == 7), sig=True)
            k.mm(PS[p2][:, :], onesb[:, :], zsq[i][:], oc == 0, oc == 7, reads=[B_ones, B_zsq[i]],
                 wbuf=PSB[p2], first=(oc == 0), last=(oc == 7), sig=True)
        return mms

    def layer_norm_tile(R, B_Rc, tsl, gname, bname, stat_ps, tmp, outb=None, B_outb=None, skip_stats=False,
                        fin=None):
        zb, zsq, B_zb, B_zsq = tmp["zb"], tmp["zsq"], tmp["B_zb"], tmp["B_zsq"]
        m, q, B_m, B_q = tmp["m"], tmp["q"], tmp["B_m"], tmp["B_q"]
        p1, p2 = stat_ps
        nb = len(zb)
        for oc in range(0 if skip_stats else 8):
            i = tmp["rot"][0] % nb
            tmp["rot"][0] += 1
            k.op(DVE, lambda: nc.vector.tensor_copy(out=zb[i][:], in_=R[:, oc, tsl]), reads=[B_Rc[oc]],
                 writes=[B_zb[i]])
            k.op(ACT, lambda: nc.scalar.activation(out=zsq[i][:], in_=R[:, oc, tsl], func=AF.Square),
                 reads=[B_Rc[oc]], writes=[B_zsq[i]])
            k.mm(PS[p1][:, :], onesb[:, :], zb[i][:], oc == 0, oc == 7, reads=[B_ones, B_zb[i]],
                 wbuf=PSB[p1], first=(oc == 0), last=(oc == 7), sig=True)
            k.mm(PS[p2][:, :], onesb[:, :], zsq[i][:], oc == 0, oc == 7, reads=[B_ones, B_zsq[i]],
                 wbuf=PSB[p2], first=(oc == 0), last=(oc == 7), sig=True)
        k.op(DVE, lambda: nc.vector.tensor_scalar(out=m[:], in0=PS[p1][:, :], scalar1=1.0 / D, scalar2=None,
                                                  op0=ALU.mult), reads=[PSB[p1]], writes=[B_m])
        k.op(DVE, lambda: nc.vector.tensor_tensor(out=q[:], in0=m[:], in1=m[:], op=ALU.mult),
             reads=[B_m], writes=[B_q])
        k.op(DVE, lambda: nc.vector.scalar_tensor_tensor(out=q[:], in0=PS[p2][:, :], scalar=1.0 / D, in1=q[:],
                                                         op0=ALU.mult, op1=ALU.subtract),
             reads=[PSB[p2], B_q], writes=[B_q])
        k.op(ACT, lambda: nc.scalar.activation(out=q[:], in_=q[:], func=AF.Ln, bias=EPS, scale=1.0),
             reads=[B_q], writes=[B_q])
        k.op(ACT, lambda: nc.scalar.activation(out=q[:], in_=q[:], func=AF.Exp, scale=-0.5),
             reads=[B_q], writes=[B_q])
        k.op(DVE, lambda: nc.vector.tensor_tensor(out=m[:], in0=m[:], in1=q[:], op=ALU.mult),
             reads=[B_m, B_q], writes=[B_m])
        def ln_mult(oc):
            k.op(DVE, lambda: nc.vector.tensor_tensor(out=R[:, oc, tsl], in0=R[:, oc, tsl], in1=q[:], op=ALU.mult),
                 reads=[B_Rc[oc], B_q], writes=[B_Rc[oc]])

        def ln_rest(oc):
            k.op(DVE, lambda: nc.vector.tensor_tensor(out=R[:, oc, tsl], in0=R[:, oc, tsl], in1=m[:],
                                                      op=ALU.subtract), reads=[B_Rc[oc], B_m], writes=[B_Rc[oc]])
            if fin is not None:
                k.op(ACT, lambda: nc.scalar.activation(out=fin[oc][0], in_=R[:, oc, tsl], func=AF.Identity,
                                                       bias=pc(bname, oc), scale=pc(gname, oc)),
                     reads=[B_Rc[oc], B_pvec], writes=fin[oc][1])
                return
            k.op(ACT, lambda: nc.scalar.activation(out=R[:, oc, tsl], in_=R[:, oc, tsl], func=AF.Identity,
                                                   bias=pc(bname, oc), scale=pc(gname, oc)),
                 reads=[B_Rc[oc], B_pvec], writes=[B_Rc[oc]])
            if outb is not None:
                k.op(ACT, lambda: nc.scalar.copy(out=outb[:, oc, tsl], in_=R[:, oc, tsl]),
                     reads=[B_Rc[oc]], writes=[B_outb[oc]])

        ln_mult(0)
        for oc in range(1, 8):
            ln_mult(oc)
            ln_rest(oc - 1)
        ln_rest(7)

    stage("setup")
    try:
      for s in range(nseq):
          with ExitStack() as sAB:
              Tb = sb("Tb", [128, 3, 12, 256], F32, sAB)
              xTb = sb("xTb", [128, 8, S], BF16, sAB)
              B_Tb = k.buf("Tb")
              B_x = [k.buf(f"xTb{i}") for i in range(8)]
              for kc in range(8):
                  k.dma(POOL, xTb[:, kc, :], xT[s, kc * 128:(kc + 1) * 128, :], writes=[B_x[kc]])

              with ExitStack() as sC:
                  convD = sb("convD", [128, 2, CK, 128], BF16, sC)
                  wc = sb("wc", [128, 8, 512], BF16, sC)
                  uB = sb("uB", [128, 2, S + 32], BF16, sC)
                  cv = sb("cv", [128, 2, S], F32, sC)
                  t1 = [sb(f"ct1_{i}", [128, 512], F32, sC) for i in range(2)]
                  t2 = [sb(f"ct2_{i}", [128, 512], F32, sC) for i in range(2)]
                  cb = [sb(f"cb_{i}", [128, 512], BF16, sC) for i in range(2)]
                  cq = [sb(f"cq_{i}", [128, 512], BF16, sC) for i in range(2)]
                  cm = sb("cm", [128, S], F32, sC)
                  cr = sb("cr", [128, S], F32, sC)
                  B_cD, B_wc, B_uB, B_cv = k.buf("convD"), k.buf("wc"), k.buf("uB"), k.buf("cv")
                  B_cDc = [k.buf("convD0"), k.buf("convD1")]
                  B_t1 = [k.buf(f"ct1_{i}") for i in range(2)]
                  B_t2 = [k.buf(f"ct2_{i}") for i in range(2)]
                  B_cb = [k.buf(f"cb_{i}") for i in range(2)]
                  B_cq = [k.buf(f"cq_{i}") for i in range(2)]
                  B_cm, B_cr = k.buf("cm"), k.buf("cr")

                  for kc in range(0, 8, 2):
                      k.dma(POOL, wc[:, kc:kc + 2, :], w_in_v[:, kc:kc + 2, 2304:2816], writes=[B_wc])
                  for cc in range(2):
                      for j in range(CK):
                          if cc == 0:
                              k.op(DVE, lambda: nc.vector.tensor_scalar(out=convD[:, cc, j, :], in0=identb[:, :],
                                                                        scalar1=pder[:, cc * CK + j:cc * CK + j + 1],
                                                                        scalar2=None, op0=ALU.mult),
                                   reads=[B_ident, B_pder], writes=[B_cDc[cc]])
                          else:
                              k.op(ACT, lambda: nc.scalar.activation(out=convD[:, cc, j, :], in_=identb[:, :],
                                                                     func=AF.Copy,
                                                                     scale=pder[:, cc * CK + j:cc * CK + j + 1]),
                                   reads=[B_ident, B_pder], writes=[B_cDc[cc]])
                  k.op(DVE, lambda: nc.vector.memset(uB[:, :, 0:30], 0.0), writes=[B_uB])
                  it = 0
                  for cc in range(2):
                      for tt in range(4):
                          tsl = slice(tt * 512, (tt + 1) * 512)
                          i = it % 2
                          pa, pg = 2 * i, 2 * i + 1
                          for kc in range(8):
                              k.mm(PS[pa][:, :], wc[:, kc, cc * 128:(cc + 1) * 128], xTb[:, kc, tsl], kc == 0, kc == 7,
                                   reads=[B_wc, B_x[kc]], wbuf=PSB[pa], first=(kc == 0), last=(kc == 7))
                          for kc in range(8):
                              k.mm(PS[pg][:, :], wc[:, kc, 256 + cc * 128:256 + (cc + 1) * 128], xTb[:, kc, tsl],
                                   kc == 0, kc == 7, reads=[B_wc, B_x[kc]], wbuf=PSB[pg], first=(kc == 0),
                                   last=(kc == 7))
                          k.op(ACT, lambda: nc.scalar.activation(out=t1[i][:], in_=PS[pg][:, :], func=AF.Tanh,
                                                                 bias=pder[:, 62 + cc:63 + cc], scale=0.5),
                               reads=[PSB[pg], B_pder], writes=[B_t1[i]])
                          k.op(ACT, lambda: nc.scalar.activation(out=t2[i][:], in_=PS[pa][:, :], func=AF.Identity,
                                                                 bias=pc("b_in", 18 + cc), scale=1.0),
                               reads=[PSB[pa], B_pvec], writes=[B_t2[i]])
                          k.op(DVE, lambda: nc.vector.scalar_tensor_tensor(
                              out=uB[:, cc, 30 + tt * 512:30 + (tt + 1) * 512], in0=t1[i][:], scalar=1.0,
                              in1=t2[i][:], op0=ALU.add, op1=ALU.mult),
                              reads=[B_t1[i], B_t2[i]], writes=[B_uB])
                          it += 1
                  for tt in range(4):
                      tsl = slice(tt * 512, (tt + 1) * 512)
                      for cc in range(2):
                          pb = (tt * 2 + cc) % 2
                          for j in range(CK):
                              k.mm(PS[pb][:, :], convD[:, cc, j, :], uB[:, cc, tt * 512 + j:tt * 512 + j + 512],
                                   j == 0, j == CK - 1, reads=[B_cDc[cc], B_uB], wbuf=PSB[pb], first=(j == 0),
                                   last=(j == CK - 1))
                          k.op(ACT, lambda: nc.scalar.activation(out=cv[:, cc, tsl], in_=PS[pb][:, :],
                                                                 func=AF.Identity, bias=pc("conv_b", cc), scale=1.0),
                               reads=[PSB[pb], B_pvec], writes=[B_cv])
                          k.op(ACT, lambda: nc.scalar.activation(out=cq[cc][:], in_=PS[pb][:, :], func=AF.Square,
                                                                 bias=pc("conv_b", cc), scale=1.0),
                               reads=[PSB[pb], B_pvec], writes=[B_cq[cc]])
                          k.op(DVE, lambda: nc.vector.tensor_copy(out=cb[cc][:], in_=cv[:, cc, tsl]),
                               reads=[B_cv], writes=[B_cb[cc]])
                      for cc in range(2):
                          k.mm(PS[2][:, :], onesb[:, :], cb[cc][:], cc == 0, cc == 1, reads=[B_ones, B_cb[cc]],
                               wbuf=PSB[2], first=(cc == 0), last=(cc == 1), sig=True)
                          k.mm(PS[3][:, :], onesb[:, :], cq[cc][:], cc == 0, cc == 1, reads=[B_ones, B_cq[cc]],
                               wbuf=PSB[3], first=(cc == 0), last=(cc == 1), sig=True)
                      k.op(DVE, lambda: nc.vector.tensor_scalar(out=cm[:, tsl], in0=PS[2][:, :], scalar1=1.0 / CW,
                                                                scalar2=None, op0=ALU.mult),
                           reads=[PSB[2]], writes=[B_cm])
                      k.op(DVE, lambda: nc.vector.tensor_tensor(out=cr[:, tsl], in0=cm[:, tsl], in1=cm[:, tsl],
                                                                op=ALU.mult), reads=[B_cm], writes=[B_cr])
                      k.op(DVE, lambda: nc.vector.scalar_tensor_tensor(out=cr[:, tsl], in0=PS[3][:, :],
                                                                       scalar=1.0 / CW, in1=cr[:, tsl],
                                                                       op0=ALU.mult, op1=ALU.subtract),
                           reads=[PSB[3], B_cr], writes=[B_cr])
                  with ExitStack() as sT:
                      Hk = [sb(f"Hk{i}", [128, 4, 2, 128], F32, sT) for i in range(2)]
                      B_Hk = [k.buf(f"Hk{i}") for i in range(2)]
                      it = 0
                      for b in range(3):
                          for h0 in range(0, 12, 4):
                              i = it % 2
                              if b < 2:
                                  src = bass.AP(tensor=cdram.tensor, offset=h0 * 3 * 384 + b * 384,
                                                ap=[[1, 128], [3 * 384, 4], [128, 2], [1, 128]])
                                  k.dma(SP, Hk[i][:], src, reads=[k.buf("cdram")], writes=[B_Hk[i]])
                              else:
                                  for part in range(2):
                                      src = bass.AP(tensor=cdram.tensor, offset=h0 * 3 * 384 + b * 384,
                                                    ap=[[1, 128], [3 * 384, 4], [1, 128]])
                                      k.dma(SP, Hk[i][:, :, part, :], src, reads=[k.buf("cdram")], writes=[B_Hk[i]])
                              for half in range(2):
                                  pb = (it * 2 + half) % 2
                                  k.mm(PS[pb][:, :], jmat[:, :],
                                       Hk[i][:, 2 * half:2 * half + 2, :, :].rearrange("p a b c -> p (a b c)"),
                                       True, True, reads=[B_j, B_Hk[i]], wbuf=PSB[pb], first=True, last=True)
                                  k.op(DVE, lambda: nc.vector.tensor_copy(
                                      out=Tb[:, b, h0 + 2 * half:h0 + 2 * half + 2, :].rearrange("p a b -> p (a b)"),
                                      in_=PS[pb][:, :]), reads=[PSB[pb]], writes=[B_Tb])
                              it += 1
                  if DEBUG and s == 0:
                      k.dma(SP, dbg_tb, Tb[:], reads=[B_Tb], writes=[k.buf("dbg_tb")])
                  stage("tables")

                  k.op(ACT, lambda: nc.scalar.activation(out=cr[:], in_=cr[:], func=AF.Ln, bias=EPS, scale=1.0),
                       reads=[B_cr], writes=[B_cr])
                  k.op(ACT, lambda: nc.scalar.activation(out=cr[:], in_=cr[:], func=AF.Exp, scale=-0.5),
                       reads=[B_cr], writes=[B_cr])
                  for cc in range(2):
                      k.op(DVE, lambda: nc.vector.tensor_tensor(out=cv[:, cc, :], in0=cv[:, cc, :], in1=cm[:],
                                                                op=ALU.subtract), reads=[B_cv, B_cm], writes=[B_cv])
                      k.op(DVE, lambda: nc.vector.tensor_tensor(out=cv[:, cc, :], in0=cv[:, cc, :], in1=cr[:],
                                                                op=ALU.mult), reads=[B_cv, B_cr], writes=[B_cv])
                  for cc in range(2):
                      k.op(ACT, lambda: nc.scalar.activation(out=cv[:, cc, :], in_=cv[:, cc, :], func=AF.Silu,
                                                             bias=pc("cln_b", cc), scale=pc("cln_g", cc)),
                           reads=[B_cv, B_pvec], writes=[B_cv])
                  for tt in range(4):
                      tsl = slice(tt * 512, (tt + 1) * 512)
                      for cc in range(2):
                          k.op(ACT, lambda: nc.scalar.activation(out=cq[cc][:], in_=cv[:, cc, tsl], func=AF.Square),
                               reads=[B_cv], writes=[B_cq[cc]])
                          k.mm(PS[3][:, :], onesb[:, :], cq[cc][:], cc == 0, cc == 1, reads=[B_ones, B_cq[cc]],
                               wbuf=PSB[3], first=(cc == 0), last=(cc == 1), sig=True)
                      k.op(ACT, lambda: nc.scalar.activation(out=cr[:, tsl], in_=PS[3][:, :], func=AF.Ln, bias=EPS,
                                                             scale=1.0 / CW), reads=[PSB[3]], writes=[B_cr])
                  k.op(ACT, lambda: nc.scalar.activation(out=cr[:], in_=cr[:], func=AF.Exp, scale=-0.5),
                       reads=[B_cr], writes=[B_cr])
                  for cc in range(2):
                      k.op(DVE, lambda: nc.vector.scalar_tensor_tensor(out=uN[:, cc, :], in0=cv[:, cc, :],
                                                                       scalar=pc("cn_g", cc), in1=cr[:],
                                                                       op0=ALU.mult, op1=ALU.mult),
                           reads=[B_cv, B_cr, B_pvec], writes=[B_uN])
                  k.barrier()
              if DEBUG:
                  k.dma(SP, dbg_un[s], uN[:], reads=[B_uN], writes=[k.buf("dbg_un")])
              stage("conv")

              with ExitStack() as sA:
                  NST = 4
                  wq = [sb(f"wq{i}", [128, 8, 384], BF16, sA) for i in range(2)]
                  qz = [sb(f"qz{i}", [128, S], BF16, sA) for i in range(2)]
                  kv = sb("kv", [128, 2, S], BF16, sA)
                  Vtok = sb("Vtok", [128, 96, 65], BF16, sA)
                  acc = [sb(f"acc{i}", [128, S], F32, sA) for i in range(2)]
                  denb = sb("denb", [128, S], F32, sA)
                  dq = sb("dq", [128, 2, 16], F32, sA)
                  stmp = [sb(f"stmp{i}", [128, 2, 256], F32, sA) for i in range(NST)]
                  PT = [sb(f"PT{i}", [128, 2, 256], BF16, sA) for i in range(NST)]
                  mtmp = [sb(f"mtmp{i}", [128, 512], F32, sA) for i in range(4)]
                  sq = [sb(f"asq{i}", [128, 512], BF16, sA) for i in range(2)]
                  rra = sb("rra", [128, S], F32, sA)
                  B_wq = [k.buf(f"wq{i}") for i in range(2)]
                  B_qz = [k.buf(f"qz{i}") for i in range(2)]
                  B_kv = [k.buf(f"kv{i}") for i in range(2)]
                  B_V = k.buf("Vtok")
                  B_Vg = [[k.buf(f"Vtok{b_}_{g_}") for g_ in range(4)] for b_ in range(3)]
                  B_acc = [k.buf(f"acc{i}") for i in range(2)]
                  B_denb = k.buf("denb")
                  B_dnd = k.buf("dnd")
                  B_dnd2 = k.buf("dnd2")
                  B_dq = k.buf("dq")
                  B_st = [k.buf(f"stmp{i}") for i in range(NST)]
                  B_PT = [k.buf(f"PT{i}") for i in range(NST)]
                  B_mt = [k.buf(f"mtmp{i}") for i in range(4)]
                  B_sq = [k.buf(f"asq{i}") for i in range(2)]
                  B_rra = k.buf("rra")

                  k.op(DVE, lambda: nc.vector.memset(Vtok[:, :, 64:65], 1.0), writes=[B_V])
                  k.op(POOL, lambda: nc.gpsimd.memset(qz[0][64:128, :], 0.0), writes=[B_qz[0]])
                  k.op(POOL, lambda: nc.gpsimd.memset(qz[1][0:64, :], 0.0), writes=[B_qz[1]])

                  def load_wq(c):
                      i = c % 2
                      for j in range(3):
                          k.dma(POOL, wq[i][:, :, j * 128:(j + 1) * 128],
                                w_in_v[:, :, j * AW + c * 128:j * AW + (c + 1) * 128], writes=[B_wq[i]])

                  evac_flip = [0]
                  mt_i = [0]

                  def project(c):
                      wi = c % 2
                      for j in range(3):
                          for tt in range(4):
                              tsl = slice(tt * 512, (tt + 1) * 512)
                              pb = (j * 4 + tt) % 4
                              for kc in range(8):
                                  k.mm(PS[pb][:, :], wq[wi][:, kc, j * 128:(j + 1) * 128], xTb[:, kc, tsl],
                                       kc == 0, kc == 7, reads=[B_wq[wi], B_x[kc]], wbuf=PSB[pb],
                                       first=(kc == 0), last=(kc == 7))
                              if j == 0:
                                  for hh in range(2):
                                      ps_ = slice(hh * 64, (hh + 1) * 64)
                                      k.op(ACT, lambda: nc.scalar.activation(
                                          out=qz[hh][ps_, tsl], in_=PS[pb][ps_, :], func=AF.Identity,
                                          bias=pvec[ps_, PCOL["b_in"] + c:PCOL["b_in"] + c + 1], scale=1.0),
                                          reads=[PSB[pb], B_pvec], writes=[B_qz[hh]])
                              else:
                                  k.op(ACT, lambda: nc.scalar.activation(out=kv[:, j - 1, tsl], in_=PS[pb][:, :],
                                                                         func=AF.Identity,
                                                                         bias=pc("b_in", j * 6 + c), scale=1.0),
                                       reads=[PSB[pb], B_pvec], writes=[B_kv[j - 1]])

                  def vtok(c):
                      for b, (d, nres, nblk) in enumerate(BRANCH):
                          for g in range(4):
                              pb = g % 4
                              pv = PS[pb][:, 0:256].bitcast(BF16)
                              for jj in range(4):
                                  blk = 4 * g + jj
                                  if d == 1:
                                      tsl = tok_ap(None, 1, 0, blk)
                                  elif d == 4:
                                      tsl = tok_ap(None, 4, blk // 4, blk % 4)
                                  else:
                                      tsl = tok_ap(None, 16, blk, 0)
                                  k.mm(pv[:, jj * 128:(jj + 1) * 128], kv[:, 1, tsl], identb[:, :], True, True,
                                       reads=[B_kv[1], B_ident], wbuf=PSB[pb], first=(jj == 0), last=(jj == 3),
                                       transpose=True)
                              src = pv.rearrange("p (a e) -> p a e", a=8)
                              dst = Vtok[:, (b * 16 + 4 * g) * 2:(b * 16 + 4 * g + 4) * 2, 0:64]
                              if evac_flip[0] % 2 == 0:
                                  k.op(DVE, lambda: nc.vector.tensor_copy(out=dst, in_=src), reads=[PSB[pb], B_V],
                                       writes=[B_Vg[b][g]])
                              else:
                                  k.op(ACT, lambda: nc.scalar.copy(out=dst, in_=src), reads=[PSB[pb], B_V],
                                       writes=[B_Vg[b][g]])
                              evac_flip[0] += 1

                  def post(c):
                      for hh in range(2):
                          k.dma(SP, dnd[hh:hh + 1, :], acc[hh][64:65, :], reads=[B_acc[hh]], writes=[B_dnd])
                      k.dma(SP, acc[0][64:128, :], acc[1][0:64, :], reads=[B_acc[1], B_dnd], writes=[B_acc[0]])
                      src = bass.AP(tensor=dnd.tensor, offset=0, ap=[[16, 128], [S, 2], [1, 16]])
                      k.dma(SP, dq[:], src, reads=[B_dnd], writes=[B_dq])
                      k.op(DVE, lambda: nc.vector.reciprocal(out=dq[:], in_=dq[:]), reads=[B_dq], writes=[B_dq])
                      dst = bass.AP(tensor=dnd2.tensor, offset=0, ap=[[16, 128], [S, 2], [1, 16]])
                      k.dma(SP, dst, dq[:], reads=[B_dq], writes=[B_dnd2])
                      for hh in range(2):
                          src = bass.AP(tensor=dnd2.tensor, offset=hh * S, ap=[[0, 64], [1, S]])
                          k.dma(SP, denb[hh * 64:(hh + 1) * 64, :], src, reads=[B_dnd2], writes=[B_denb])

                  def finalize(c):
                      k.op(DVE, lambda: nc.vector.tensor_tensor(out=attnT[:, c, :], in0=acc[0][:], in1=denb[:],
                                                                op=ALU.mult),
                           reads=[B_acc[0], B_denb], writes=[B_attn[c]])

                  def steps_for(c):
                      pend = []
                      gs = [0]

                      def flush(upto=None, obs=None):
                          keep = []
                          for item in pend:
                              at, ob_l, fn = item
                              if (upto is None and obs is None) or (upto is not None and at <= upto) or \
                                      (obs is not None and any(o in obs for o in ob_l)):
                                  fn()
                              else:
                                  keep.append(item)
                          pend[:] = keep

                      allsteps = []
                      for b, (d, nres, nblk) in enumerate(BRANCH):
                          if d == 16:
                              st_ = [[(r0, 0, 0, 128), (r0 + 1, 0, 128, 128)] for r0 in range(0, 16, 2)]
                          else:
                              st_ = [[(r, n, 0, 256 if n + 1 < nblk else 128)] for r in range(nres)
                                     for n in range(nblk)]
                          allsteps += [(b, d, subs) for subs in st_]

                      def group_of(d, r, n):
                          if d == 1:
                              return n // 4, n % 4
                          if d == 4:
                              return r, n
                          return r // 4, r % 4

                      def qk(si):
                          cur = si % NST
                          b, d, subs = allsteps[si]
                          nmm = 2 * len(subs)
                          i_ = 0
                          for hh in range(2):
                              for (r, n, co, N) in subs:
                                  k.mm(PS[cur][:, hh * 256 + co:hh * 256 + co + N],
                                       kv[:, 0, tok_ap(None, d, r, n)],
                                       qz[hh][:, tok_ap(None, d, r, n, 2 if N == 256 else 1)], True, True,
                                       reads=[B_kv[0], B_qz[hh]], wbuf=PSB[cur], first=(i_ == 0),
                                       last=(i_ == nmm - 1))
                                  i_ += 1

                      for si in range(min(NST - 1, len(allsteps))):
                          qk(si)
                      for si, (b, d, subs) in enumerate(allsteps):
                          cur = si % NST
                          NT = max(co + N for (_, _, co, N) in subs)
                          if si + NST - 1 < len(allsteps):
                              qk(si + NST - 1)
                          sv = PS[cur][:, :].rearrange("p (h n) -> p h n", h=2)[:, :, 0:NT]
                          k.op(DVE, lambda: nc.vector.scalar_tensor_tensor(
                              out=stmp[cur][:, :, 0:NT], in0=sv, scalar=0.125,
                              in1=Tb[:, b, 2 * c:2 * c + 2, 0:NT], op0=ALU.mult, op1=ALU.add),
                              reads=[PSB[cur], B_Tb], writes=[B_st[cur]])
                          k.op(ACT, lambda: nc.scalar.activation(out=PT[cur][:, :, 0:NT],
                                                                 in_=stmp[cur][:, :, 0:NT], func=AF.Exp),
                               reads=[B_st[cur]], writes=[B_PT[cur]])
                          batch = []
                          closing = []
                          for hh in range(2):
                              for (r, n, co, N) in subs:
                                  blk = n if d == 1 else (r * 4 + n if d == 4 else r)
                                  g, slot = group_of(d, r, n)
                                  ob = 4 + 2 * hh + (g % 2)
                                  vaug = Vtok[:, (b * 16 + blk) * 2 + hh, :]
                                  batch.append((PS[ob][0:65, slot * 128:(slot + 1) * 128], vaug,
                                                PT[cur][:, hh, co:co + 128], n == 0, True, ob,
                                                (slot == 0 and n == 0)))
                                  if slot == 3:
                                      closing.append((hh, g, ob))
                                  if N == 256:
                                      g2, slot2 = group_of(d, r, n + 1)
                                      ob2 = 4 + 2 * hh + (g2 % 2)
                                      batch.append((PS[ob2][0:65, slot2 * 128:(slot2 + 1) * 128], vaug,
                                                    PT[cur][:, hh, co + 128:co + 256], True, False, ob2,
                                                    (slot2 == 0)))
                          vb_ = []
                          for (r, n, co, N) in subs:
                              blk = n if d == 1 else (r * 4 + n if d == 4 else r)
                              if B_Vg[b][blk // 4] not in vb_:
                                  vb_.append(B_Vg[b][blk // 4])
                          opening = [ob_ for (_, _, _, _, _, ob_, fi_) in batch if fi_]
                          if opening:
                              flush(obs=opening)
                          for bi_, (o_, l_, r_, st_, sp_, ob_, fi_) in enumerate(batch):
                              k.mm(o_, l_, r_, st_, sp_, reads=vb_ + [B_PT[cur]], wbuf=PSB[ob_], first=fi_,
                                   last=False, sig=(bi_ == len(batch) - 1))
                          if closing and not k.stopped:
                              tokc = PE.last()
                              for (hh, g, ob) in closing:
                                  PSB[ob].w = tokc
                                  PSB[ob].r = {}
                                  PSB[ob].consumed = False
                                  if d == 1:
                                      a_ap = acc[hh][0:65, 512 * g:512 * (g + 1)]
                                      p_ap = PS[ob][0:65, :]
                                  elif d == 4:
                                      a_ap = acc[hh][0:65, g:S:4]
                                      p_ap = PS[ob][0:65, :]
                                  else:
                                      a_ap = acc[hh][0:65, :].rearrange("p (i r) -> p r i", r=16)[:, 4 * g:4 * g + 4, :]
                                      p_ap = PS[ob][0:65, :].rearrange("p (s i) -> p s i", s=4)

                                  def merge(a_ap=a_ap, p_ap=p_ap, ob=ob, hh=hh, b=b, d=d):
                                      if b == 0:
                                          k.op(ACT, lambda: nc.scalar.copy(out=a_ap, in_=p_ap), reads=[PSB[ob]],
                                               writes=[B_acc[hh]])
                                      else:
                                          mi = mt_i[0] % 4
                                          mt_i[0] += 1
                                          m_ap = mtmp[mi][0:65, :]
                                          if d == 16:
                                              m_ap = m_ap.rearrange("p (s i) -> p s i", s=4)
                                          k.op(ACT, lambda: nc.scalar.copy(out=m_ap, in_=p_ap), reads=[PSB[ob]],
                                               writes=[B_mt[mi]])
                                          k.op(POOL, lambda: nc.gpsimd.tensor_tensor(out=a_ap, in0=m_ap,
                                                                                    in1=a_ap, op=ALU.add),
                                               reads=[B_mt[mi], B_acc[hh]], writes=[B_acc[hh]])
                                  pend.append((gs[0] + 2, [ob], merge))
                          flush(upto=gs[0])
                          gs[0] += 1
                      flush()

                  load_wq(0)
                  for c in range(6):
                      if c + 1 < 6:
                          load_wq(c + 1)
                      project(c)
                      vtok(c)
                      if c > 0:
                          finalize(c - 1)
                      steps_for(c)
                      post(c)
                  finalize(5)
                  stage("attfin")
                  for tt in range(4):
                      tsl = slice(tt * 512, (tt + 1) * 512)
                      for c in range(6):
                          i = c % 2
                          k.op(ACT, lambda: nc.scalar.activation(out=sq[i][:], in_=attnT[:, c, tsl], func=AF.Square),
                               reads=[B_attn[c]], writes=[B_sq[i]])
                          k.mm(PS[0][:, :], onesb[:, :], sq[i][:], c == 0, c == 5, reads=[B_ones, B_sq[i]],
                               wbuf=PSB[0], first=(c == 0), last=(c == 5), sig=True)
                      k.op(ACT, lambda: nc.scalar.activation(out=rra[:, tsl], in_=PS[0][:, :], func=AF.Ln, bias=EPS,
                                                             scale=1.0 / AW), reads=[PSB[0]], writes=[B_rra])
                  k.op(ACT, lambda: nc.scalar.activation(out=rra[:], in_=rra[:], func=AF.Exp, scale=-0.5),
                       reads=[B_rra], writes=[B_rra])
                  for c in range(6):
                      k.op(DVE, lambda: nc.vector.scalar_tensor_tensor(out=attnT[:, c, :], in0=attnT[:, c, :],
                                                                       scalar=pc("an_g", c), in1=rra[:],
                                                                       op0=ALU.mult, op1=ALU.mult),
                           reads=[B_attn[c], B_rra, B_pvec], writes=[B_attn[c]])
                  if DEBUG:
                      k.dma(SP, dbg_attn[s], attnT[:], reads=B_attn, writes=[k.buf("dbg_attn")])
                  stage("attn")
                  k.barrier()
              k.barrier()

          with ExitStack() as sF:
              R = sb("R", [128, 8, 1024], F32, sF)
              x1b = sb("x1b", [128, 8, 1024], BF16, sF)
              zb = [sb(f"zb{i}", [128, 512], BF16, sF) for i in range(3)]
              zsq = [sb(f"zsq{i}", [128, 512], BF16, sF) for i in range(3)]
              lm = sb("lm", [128, 512], F32, sF)
              lq = sb("lq", [128, 512], F32, sF)
              wo = [sb(f"wo{i}", [128, 8, 128], BF16, sF) for i in range(3)]
              actT = sb("actT", [128, 22, 1024], BF16, sF)
              Hrow = [[sb(f"Hrow{g}_{i}", [128, 1026], F32, sF) for i in range(2)] for g in range(2)]
              Tt = [[sb(f"Tt{g}_{i}", [128, 1024], F32, sF) for i in range(2)] for g in range(2)]
              sgt = [sb(f"sgt{i}", [128, 1024], F32, sF) for i in range(2)]
              wu = [sb(f"wu{i}", [128, 8, 2, 128], BF16, sF) for i in range(3)]
              wd = [sb(f"wd{i}", [128, 22, 128], BF16, sF) for i in range(2)]
              B_Rt = [[k.buf(f"R{t}_{o}") for o in range(8)] for t in range(2)]
              B_x1t = [[k.buf(f"x1b{t}_{o}") for o in range(8)] for t in range(2)]
              B_wo = [k.buf(f"wo{i}") for i in range(3)]
              B_act = [k.buf(f"actT{i}") for i in range(22)]
              B_H = [[k.buf(f"Hrow{g}_{i}") for i in range(2)] for g in range(2)]
              B_Hh = [[k.buf(f"Hrowh{g}_{i}") for i in range(2)] for g in range(2)]
              B_hc = [k.buf(f"halo{i}") for i in range(44)]
              B_T = [[k.buf(f"Tt{g}_{i}") for i in range(2)] for g in range(2)]
              B_sg = [k.buf(f"sgt{i}") for i in range(2)]
              B_wu = [k.buf(f"wu{i}") for i in range(3)]
              B_wd = [k.buf(f"wd{i}") for i in range(2)]
              B_out = k.buf("outT")
              tmp = dict(zb=zb, zsq=zsq, B_zb=[k.buf(f"zb{i}") for i in range(3)],
                         B_zsq=[k.buf(f"zsq{i}") for i in range(3)],
                         m=lm, q=lq, B_m=k.buf("lm"), B_q=k.buf("lq"), rot=[0])
              MB = [0, 1, 4, 5, 6, 7]
              k.op(POOL, lambda: nc.gpsimd.memset(halo[:], 0.0), writes=[B_halo] + B_hc)
              wo_n = [0]
              wd_n = [0]

              def load_wo(oc):
                  i = wo_n[0] % 3
                  wo_n[0] += 1
                  k.dma(POOL, wo[i][:], w_out_v[:, :, oc * 128:(oc + 1) * 128], writes=[B_wo[i]])
                  return i

              def load_wu(fp):
                  i = fp % 3
                  k.dma(POOL, wu[i][:, :, 0, :], w_up_v[:, :, fp * 128:(fp + 1) * 128], writes=[B_wu[i]])
                  k.dma(POOL, wu[i][:, :, 1, :], w_up_v[:, :, DFF + fp * 128:DFF + (fp + 1) * 128],
                        writes=[B_wu[i]])

              def load_wd(oc):
                  i = wd_n[0] % 2
                  wd_n[0] += 1
                  k.dma(POOL, wd[i][:, 0:11, :], w_down_v[:, 0:11, oc * 128:(oc + 1) * 128], writes=[B_wd[i]])
                  k.dma(POOL, wd[i][:, 11:22, :], w_down_v[:, 11:22, oc * 128:(oc + 1) * 128], writes=[B_wd[i]])
                  return i

              mmi = 0
              for hf in range(2):
                  t0 = hf * 1024
                  def load_x(hx, tt):
                      tl = slice(tt * 512, (tt + 1) * 512)
                      for kc in range(8):
                          k.dma(SP, R[:, kc, tl], xT[s, kc * 128:(kc + 1) * 128,
                                                     hx * 1024 + tt * 512:hx * 1024 + (tt + 1) * 512],
                                writes=[B_Rt[tt][kc]])
                  if hf == 0:
                      load_x(0, 0)
                      load_x(0, 1)
                  for tt in range(2):
                      tl = slice(tt * 512, (tt + 1) * 512)
                      tg = slice(t0 + tt * 512, t0 + (tt + 1) * 512)
                      wq_ = [load_wo(0), load_wo(1)]
                      for oc in range(8):
                          if oc + 2 < 8:
                              wq_.append(load_wo(oc + 2))
                          wi = wq_[oc]
                          pb = MB[mmi % 6]
                          mmi += 1
                          for kc in range(8):
                              rhs = attnT[:, kc, tg] if kc < 6 else uN[:, kc - 6, tg]
                              rb = B_attn[kc] if kc < 6 else B_uN
                              k.mm(PS[pb][:, :], wo[wi][:, kc, :], rhs, kc == 0, kc == 7,
                                   reads=[B_wo[wi], rb], wbuf=PSB[pb], first=(kc == 0), last=(kc == 7))
                          k.op(DVE, lambda: nc.vector.scalar_tensor_tensor(out=R[:, oc, tl], in0=R[:, oc, tl],
                                                                           scalar=ALPHA, in1=PS[pb][:, :],
                                                                           op0=ALU.mult, op1=ALU.add),
                               reads=[B_Rt[tt][oc], PSB[pb]], writes=[B_Rt[tt][oc]])
                      layer_norm_tile(R, B_Rt[tt], tl, "ln1_g", "ln1_b", (2, 3), tmp, outb=x1b, B_outb=B_x1t[tt])
                  if DEBUG:
                      k.dma(SP, dbg_x1[s][:, :, t0:t0 + 1024], x1b[:], reads=B_x1t[0] + B_x1t[1],
                            writes=[k.buf("dbg_x1")])
                  if hf == 0:
                      stage("C0")

                  def silu_mul(fq):
                      bq = fq % 2
                      k.op(ACT, lambda: nc.scalar.activation(out=sgt[bq][:], in_=Tt[0][bq][:], func=AF.Silu),
                           reads=[B_T[0][bq]], writes=[B_sg[bq]])
                      k.op(DVE, lambda: nc.vector.tensor_tensor(out=actT[:, fq, :], in0=sgt[bq][:],
                                                                in1=Tt[1][bq][:], op=ALU.mult),
                           reads=[B_sg[bq], B_T[1][bq]], writes=[B_act[fq]])

                  load_wu(0)
                  load_wu(1)
                  for fp in range(22):
                      if fp + 2 < 22:
                          load_wu(fp + 2)
                      wi = fp % 3
                      bi = fp % 2
                      for gv in range(2):
                          ch = gv * 22 + fp
                          H = Hrow[gv][bi]
                          BH = B_H[gv][bi]
                          BHh = B_Hh[gv][bi]
                          k.op(POOL, lambda: nc.gpsimd.tensor_copy(out=H[:, 0:2], in_=halo[:, ch, :]),
                               reads=[B_hc[ch]], writes=[BHh])
                          for tt in range(2):
                              tl = slice(tt * 512, (tt + 1) * 512)
                              pb = MB[mmi % 6]
                              mmi += 1
                              for kc in range(8):
                                  k.mm(PS[pb][:, :], wu[wi][:, kc, gv, :], x1b[:, kc, tl], kc == 0, kc == 7,
                                       reads=[B_wu[wi], B_x1t[tt][kc]], wbuf=PSB[pb], first=(kc == 0), last=(kc == 7))
                              k.op(ACT, lambda: nc.scalar.copy(out=H[:, 2 + tt * 512:2 + (tt + 1) * 512],
                                                               in_=PS[pb][:, :]), reads=[PSB[pb]], writes=[BH])
                      for gv in range(2):
                          ch = gv * 22 + fp
                          H = Hrow[gv][bi]
                          BH = B_H[gv][bi]
                          T = Tt[gv][bi]
                          BT = B_T[gv][bi]
                          k.op(ACT, lambda: nc.scalar.copy(out=halo[:, ch, :], in_=H[:, 1024:1026]),
                               reads=[BH], writes=[B_hc[ch]])
                          k.op(ACT, lambda: nc.scalar.activation(out=T[:], in_=H[:, 2:1026], func=AF.Identity,
                                                                 bias=pc("fb", ch), scale=pc("fw", 2 * 44 + ch)),
                               reads=[BH, B_pvec], writes=[BT])
                      for tap, wcol in ((1, 44), (0, 0)):
                          for gv in range(2):
                              ch = gv * 22 + fp
                              H = Hrow[gv][bi]
                              T = Tt[gv][bi]
                              k.op(DVE, lambda: nc.vector.scalar_tensor_tensor(out=T[:], in0=H[:, tap:tap + 1024],
                                                                               scalar=pc("fw", wcol + ch), in1=T[:],
                                                                               op0=ALU.mult, op1=ALU.add),
                                   reads=[B_H[gv][bi], B_Hh[gv][bi], B_T[gv][bi], B_pvec], writes=[B_T[gv][bi]])
                      if fp >= 1:
                          silu_mul(fp - 1)
                      if fp == 19:
                          wdq = [load_wd(0)]
                      if fp == 20:
                          wdq.append(load_wd(1))
                  silu_mul(21)
                  if DEBUG and hf == 0 and s == 0:
                      k.dma(SP, dbg_act, actT[:], reads=B_act, writes=[k.buf("dbg_act")])
                  if hf == 0:
                      stage("D0")
                  MBE = [4, 5, 6, 7]
                  SPE = [(2, 3), (0, 1)]
                  pend = []
                  for oc in range(8):
                      wi = wdq[oc]
                      for tt in range(2):
                          tl = slice(tt * 512, (tt + 1) * 512)
                          pb = MBE[mmi % 4]
                          mmi += 1
                          for kc in range(22):
                              k.mm(PS[pb][:, :], wd[wi][:, kc, :], actT[:, kc, tl], kc == 0, kc == 21,
                                   reads=[B_wd[wi], B_act[kc]], wbuf=PSB[pb], first=(kc == 0), last=(kc == 21))
                          for f in pend:
                              f()
                          pend = []
                          k.op(DVE, lambda: nc.vector.scalar_tensor_tensor(out=R[:, oc, tl], in0=R[:, oc, tl],
                                                                           scalar=ALPHA, in1=PS[pb][:, :],
                                                                           op0=ALU.mult, op1=ALU.add),
                               reads=[B_Rt[tt][oc], PSB[pb]], writes=[B_Rt[tt][oc]])
                          pend.append(ln_stats_chunk(R, B_Rt[tt], tl, oc, SPE[tt], tmp))
                      if oc + 2 < 8:
                          wdq.append(load_wd(oc + 2))
                  for f in pend:
                      f()
                  for tt in range(2):
                      tl = slice(tt * 512, (tt + 1) * 512)
                      fin = []
                      for kc in range(8):
                          g, i, h = kc // 4, (kc // 2) % 2, kc % 2
                          if tt == 0:
                              fin.append((Tt[g][i][:, h * 512:(h + 1) * 512], [B_T[g][i]]))
                          else:
                              fin.append((Hrow[g][i][:, h * 512:(h + 1) * 512], [B_H[g][i], B_Hh[g][i]]))
                      layer_norm_tile(R, B_Rt[tt], tl, "ln2_g", "ln2_b", SPE[tt], tmp, skip_stats=True, fin=fin)
                      for kc in range(8):
                          k.dma(SP, outT[s, kc * 128:(kc + 1) * 128, t0 + tt * 512:t0 + (tt + 1) * 512],
                                fin[kc][0], reads=fin[kc][1], writes=[], sbuf=B_out)
                      if hf == 0:
                          load_x(1, tt)
                  if hf == 0:
                      stage("E0")
              k.barrier()
    except _Stop:
        pass
    k.barrier()
    es.close()
    print("semaphores used:", k.nsem, "PE sigs", PE.n, "ACT", ACT.n, "DVE", DVE.n, "POOL", POOL.n)
    return nc


_NC_CACHE = {}


def kernel(**inp):
    x = np.asarray(inp["x"], np.float32)
    pv = pack_params(inp)
    relx = np.concatenate([np.asarray(inp["rel_table"], np.float32), np.ones((1, 12), np.float32)], axis=0)
    oh = onehot_const()
    ident = np.eye(128, dtype=np.float32)
    jmat = np.ascontiguousarray(np.eye(128, dtype=np.float32)[::-1])
    w_in = np.ascontiguousarray(np.asarray(inp["w_in"], np.float32)[0])
    w_out = np.ascontiguousarray(np.asarray(inp["w_out"], np.float32)[0])
    w_up = np.ascontiguousarray(np.asarray(inp["w_up"], np.float32)[0])
    w_down = np.ascontiguousarray(np.asarray(inp["w_down"], np.float32)[0])
    if "nc" not in _NC_CACHE:
        _NC_CACHE["nc"] = build()
    nc = _NC_CACHE["nc"]
    in_maps = []
    for cid in range(NCORE):
        xs = x[cid * NSEQ:(cid + 1) * NSEQ]
        xTs = np.ascontiguousarray(xs.transpose(0, 2, 1))
        in_maps.append({"xT": xTs, "w_in": w_in, "w_out": w_out, "w_up": w_up, "w_down": w_down,
                        "pvec": pv, "relx": relx, "oh": oh, "ident": ident, "jmat": jmat})
    res = run_bass_kernel_spmd(nc, in_maps, core_ids=list(range(NCORE)))
    outs = []
    for cid in range(NCORE):
        o = np.asarray(res.results[cid]["outT"], np.float32)
        outs.append(o.transpose(0, 2, 1))
    out = np.ascontiguousarray(np.concatenate(outs, axis=0))
    if DEBUG:
        kernel.last_results = res.results
    return out
```
